# Optimizing a Trainium2 kernel written in Bass

```python
import math
import jax
import jax.numpy as jnp
from jax import lax
import numpy as np

D_MODEL = 1024
BATCH = 32
SEQ = 256
DEPTH = 1
DEC_BATCH = 4
DEC_SEQ = 1024
PAST_LEN = 256

GRID_W = 64
RET_HEADS = 4
RET_DK = D_MODEL // 8
RET_DV = D_MODEL // 8
RET_QK = RET_HEADS * RET_DK
RET_WIDTH = RET_HEADS * RET_DV
RET_CHUNK = 128
HY_WIDTH = D_MODEL // 2
HY_ORDER = 2
HY_EMB_BANDS = 16
HY_EMB = 1 + 2 * HY_EMB_BANDS
HY_FILTER_HIDDEN = 64
HY_FAST_DECAY = 0.3
HY_SLOW_DECAY = 1.5
HY_TARGET = 1e-2
SHORT_CONV = 3
D_FF = -(-8 * D_MODEL // (3 * 256)) * 256
N_IN = 2 * RET_QK + 2 * RET_WIDTH + 3 * HY_WIDTH + 2 * D_MODEL
N_MOD = 6
EPS = 1e-6

kernel_name = 'retention_hyena_flow_step'


def rmsnorm(x, g):
    x32 = x.astype(jnp.float32)
    y = x32 * lax.rsqrt(jnp.mean(x32 * x32, axis=-1, keepdims=True) + EPS)
    return (y * g.astype(jnp.float32)).astype(x.dtype)


def ada_mod(cond, w, b):
    m = jax.nn.silu(cond) @ w + b
    return jnp.split(m[..., None, :], N_MOD, axis=-1)


def retention_chunkwise(q, k, v, gamma, s0):
    B, H, L, _ = q.shape
    n = L // RET_CHUNK

    def chunks(t):
        return jnp.moveaxis(t.reshape(B, H, n, RET_CHUNK, t.shape[-1]), 2, 0)

    log_g = jnp.log(gamma)
    pos = jnp.arange(RET_CHUNK, dtype=jnp.float32)
    rel = pos[:, None] - pos[None, :]
    lower = rel >= 0
    dmat = jnp.where(lower[None], jnp.exp(jnp.where(lower, rel, 0.0)[None] * log_g[:, None, None]), 0.0)
    q_decay = jnp.exp((pos + 1.0)[None] * log_g[:, None])
    k_decay = jnp.exp((RET_CHUNK - 1.0 - pos)[None] * log_g[:, None])
    chunk_decay = jnp.exp(RET_CHUNK * log_g)

    def step(s, qkv):
        qc, kc, vc = qkv
        att = jnp.einsum('bhid,bhjd->bhij', qc, kc) * dmat
        o = (jnp.einsum('bhij,bhjv->bhiv', att, vc)
             + jnp.einsum('bhid,bhdv->bhiv', qc, s) * q_decay[None, :, :, None])
        s = (s * chunk_decay[None, :, None, None]
             + jnp.einsum('bhjd,bhjv->bhdv', kc * k_decay[None, :, :, None], vc))
        return s, o

    s_final, o = lax.scan(step, s0, (chunks(q), chunks(k), chunks(v)))
    o = jnp.moveaxis(o, 0, 2).reshape(B, H, L, v.shape[-1])
    return o, s_final


def bidir_retention(q, k, v, gamma_f, gamma_b, s0_f, s0_b):
    o_f, s_f = retention_chunkwise(q, k, v, gamma_f, s0_f)
    o_b, s_b = retention_chunkwise(jnp.flip(q, 2), jnp.flip(k, 2), jnp.flip(v, 2), gamma_b, s0_b)
    return o_f + jnp.flip(o_b, 2), s_f, s_b


def head_groupnorm(o):
    mu = jnp.mean(o, axis=-1, keepdims=True)
    var = jnp.mean(jnp.square(o - mu), axis=-1, keepdims=True)
    return (o - mu) * lax.rsqrt(var + EPS)


def short_conv3(u, w, b, grid_w):
    B, L, C = u.shape
    if grid_w is not None:
        rows = L // grid_w
        u = u.reshape(B, rows, grid_w, C)
    pad = [(0, 0)] * (u.ndim - 2) + [(1, 1), (0, 0)]
    up = jnp.pad(u, pad)
    n = u.shape[-2]
    y = up[..., 0:n, :] * w[0] + up[..., 1:n + 1, :] * w[1] + up[..., 2:n + 2, :] * w[2] + b
    return y.reshape(B, L, C)


def hyena_filters(L, w1, b1, w2, b2, w3, freq):
    t = jnp.linspace(0.0, 1.0, L, dtype=jnp.float32)[:, None]
    ang = 2.0 * math.pi * jnp.arange(L, dtype=jnp.float32)[:, None] / L
    bands = jnp.linspace(1e-4, HY_EMB_BANDS - 1, HY_EMB_BANDS, dtype=jnp.float32)[None]
    z = jnp.concatenate([t, jnp.cos(bands * ang), -jnp.sin(bands * ang)], axis=-1)
    h = jnp.sin(freq * (z @ w1 + b1))
    h = jnp.sin(freq * (h @ w2 + b2))
    h = (h @ w3).reshape(L, HY_ORDER, 2, HY_WIDTH)
    max_decay = math.log(HY_TARGET) / HY_FAST_DECAY
    min_decay = math.log(HY_TARGET) / HY_SLOW_DECAY
    deltas = jnp.linspace(min_decay, max_decay, HY_WIDTH, dtype=jnp.float32)
    window = jnp.exp(-t * jnp.abs(deltas))
    h = h * window[:, None, None, :]
    fwd = h[:, :, 0]
    bwd = h[:, :, 1]
    k = jnp.concatenate([fwd, jnp.zeros_like(fwd[:1]), jnp.flip(bwd[1:], axis=0)], axis=0)
    k = k / jnp.sum(jnp.abs(k), axis=0, keepdims=True)
    return jnp.moveaxis(k, 1, 0)


def long_conv(u, k, bias):
    L = u.shape[1]
    uf = jnp.fft.rfft(u, n=2 * L, axis=1)
    kf = jnp.fft.rfft(k, n=2 * L, axis=0)
    y = jnp.fft.irfft(uf * kf[None], n=2 * L, axis=1)[:, :L]
    return y + u * bias


def trunk_layer(x, shift1, scale1, gate1, shift2, scale2, gate2, s0_fwd, s0_bwd, grid_w,
                norm1, norm2, w_in, decay_fwd, decay_bwd, conv_w, conv_b,
                pos_w1, pos_b1, pos_w2, pos_b2, pos_w3, sin_freq, hy_bias,
                w_ret_o, w_hy_o, w_out, w_ffn_in, w_ffn_out):
    B, L, _ = x.shape
    f32 = jnp.float32
    h = rmsnorm(x, norm1) * (1 + scale1) + shift1
    proj = h @ w_in
    splits = list(np.cumsum([RET_QK, RET_QK, RET_WIDTH, RET_WIDTH, 3 * HY_WIDTH, D_MODEL]))
    q, k, v, g, hu, g_ret, g_hy = jnp.split(proj, splits, axis=-1)

    def heads(t, d):
        return t.reshape(B, L, RET_HEADS, d).transpose(0, 2, 1, 3).astype(f32)
    qh = heads(q, RET_DK)
    kh = heads(k, RET_DK) * (RET_DK ** -0.5)
    vh = heads(v, RET_DV)
    gamma_f = jax.nn.sigmoid(decay_fwd.astype(f32))
    gamma_b = jax.nn.sigmoid(decay_bwd.astype(f32))
    o, s_f, s_b = bidir_retention(qh, kh, vh, gamma_f, gamma_b, s0_fwd.astype(f32), s0_bwd.astype(f32))
    o = head_groupnorm(o).transpose(0, 2, 1, 3).reshape(B, L, RET_WIDTH).astype(x.dtype)
    y_ret = (jax.nn.silu(g) * o) @ w_ret_o

    u = short_conv3(hu, conv_w, conv_b, grid_w).astype(f32)
    hv, hx1, hx2 = jnp.split(u, 3, axis=-1)
    kers = hyena_filters(L, pos_w1.astype(f32), pos_b1.astype(f32), pos_w2.astype(f32),
                         pos_b2.astype(f32), pos_w3.astype(f32), sin_freq.astype(f32))
    hb = hy_bias.astype(f32)
    z = hx1 * long_conv(hv, kers[0], hb[0])
    z = hx2 * long_conv(z, kers[1], hb[1])
    y_hy = z.astype(x.dtype) @ w_hy_o

    mix = jax.nn.sigmoid(g_ret) * y_ret + jax.nn.sigmoid(g_hy) * y_hy
    x = x + gate1 * (mix @ w_out)

    h2 = rmsnorm(x, norm2) * (1 + scale2) + shift2
    a, bgt = jnp.split(h2 @ w_ffn_in, 2, axis=-1)
    x = x + gate2 * ((jax.nn.silu(a) * bgt) @ w_ffn_out)
    return x, s_f, s_b


def setup_inputs(seed: int = 0) -> dict:
    key = jax.random.key(seed)
    ks = jax.random.split(key, 32)
    f32 = jnp.float32

    def nrm(k, shape, scale):
        return jax.random.normal(k, shape, f32) * scale

    st_shape = (DEC_BATCH, DEPTH, RET_HEADS, RET_DK, RET_DV)
    decay_init = jnp.log(2.0 ** (5.0 + jnp.arange(RET_HEADS, dtype=f32)) - 1.0)
    return {
        'x_prompt': nrm(ks[0], (BATCH, SEQ, D_MODEL), 1.0),
        'x_sample': nrm(ks[1], (DEC_BATCH, DEC_SEQ, D_MODEL), 1.0),
        'state_ret_fwd': nrm(ks[2], st_shape, 0.5),
        'state_ret_bwd': nrm(ks[3], st_shape, 0.5),
        'c': nrm(ks[4], (DEC_BATCH, D_MODEL), 1.0),
        'c_ctx': nrm(ks[5], (D_MODEL,), 1.0),
        'norm1_g': 1.0 + nrm(ks[6], (DEPTH, D_MODEL), 0.02),
        'norm2_g': 1.0 + nrm(ks[7], (DEPTH, D_MODEL), 0.02),
        'w_ada': nrm(ks[8], (DEPTH, D_MODEL, N_MOD * D_MODEL), 0.3 * D_MODEL ** -0.5),
        'b_ada': nrm(ks[9], (DEPTH, N_MOD * D_MODEL), 0.02),
        'w_in': nrm(ks[10], (DEPTH, D_MODEL, N_IN), D_MODEL ** -0.5),
        'ret_decay_fwd': decay_init[None] + nrm(ks[11], (DEPTH, RET_HEADS), 0.1),
        'ret_decay_bwd': decay_init[None] + nrm(ks[12], (DEPTH, RET_HEADS), 0.1),
        'hy_conv_w': nrm(ks[13], (DEPTH, SHORT_CONV, 3 * HY_WIDTH), SHORT_CONV ** -0.5),
        'hy_conv_b': nrm(ks[14], (DEPTH, 3 * HY_WIDTH), 0.02),
        'hy_pos_w1': nrm(ks[15], (DEPTH, HY_EMB, HY_FILTER_HIDDEN), HY_EMB ** -0.5),
        'hy_pos_b1': nrm(ks[16], (DEPTH, HY_FILTER_HIDDEN), 0.1),
        'hy_pos_w2': nrm(ks[17], (DEPTH, HY_FILTER_HIDDEN, HY_FILTER_HIDDEN), HY_FILTER_HIDDEN ** -0.5),
        'hy_pos_b2': nrm(ks[18], (DEPTH, HY_FILTER_HIDDEN), 0.1),
        'hy_pos_w3': nrm(ks[19], (DEPTH, HY_FILTER_HIDDEN, HY_ORDER * 2 * HY_WIDTH), HY_FILTER_HIDDEN ** -0.5),
        'hy_sin_freq': 1.0 + nrm(ks[20], (DEPTH, HY_FILTER_HIDDEN), 0.1),
        'hy_bias': nrm(ks[21], (DEPTH, HY_ORDER, HY_WIDTH), 0.1),
        'w_ret_o': nrm(ks[22], (DEPTH, RET_WIDTH, D_MODEL), RET_WIDTH ** -0.5),
        'w_hy_o': nrm(ks[23], (DEPTH, HY_WIDTH, D_MODEL), HY_WIDTH ** -0.5),
        'w_out': nrm(ks[24], (DEPTH, D_MODEL, D_MODEL), D_MODEL ** -0.5),
        'w_ffn_in': nrm(ks[25], (DEPTH, D_MODEL, 2 * D_FF), D_MODEL ** -0.5),
        'w_ffn_out': nrm(ks[26], (DEPTH, D_FF, D_MODEL), D_FF ** -0.5),
        'final_g': 1.0 + nrm(ks[27], (D_MODEL,), 0.02),
    }


def reference(x_prompt, x_sample, state_ret_fwd, state_ret_bwd, c, c_ctx,
              norm1_g, norm2_g, w_ada, b_ada, w_in, ret_decay_fwd, ret_decay_bwd,
              hy_conv_w, hy_conv_b, hy_pos_w1, hy_pos_b1, hy_pos_w2, hy_pos_b2, hy_pos_w3,
              hy_sin_freq, hy_bias, w_ret_o, w_hy_o, w_out, w_ffn_in, w_ffn_out, final_g):
    zero_state = jnp.zeros((x_prompt.shape[0], RET_HEADS, RET_DK, RET_DV), jnp.float32)
    xp = x_prompt
    new_f = []
    new_b = []
    for l in range(DEPTH):
        sh1, sc1, g1, sh2, sc2, g2 = ada_mod(c_ctx, w_ada[l], b_ada[l])
        xp, s_f, s_b = trunk_layer(
            xp, sh1, sc1, g1, sh2, sc2, g2, zero_state, zero_state, None,
            norm1_g[l], norm2_g[l], w_in[l], ret_decay_fwd[l], ret_decay_bwd[l],
            hy_conv_w[l], hy_conv_b[l], hy_pos_w1[l], hy_pos_b1[l], hy_pos_w2[l], hy_pos_b2[l],
            hy_pos_w3[l], hy_sin_freq[l], hy_bias[l], w_ret_o[l], w_hy_o[l], w_out[l],
            w_ffn_in[l], w_ffn_out[l])
        new_f.append(s_f)
        new_b.append(s_b)
    y_prompt = rmsnorm(xp, final_g)
    new_state_ret_fwd = jnp.stack(new_f, axis=1).astype(x_prompt.dtype)
    new_state_ret_bwd = jnp.stack(new_b, axis=1).astype(x_prompt.dtype)

    xs = x_sample
    for l in range(DEPTH):
        sh1, sc1, g1, sh2, sc2, g2 = ada_mod(c, w_ada[l], b_ada[l])
        xs, _, _ = trunk_layer(
            xs, sh1, sc1, g1, sh2, sc2, g2, state_ret_fwd[:, l], state_ret_bwd[:, l], GRID_W,
            norm1_g[l], norm2_g[l], w_in[l], ret_decay_fwd[l], ret_decay_bwd[l],
            hy_conv_w[l], hy_conv_b[l], hy_pos_w1[l], hy_pos_b1[l], hy_pos_w2[l], hy_pos_b2[l],
            hy_pos_w3[l], hy_sin_freq[l], hy_bias[l], w_ret_o[l], w_hy_o[l], w_out[l],
            w_ffn_in[l], w_ffn_out[l])
    y_sample = rmsnorm(xs, final_g)
    return (y_prompt, y_sample, new_state_ret_fwd, new_state_ret_bwd)
```

```python
import numpy as np
import concourse.bass as bass
import concourse.mybir as mybir
from concourse.bass_utils import run_bass_kernel_spmd

F32 = mybir.dt.float32
BF16 = mybir.dt.bfloat16
AF = mybir.ActivationFunctionType
ALU = mybir.AluOpType
AX = mybir.AxisListType

N_DMA_SEMS = 12


def _rects(ap):
    t = ap.tensor
    dims = ap.ap
    off = int(ap.offset)
    sp = str(ap.space)
    if sp in ("SB", "PSUM"):
        shp = t.shape
        fs = 1
        for s in shp[1:]:
            fs *= s
        p0 = off // fs
        f0 = off % fs
        npart = dims[0][1]
        if sp == "PSUM":
            return [(t.name, 0, 128, 0, fs)]
        if len(dims) == 3 and dims[2][0] == 1 and 1 < dims[1][1] <= 32 and dims[1][0] > dims[2][1]:
            return [(t.name, p0, p0 + npart, f0 + r * dims[1][0], f0 + r * dims[1][0] + dims[2][1]) for r in range(dims[1][1])]
        ext = 0
        for st, cnt in dims[1:]:
            ext += abs(st) * (cnt - 1)
        return [(t.name, p0, p0 + npart, f0, f0 + ext + 1)]
    ext = 0
    for st, cnt in dims:
        ext += abs(st) * (cnt - 1)
    return [(t.name, 0, 1, off, off + ext + 1)]


class Prog:
    def __init__(self, nc):
        self.nc = nc
        self.lists = {"pe": [], "act": [], "dve": [], "pool": [], "sp": []}
        self.cnt = {"pe": 0, "act": 0, "dve": 0, "pool": 0}
        self.sems = {}
        self.seen = {e: {} for e in self.lists}
        self.regions = {}
        self.dma_cnt = {"sp": 0, "act": 0, "pool": 0}
        self.dma_sems = {}
        self.all_dma = []
        self.n_wait = 0
        self.n_ins = 0

    def _overlaps(self, r):
        recs = self.regions.get(r[0])
        if not recs:
            return []
        out = []
        for k, v in recs.items():
            if k[1] < r[2] and r[1] < k[2] and k[3] < r[4] and r[3] < k[4]:
                out.append((k, v))
        return out

    def _deps(self, reads, writes):
        deps = {}

        def add(d):
            if d is None:
                return
            k, v = d
            if deps.get(k, -1) < v:
                deps[k] = v
        for ap in reads:
            for r in _rects(ap):
                for k, v in self._overlaps(r):
                    add(v[0])
        for ap in writes:
            for r in _rects(ap):
                for k, v in self._overlaps(r):
                    add(v[0])
                    for d in v[1].items():
                        add(d)
        return deps

    def _commit(self, reads, writes, tok):
        for ap in writes:
            for r in _rects(ap):
                recs = self.regions.setdefault(r[0], {})
                for k in list(recs.keys()):
                    if k[1] >= r[1] and k[2] <= r[2] and k[3] >= r[3] and k[4] <= r[4]:
                        del recs[k]
                recs[r] = [tok, {}]
        for ap in reads:
            for r in _rects(ap):
                recs = self.regions.setdefault(r[0], {})
                if r not in recs:
                    recs[r] = [None, {}]
                rd = recs[r][1]
                if rd.get(tok[0], -1) < tok[1]:
                    rd[tok[0]] = tok[1]

    def _emit_waits(self, eng, deps, skip_self=False):
        for k, v in deps.items():
            if skip_self and k == eng:
                continue
            if self.seen[eng].get(k, 0) >= v:
                continue
            self.seen[eng][k] = v
            self.lists[eng].append(("wait", k, v))
            self.n_wait += 1

    def op(self, eng, fn, reads=(), writes=(), signal=True):
        writes = list(writes) + [ap for ap in reads if str(ap.space) == "PSUM"]
        deps = self._deps(reads, writes)
        self._emit_waits(eng, deps, skip_self=(eng == "pe"))
        if signal:
            self.cnt[eng] += 1
            tick = self.cnt[eng]
            self.lists[eng].append(("op", fn, eng, 1))
        else:
            tick = self.cnt[eng] + 1
            self.lists[eng].append(("op", fn, None, 0))
        self._commit(reads, writes, (eng, tick))
        self.n_ins += 1

    def dma(self, q, out, in_, after=(), **kw):
        deps = self._deps([in_], [out])
        for k_, v_ in after:
            deps[k_] = max(deps.get(k_, 0), v_)
        i = self.dma_cnt[q]
        self.dma_cnt[q] += 1
        slot = i % N_DMA_SEMS
        semkey = ("dma", q, slot)
        prev = 16 * (i // N_DMA_SEMS)
        if prev > 0:
            deps[semkey] = max(deps.get(semkey, 0), prev)
        self._emit_waits(q, deps)
        val = prev + 16
        self.lists[q].append(("dma", out, in_, kw, semkey))
        self._commit([in_], [out], (semkey, val))
        self.all_dma.append((semkey, val))
        self.n_ins += 1
        return (semkey, val)

    def finish(self, eng="sp"):
        last = {}
        for k, v in self.all_dma:
            last[k] = max(last.get(k, 0), v)
        self._emit_waits(eng, last)
        fin = {e: c for e, c in self.cnt.items() if c > 0}
        self._emit_waits(eng, fin)

    def build(self):
        nc = self.nc
        from contextlib import ExitStack
        with ExitStack() as es:
            for e in ("pe", "act", "dve", "pool"):
                self.sems[e] = es.enter_context(nc.semaphore("s_" + e))
            for q in ("sp", "act", "pool"):
                for s in range(N_DMA_SEMS):
                    self.sems[("dma", q, s)] = es.enter_context(nc.semaphore("d_%s_%d" % (q, s)))
            block = es.enter_context(nc.Block())
            sems = self.sems

            def replay(lst):
                def run(engine):
                    for it in lst:
                        if it[0] == "wait":
                            engine.wait_ge(sems[it[1]], it[2])
                        elif it[0] == "op":
                            ins = it[1](engine)
                            if it[2] is not None:
                                ins.then_inc(sems[it[2]], 1)
                        else:
                            _, out, in_, kw, semkey = it
                            engine.dma_start(out=out, in_=in_, **kw).then_inc(sems[semkey], 16)
                return run
            block.tensor(replay(self.lists["pe"]))
            block.scalar(replay(self.lists["act"]))
            block.vector(replay(self.lists["dve"]))
            block.gpsimd(replay(self.lists["pool"]))
            block.sync(replay(self.lists["sp"]))


D = 1024
NIN = 5632
DFF = 2816
EPS = 1e-6
MAGIC = 12582912.0
TWO_PI = 6.283185307179586

NBF = 83968
NF32 = 10240


def _prod(s):
    r = 1
    for v in s:
        r *= v
    return r


class Ctx:
    pass


def build_nc():
    nc = bass.Bass("TRN2", target_bir_lowering=False)
    P = Prog(nc)
    G = Ctx()

    def din(name, shape, dt=F32):
        return nc.dram_tensor(name, list(shape), dt, kind="ExternalInput").ap()

    def dout(name, shape, dt=F32):
        return nc.dram_tensor(name, list(shape), dt, kind="ExternalOutput").ap()

    xc = din("xc", [1024, D]); xs = din("xs", [1024, D])
    s0f_d = din("s0f", [4, 128, 128]); s0b_d = din("s0b", [4, 128, 128])
    cc_d = din("cc", [128, 16]); sel_d = din("sel", [128, 2])
    n1g_d = din("n1g", [128, 8]); n2g_d = din("n2g", [128, 8]); fg_d = din("fg", [1, D])
    wada_d = din("w_ada", [D, 6144]); bcol_d = din("b_col", [128, 48]); brow_d = din("b_row", [1, 6144])
    win_d = din("w_in", [D, NIN]); dec_d = din("dec", [1, 8])
    cw_d = din("cw", [128, 12, 3]); cb_d = din("cb", [128, 12])
    pw1_d = din("pw1", [33, 64]); pb1_d = din("pb1", [64, 1]); pw2_d = din("pw2", [64, 64]); pb2_d = din("pb2", [64, 1])
    pw3_d = din("pw3", [64, 2048]); pfr_d = din("pfr", [64, 1]); hyb_d = din("hyb", [2, 512])
    wro_d = din("w_ro", [512, D]); who_d = din("w_ho", [512, D]); wo_d = din("w_o", [D, D])
    wf1_d = din("w_f1", [D, NIN]); wf2_d = din("w_f2", [DFF, D])
    ident_d = din("ident", [128, 128], BF16); rc_d = din("rc", [128, 8, 128]); identf_d = din("identf", [128, 128])
    HY = {}
    for nm, L in (("c", 256), ("s", 1024)):
        nT = L // 128
        HY[nm] = dict(L=L, nT=nT, nF=nT, nTB=L // 256,
                      zT=din("zT_" + nm, [33, L]), win=din("win_" + nm, [L, 512]), winb=din("winb_" + nm, [L, 512]),
                      tfc=din("tfc_" + nm, [nT, 128, nT * 128], BF16), tfs=din("tfs_" + nm, [nT, 128, nT * 128], BF16),
                      tic=din("tic_" + nm, [L // 256, 128, nT * 256], BF16), tis=din("tis_" + nm, [L // 256, 128, nT * 256], BF16))
    yc = dout("yc", [1024, D]); ys = dout("ys", [512, D])
    nsf = dout("nsf", [4, 4, 128, 128]); nsb = dout("nsb", [4, 4, 128, 128])
    x1s = nc.dram_tensor("x1s", [1536, D], F32, kind="Internal").ap()
    kfs = nc.dram_tensor("kfs", [2, 2, 2, 8, 128, 512], BF16, kind="Internal").ap()
    gsc = nc.dram_tensor("gsc", [4, D], F32, kind="Internal").ap()

    from contextlib import ExitStack
    with ExitStack() as es:
        ABF = es.enter_context(nc.sbuf_tensor("abf", [128, NBF], BF16))
        AF_ = es.enter_context(nc.sbuf_tensor("af32", [128, NF32], F32))
        SM = es.enter_context(nc.sbuf_tensor("sm", [128, 512], F32))
        IDENTF = es.enter_context(nc.sbuf_tensor("identf_sb", [128, 128], F32))
        PS = [es.enter_context(nc.psum_tensor("ps%d" % i, [128, 512], F32)) for i in range(8)]

        def bfv(off, *shape):
            n = _prod(shape)
            assert off + n <= NBF, (off, n)
            ap = ABF[:, off:off + n]
            if len(shape) == 2:
                ap = ap.rearrange("p (a b) -> p a b", a=shape[0])
            return ap

        def f32v(off, *shape):
            n = _prod(shape)
            assert off + n <= NF32, (off, n)
            ap = AF_[:, off:off + n]
            if len(shape) == 2:
                ap = ap.rearrange("p (a b) -> p a b", a=shape[0])
            return ap

        G.banks = list(range(8)); G.bi = 0

        G.live = set()

        def bank(hold=False):
            for _ in range(16):
                b = G.banks[G.bi % len(G.banks)]
                G.bi += 1
                if b not in G.live:
                    break
            if hold:
                G.live.add(b)
            return PS[b]

        def release(ps):
            G.live.discard(PS.index(ps))

        def mm(out, lhsT, rhs, start, stop, signal=None):
            P.op("pe", lambda e: e.matmul(out, lhsT=lhsT, rhs=rhs, start=start, stop=stop),
                 reads=[lhsT, rhs], writes=[out], signal=(stop if signal is None else signal))

        def act(out, in_, func, bias=None, scale=None, accum=None):
            kw = {}
            rd = [in_]
            wr = [out]
            if bias is not None:
                kw["bias"] = bias
                if not isinstance(bias, float):
                    rd.append(bias)
            if scale is not None:
                kw["scale"] = scale
                if not isinstance(scale, float):
                    rd.append(scale)
            if accum is not None:
                kw["accum_out"] = accum
                wr.append(accum)
            P.op("act", lambda e: e.activation(out=out, in_=in_, func=func, **kw), reads=rd, writes=wr)

        def tt(eng, out, in0, in1, op):
            P.op(eng, lambda e: e.tensor_tensor(out=out, in0=in0, in1=in1, op=op), reads=[in0, in1], writes=[out])

        def ts(eng, out, in0, s1, s2, op0, op1=None):
            rd = [in0] + [s for s in (s1, s2) if s is not None and not isinstance(s, float)]
            if op1 is None:
                P.op(eng, lambda e: e.tensor_scalar(out=out, in0=in0, scalar1=s1, scalar2=None, op0=op0), reads=rd, writes=[out])
            else:
                P.op(eng, lambda e: e.tensor_scalar(out=out, in0=in0, scalar1=s1, scalar2=s2, op0=op0, op1=op1), reads=rd, writes=[out])

        def stt(out, in0, scalar, in1, op0, op1, accum=None):
            rd = [in0, in1] + ([] if isinstance(scalar, float) else [scalar])
            if accum is None:
                P.op("dve", lambda e: e.scalar_tensor_tensor(out=out, in0=in0, scalar=scalar, in1=in1, op0=op0, op1=op1), reads=rd, writes=[out])
            else:
                P.op("dve", lambda e: e.scalar_tensor_tensor(out=out, in0=in0, scalar=scalar, in1=in1, op0=op0, op1=op1, accum_out=accum), reads=rd, writes=[out, accum])

        def cp(eng, out, in_):
            if eng == "act":
                P.op("act", lambda e: e.copy(out=out, in_=in_), reads=[in_], writes=[out])
            else:
                P.op(eng, lambda e: e.tensor_copy(out=out, in_=in_), reads=[in_], writes=[out])

        def recip(out, in_):
            P.op("dve", lambda e: e.reciprocal(out=out, in_=in_), reads=[in_], writes=[out])

        def memset(eng, out, val):
            P.op(eng, lambda e: e.memset(out, val), reads=[], writes=[out])


        G.wtok = []

        def wdma(out, in_, window=2):
            after = [G.wtok[-window]] if (window and len(G.wtok) >= window) else []
            G.wtok.append(P.dma("pool", out, in_, after=after))

        colm = SM[:, 0:96].rearrange("p (a b) -> p a b", a=48)
        A1 = SM[:, 96:112].rearrange("p (a b) -> p a b", a=8)
        A2 = SM[:, 112:128].rearrange("p (a b) -> p a b", a=8)
        cond = SM[:, 128:144]
        selt = SM[:, 144:146]
        dect = SM[:, 146:154]
        lgt = SM[:, 154:162]
        cdt = SM[:, 162:170]
        n1g = SM[:, 170:178]; n2g = SM[:, 178:186]
        bcol = SM[:, 186:234]
        cbt = SM[:, 234:246]
        cwt = SM[:, 246:282].rearrange("p (a b) -> p a b", a=12)
        stat = SM[:, 282:330]
        pb1 = SM[0:64, 330:331]; pb2 = SM[0:64, 331:332]; pfr = SM[0:64, 332:333]
        ones_f = SM[:, 384:512]
        EPSC = SM[:, 333:334]
        stat2 = SM[:, 334:382]

        IDENT = ABF[:, NBF - 128:NBF]
        DM = bfv(0, 4, 128); QDF = bfv(512, 4, 128); QDB = bfv(1024, 4, 128); KDF = bfv(1536, 4, 128); KDB = bfv(2048, 4, 128)
        Z2T = bfv(2560, 4, 1536)
        WK0 = 57856

        P.dma("sp", IDENT, ident_d)
        P.dma("sp", IDENTF[:, :], identf_d)
        P.dma("sp", cond, cc_d)
        P.dma("sp", selt, sel_d)
        P.dma("sp", dect, dec_d[0, :].partition_broadcast(128))
        P.dma("sp", n1g, n1g_d); P.dma("sp", n2g, n2g_d)
        P.dma("sp", bcol, bcol_d)
        P.dma("sp", cbt, cb_d)
        P.dma("sp", SM[:, 246:282], cw_d.rearrange("p a b -> p (a b)"))
        P.dma("sp", pb1, pb1_d); P.dma("sp", pb2, pb2_d); P.dma("sp", pfr, pfr_d)
        memset("dve", ones_f, 1.0)
        memset("dve", EPSC, EPS)

        scf = f32v(1280, 16)
        act(scf, cond, AF.Silu)
        SCB = bfv(2560, 8, 2)
        cp("dve", SCB, scf.rearrange("p (g k) -> p k g", g=2))
        SCBB = bfv(2576, 2, 8 * 128)
        onesb = f32v(1296, 128)
        memset("dve", onesb, 1.0)
        for g in range(2):
            for k in range(8):
                ts("dve", SCBB[:, g, k * 128:(k + 1) * 128], onesb, scf[:, g * 8 + k:g * 8 + k + 1], None, ALU.mult)
        P.dma("pool", ABF[0:64, 8704 + 68608:8704 + 68608 + 2048], pw3_d)
        WA = [bfv(8704 + 12288 + i * 4096, 8, 512) for i in range(5)] + [bfv(8704 + 38912 + i * 4096, 8, 512) for i in range(3)]
        for blk in range(6):
            wdma(WA[blk], wada_d[:, blk * 512:(blk + 1) * 512].rearrange("(k p) n -> p k n", p=128), window=0)
        G.fillq = []

        def fill(n):
            for _ in range(n):
                if G.fillq:
                    G.fillq.pop(0)()

        def flush():
            while G.fillq:
                G.fillq.pop(0)()

        def make_hT(src, hT, Acol, shcol, g, xoffs, junkoffs, xnoffs, after_blk=None, defer=False, f32mode=False):
            def st1(blk):
                xb = f32v(xoffs[blk % 2], 1024)
                P.dma("sp", xb, src[blk * 128:(blk + 1) * 128, :])
                junk = f32v(junkoffs[blk % 2], 1024) if f32mode else bfv(junkoffs[blk % 2], 1024)
                ss = stat[:, 32 + 2 * blk:33 + 2 * blk]; rs = stat[:, 33 + 2 * blk:34 + 2 * blk]
                memset("dve", ss, 0.0)
                stt(junk, xb, 1.0, xb, ALU.mult, ALU.mult, accum=ss)
                act(rs, ss, AF.Ln, bias=EPSC, scale=1.0 / D)
                act(rs, rs, AF.Exp, scale=-0.5)

            def st2(blk):
                xb = f32v(xoffs[blk % 2], 1024)
                rs = stat[:, 33 + 2 * blk:34 + 2 * blk]
                xn = xb if f32mode else bfv(xnoffs[blk % 2], 1024)
                ts("dve", xn, xb, rs, None, ALU.mult)

            def st3(blk, hf):
                xn = f32v(xoffs[blk % 2], 1024) if f32mode else bfv(xnoffs[blk % 2], 1024)
                idn = IDENTF[:, :] if f32mode else IDENT
                pst = bank()
                for k4 in range(4):
                    k = hf * 4 + k4
                    mm(pst[:, k4 * 128:(k4 + 1) * 128], xn[:, k * 128:(k + 1) * 128], idn, True, True, signal=(k4 == 3))
                for k4 in range(4):
                    k = hf * 4 + k4
                    dst = hT[:, k, blk * 128:(blk + 1) * 128]
                    if k4 % 2 == 0:
                        ts("dve", dst, pst[:, k4 * 128:(k4 + 1) * 128], Acol[:, k, g:g + 1], shcol[:, k, g:g + 1], ALU.mult, ALU.add)
                    else:
                        act(dst, pst[:, k4 * 128:(k4 + 1) * 128], AF.Identity, bias=shcol[:, k, g:g + 1], scale=Acol[:, k, g:g + 1])
                if hf == 1 and after_blk is not None:
                    after_blk(blk)

            L = lambda f, *a: (lambda: f(*a))
            if xoffs[0] == xoffs[1]:
                stages = []
                for b_ in range(4):
                    stages += [L(st1, b_), L(st2, b_), L(st3, b_, 0), L(st3, b_, 1)]
            else:
                stages = [L(st1, 0), L(st1, 1), L(st2, 0), L(st3, 0, 0), L(st3, 0, 1), L(st1, 2), L(st2, 1), L(st3, 1, 0), L(st3, 1, 1),
                          L(st1, 3), L(st2, 2), L(st3, 2, 0), L(st3, 2, 1), L(st2, 3), L(st3, 3, 0), L(st3, 3, 1)]
            if defer:
                G.fillq.extend(stages)
            else:
                for t in stages:
                    t()

        TMP = [f32v(4096 + i * 512, 512) for i in range(6)]
        G.ti = 0
        G.ntmp = 6

        def tmp():
            t = TMP[G.ti % G.ntmp]
            G.ti += 1
            return t

        HB = 8704
        WHU = bfv(HB, 8, 1536)
        HT_H = bfv(HB + 12288, 8, 512)
        HV = bfv(HB + 16384, 8, 512)
        HX1 = bfv(HB + 20480, 8, 512)
        HX2 = bfv(HB + 24576, 4, 512)
        YRE = bfv(HB + 26624, 8, 512)
        YIM = bfv(HB + 30720, 8, 512)
        TFB = [[bfv(HB + 34816 + (b * 2 + s) * 1024, 1024) for s in range(2)] for b in range(2)]
        TIB = [[bfv(HB + 38912 + (b * 2 + s) * 2048, 2048) for s in range(2)] for b in range(2)]
        KFB = [[bfv(HB + 47104 + (b * 2 + s) * 512, 512) for s in range(2)] for b in range(2)]
        CVB = [bfv(HB + 49152 + i * 512, 512) for i in range(2)]
        JUNK_H = HB + 50176
        XN_H = HB + 51200
        EOD = [[bfv(HB + 52224 + (o * 2 + s) * 4096, 8, 512) for s in range(2)] for o in range(2)]
        PW3B = bfv(HB + 68608, 2048)
        H2T = bfv(HB + 70656, 512)
        KOUT = [bfv(HB + 71168 + i * 512, 512) for i in range(4)]

        for cb_ in range(3):
            wdma(WHU[:, :, cb_ * 512:(cb_ + 1) * 512], win_d[:, 2048 + cb_ * 512:2048 + (cb_ + 1) * 512].rearrange("(k p) n -> p k n", p=128), window=0)
        DIAG = bfv(HB + 60416, 36, 128)
        UBF = [bfv(HB + 65024 + i * 512, 512) for i in range(2)]
        HT_H2 = bfv(HB + 52224, 8, 512)
        HX2B = bfv(HB + 56320, 4, 512)
        CMB = [bfv(HB + 58368 + i * 512, 512) for i in range(4)]
        UREB = [bfv(HB + 66048 + i * 512, 512) for i in range(4)]

        RN = [f32v(1024 + o * 512, 512) for o in range(2)]
        BIASL = [f32v(2048 + o * 512, 512) for o in range(2)]
        PW1 = f32v(3072, 64)[0:33, :]
        PW2 = f32v(3136, 64)[0:64, :]
        WINB = [[f32v(7168 + (b * 2 + s) * 512, 512) for s in range(2)] for b in range(2)]
        H1 = f32v(9216, 512)[0:64, :]
        ZT = f32v(9728, 512)[0:33, :]

        P.dma("sp", PW1, pw1_d)
        P.dma("sp", PW2, pw2_d)

        def sin_layer(dst, ps, bcolv, W):
            a = tmp()[0:64, 0:W]
            ts("dve", a, ps, bcolv, pfr, ALU.add, ALU.mult)
            k = tmp()[0:64, 0:W]
            ts("dve", k, a, 1.0 / TWO_PI, MAGIC, ALU.mult, ALU.add)
            ts("dve", k, k, -MAGIC, None, ALU.add)
            r = tmp()[0:64, 0:W]
            stt(r, k, -TWO_PI, a, ALU.mult, ALU.add)
            ts("dve", r, r, 3.1415925, -3.1415925, ALU.min, ALU.max)
            act(dst, r, AF.Sin)

        BT = [bfv(4624 + i * 512, 512) for i in range(7)] + [bfv(HB + 32768 + i * 512, 512) for i in range(4)]
        ONESB = bfv(HB + 73216, 128)
        memset("dve", ONESB, 1.0)
        G.bti = 0

        def btile():
            t = BT[G.bti % len(BT)]
            G.bti += 1
            return t

        EOD_S = [[bfv(HB + 26624 + s_ * 4096, 8, 512) for s_ in range(2)], [bfv(HB + 38912 + s_ * 4096, 8, 512) for s_ in range(2)]]
        BT_S = [bfv(4624 + i * 512, 512) for i in range(7)] + [bfv(HB + 47104 + i * 512, 512) for i in range(4)]

        def filter_gen(hy, li, EODx, BTx, defer=False):
            L, nT = hy["L"], hy["nT"]
            st = dict(cnt=0, ko=0, bti=0)

            def bt():
                t = BTx[st["bti"] % len(BTx)]
                st["bti"] += 1
                return t

            def setup():
                G.banks = list(range(6)); G.bi = 0
                P.dma("sp", PW1, pw1_d)
                P.dma("sp", PW2, pw2_d)

            def mlp(nb):
                W = min(512, L)
                P.dma("sp", ZT[:, 0:W], hy["zT"][:, nb * 512:nb * 512 + W])
                ps1 = bank()
                mm(ps1[0:64, 0:W], PW1, ZT[:, 0:W], True, True)
                sin_layer(H1[:, 0:W], ps1[0:64, 0:W], pb1, W)
                ps2 = bank()
                mm(ps2[0:64, 0:W], PW2, H1[:, 0:W], True, True)
                sin_layer(H2T[0:64, 0:W], ps2[0:64, 0:W], pb2, W)

            def chunk(nb, c4):
                ncg = nb * 4 + c4
                wt, wbt = WINB[st["cnt"] % 2]; st["cnt"] += 1
                P.dma("sp", wt, hy["win"][ncg * 128:(ncg + 1) * 128, :])
                P.dma("sp", wbt, hy["winb"][ncg * 128:(ncg + 1) * 128, :])
                psq = []
                for q in range(4):
                    pq = bank()
                    mm(pq[:, :], H2T[0:64, c4 * 128:(c4 + 1) * 128], PW3B[0:64, q * 512:(q + 1) * 512], True, True)
                    psq.append(pq)
                for o in range(2):
                    fw = bt(); bw = bt(); af = bt(); ab = bt()
                    tt("dve", fw, psq[o * 2][:, :], wt, ALU.mult)
                    tt("dve", bw, psq[o * 2 + 1][:, :], wbt, ALU.mult)
                    tt("dve", EODx[o][0][:, ncg, :], fw, bw, ALU.add)
                    tt("pool", EODx[o][1][:, ncg, :], fw, bw, ALU.subtract)
                    act(af, fw, AF.Abs)
                    act(ab, bw, AF.Abs)
                    mm(PS[6 + o][:, :], ONESB, af, ncg == 0, False, signal=True)
                    mm(PS[6 + o][:, :], ONESB, ab, False, ncg == nT - 1, signal=True)

            def post():
                for o in range(2):
                    recip(RN[o], PS[6 + o][:, :])
                    P.dma("sp", BIASL[o], hyb_d[o, :].partition_broadcast(128))
                    ts("dve", BIASL[o], BIASL[o], 1.0 / L, None, ALU.mult)
                G.banks = list(range(8)); G.bi = 0

            def kdft(fc):
                tc_t, ts_t = TFB[fc % 2]
                P.dma("sp", tc_t[:, 0:nT * 128], hy["tfc"][fc])
                P.dma("sp", ts_t[:, 0:nT * 128], hy["tfs"][fc])
                for o in range(2):
                    ps = bank()
                    for n in range(nT):
                        mm(ps[:, :], tc_t[:, n * 128:(n + 1) * 128], EODx[o][0][:, n, :], n == 0, n == nT - 1)
                    t = tmp()
                    tt("dve", t, ps[:, :], RN[o], ALU.mult)
                    kre = KOUT[st["ko"] % 4]; st["ko"] += 1
                    stt(kre, t, 1.0 / L, BIASL[o], ALU.mult, ALU.add)
                    P.dma("act", kfs[li, o, 0, fc], kre)
                    ps2 = bank()
                    for n in range(nT):
                        mm(ps2[:, :], ts_t[:, n * 128:(n + 1) * 128], EODx[o][1][:, n, :], n == 0, n == nT - 1)
                    kim = KOUT[st["ko"] % 4]; st["ko"] += 1
                    stt(kim, ps2[:, :], 1.0 / L, RN[o], ALU.mult, ALU.mult)
                    P.dma("act", kfs[li, o, 1, fc], kim)

            L_ = lambda f, *a_: (lambda: f(*a_))
            th = [setup]
            for nb in range(L // 512 if L >= 512 else 1):
                th.append(L_(mlp, nb))
                for c4 in range(min(512, L) // 128):
                    th.append(L_(chunk, nb, c4))
            th.append(post)
            for fc in range(nT):
                th.append(L_(kdft, fc))
            if defer:
                return th
            for t_ in th:
                t_()
            return []

        filter_gen(HY["c"], 0, EOD, BT)
        for ch in range(12):
            for j in range(3):
                ts("dve", DIAG[:, ch * 3 + j, :], IDENT, cwt[:, ch, j:j + 1], None, ALU.mult)
        browt = [f32v(2048 + i * 512, 512) for i in range(2)]
        gtmp = [f32v(3072 + i * 512, 512) for i in range(2)]
        gi = 0
        G.gi = 0
        def mods_blk(blk, wa=None):
            if wa is None:
                wa = WA[blk]
            pst = bank()
            for j in range(4):
                for k in range(8):
                    mm(pst[:, j * 2:j * 2 + 2], wa[:, k, j * 128:(j + 1) * 128], SCB[:, k, :], k == 0, k == 7)
            tt("dve", colm[:, blk * 4:blk * 4 + 4, :], pst[:, 0:8].rearrange("p (a b) -> p a b", a=4),
               bcol[:, blk * 4:blk * 4 + 4].unsqueeze(2).to_broadcast([128, 4, 2]), ALU.add)
            if blk in (4, 5, 10, 11):
                gate = 0 if blk < 6 else 1
                half = blk % 2
                bt = browt[G.gi % 2]
                P.dma("sp", bt, brow_d[0, blk * 512:(blk + 1) * 512].partition_broadcast(128))
                for g in range(2):
                    psg = bank()
                    for k in range(8):
                        mm(psg[:, :], SCBB[:, g, k * 128:(k + 1) * 128], wa[:, k, :], k == 0, k == 7)
                    gt = gtmp[G.gi % 2]; G.gi += 1
                    tt("dve", gt, psg[:, :], bt, ALU.add)
                    P.dma("sp", gsc[gate * 2 + g:gate * 2 + g + 1, half * 512:(half + 1) * 512], gt[0:1, :])
        for blk in range(4):
            mods_blk(blk)
        ts("dve", A1, colm[:, 8:16, :], 1.0, None, ALU.add)
        tt("dve", A1, A1, n1g.unsqueeze(2).to_broadcast([128, 8, 2]), ALU.mult)
        make_hT(xs[0:512, :], HT_H, A1, colm[:, 0:8, :], 1, [0, 1024], [HB + 51200, HB + 73344], [HB + 51200, HB + 73344])
        for blk in range(4, 6):
            mods_blk(blk)

        WA_LATE = [bfv(HB + 52224, 8, 512), bfv(HB + 68608, 8, 512)]

        def late_mod_thunks():
            def dma_(blk):
                wdma(WA_LATE[blk % 2], wada_d[:, blk * 512:(blk + 1) * 512].rearrange("(k p) n -> p k n", p=128), window=0)

            def a2_():
                ts("dve", A2, colm[:, 32:40, :], 1.0, None, ALU.add)
                tt("dve", A2, A2, n2g.unsqueeze(2).to_broadcast([128, 8, 2]), ALU.mult)
            L_ = lambda f, *a_: (lambda: f(*a_))
            comp_ = lambda blk: mods_blk(blk, WA_LATE[blk % 2])
            th = [L_(dma_, 6), L_(dma_, 7)]
            for blk in range(6, 12):
                th.append(L_(comp_, blk))
                if blk + 2 < 12:
                    th.append(L_(dma_, blk + 2))
            th.append(a2_)
            return th


        G.kfi = 0

        def hy_fwd(hy, li, order, blocks):
            nT = hy["nT"]
            for fc in range(nT):
                tc_t, ts_t = TFB[fc % 2]
                P.dma("sp", tc_t[:, 0:nT * 128], hy["tfc"][fc])
                P.dma("sp", ts_t[:, 0:nT * 128], hy["tfs"][fc])
                kre, kim = KFB[G.kfi % 2]; G.kfi += 1
                P.dma("sp", kre, kfs[li, order, 0, fc])
                P.dma("sp", kim, kfs[li, order, 1, fc])
                pr = bank()
                for n in range(nT):
                    mm(pr[:, :], tc_t[:, n * 128:(n + 1) * 128], HV[:, blocks[n], :], n == 0, n == nT - 1)
                pi = bank()
                for n in range(nT):
                    mm(pi[:, :], ts_t[:, n * 128:(n + 1) * 128], HV[:, blocks[n], :], n == 0, n == nT - 1)
                ure = UREB[(G.kfi % 2) * 2]; uim = UREB[(G.kfi % 2) * 2 + 1]
                cp("act", ure, pr[:, :]); cp("act", uim, pi[:, :])
                t1, t2, t3, t4 = CMB
                tt("dve", t1, ure, kre, ALU.mult)
                tt("pool", t2, uim, kim, ALU.mult)
                tt("dve", t3, ure, kim, ALU.mult)
                tt("pool", t4, uim, kre, ALU.mult)
                tt("dve", YRE[:, fc, :], t1, t2, ALU.subtract)
                tt("dve", YIM[:, fc, :], t3, t4, ALU.add)
                fill(1)

        def hy_seq(hy, li, blocks, hx2, z2dst, own_len):
            nT = hy["nT"]; nF = nT
            hy_fwd(hy, li, 0, blocks)
            for tb in range(hy["nTB"]):
                ic, isn = TIB[tb % 2]
                P.dma("sp", ic[:, 0:nF * 256], hy["tic"][tb])
                P.dma("sp", isn[:, 0:nF * 256], hy["tis"][tb])
                for t2 in range(2):
                    tcn = tb * 2 + t2
                    ps = bank()
                    for fc in range(nF):
                        mm(ps[:, :], ic[:, fc * 256 + t2 * 128:fc * 256 + t2 * 128 + 128], YRE[:, fc, :], fc == 0, False)
                        mm(ps[:, :], isn[:, fc * 256 + t2 * 128:fc * 256 + t2 * 128 + 128], YIM[:, fc, :], False, fc == nF - 1)
                    tt("dve", HV[:, blocks[tcn], :], ps[:, :], HX1[:, blocks[tcn], :], ALU.mult)
                fill(1)
            hy_fwd(hy, li, 1, blocks)
            flush()
            for tb in range(own_len // 256):
                ic, isn = TIB[tb % 2]
                P.dma("sp", ic[:, 0:nF * 256], hy["tic"][tb])
                P.dma("sp", isn[:, 0:nF * 256], hy["tis"][tb])
                for cc in range(4):
                    ps = bank()
                    for fc in range(nF):
                        mm(ps[:, 0:256], YRE[:, fc, cc * 128:(cc + 1) * 128], ic[:, fc * 256:(fc + 1) * 256], fc == 0, False)
                        mm(ps[:, 0:256], YIM[:, fc, cc * 128:(cc + 1) * 128], isn[:, fc * 256:(fc + 1) * 256], False, fc == nF - 1)
                    tt("dve", z2dst[:, cc, tb * 256:(tb + 1) * 256], ps[:, 0:256], hx2[:, cc, tb * 256:(tb + 1) * 256], ALU.mult)

        def cmul(pr, pi, kre, kim, yre, yim):
            ure = UREB[(G.kfi % 2) * 2]; uim = UREB[(G.kfi % 2) * 2 + 1]; G.kfi += 1
            cp("act", ure, pr[:, :]); cp("act", uim, pi[:, :])
            t1, t2, t3, t4 = CMB
            tt("dve", t1, ure, kre, ALU.mult)
            tt("pool", t2, uim, kim, ALU.mult)
            tt("dve", t3, ure, kim, ALU.mult)
            tt("pool", t4, uim, kre, ALU.mult)
            tt("dve", yre, t1, t2, ALU.subtract)
            tt("dve", yim, t3, t4, ALU.add)

        def hy_ctx_multi(seqs):
            hy = HY["c"]
            CKF = [[(KFB[0][0], KFB[0][1]), (KFB[1][0], KFB[1][1])],
                   [(bfv(HB + 70656, 512), bfv(HB + 71168, 512)), (bfv(HB + 71680, 512), bfv(HB + 72192, 512))]]
            for fc in range(2):
                P.dma("sp", TFB[fc][0][:, 0:256], hy["tfc"][fc])
                P.dma("sp", TFB[fc][1][:, 0:256], hy["tfs"][fc])
                for o in range(2):
                    P.dma("sp", CKF[o][fc][0], kfs[0, o, 0, fc])
                    P.dma("sp", CKF[o][fc][1], kfs[0, o, 1, fc])
            ic, isn = TIB[0]
            P.dma("sp", ic[:, 0:512], hy["tic"][0])
            P.dma("sp", isn[:, 0:512], hy["tis"][0])

            def fwd(sq, order):
                for fc in range(2):
                    pr = bank()
                    for n in range(2):
                        mm(pr[:, :], TFB[fc][0][:, n * 128:(n + 1) * 128], HV[:, sq["blocks"][n], :], n == 0, n == 1)
                    pi = bank()
                    for n in range(2):
                        mm(pi[:, :], TFB[fc][1][:, n * 128:(n + 1) * 128], HV[:, sq["blocks"][n], :], n == 0, n == 1)
                    y = 2 * sq["slot"] + fc
                    cmul(pr, pi, CKF[order][fc][0], CKF[order][fc][1], YRE[:, y, :], YIM[:, y, :])

            def inv1(sq):
                for t2 in range(2):
                    ps = bank()
                    for fc in range(2):
                        y = 2 * sq["slot"] + fc
                        mm(ps[:, :], ic[:, fc * 256 + t2 * 128:fc * 256 + t2 * 128 + 128], YRE[:, y, :], fc == 0, False)
                        mm(ps[:, :], isn[:, fc * 256 + t2 * 128:fc * 256 + t2 * 128 + 128], YIM[:, y, :], False, fc == 1)
                    b_ = sq["blocks"][t2]
                    tt("dve", HV[:, b_, :], ps[:, :], HX1[:, b_, :], ALU.mult)

            def inv2(sq):
                for cc in range(4):
                    ps = bank()
                    for fc in range(2):
                        y = 2 * sq["slot"] + fc
                        mm(ps[:, 0:256], YRE[:, y, cc * 128:(cc + 1) * 128], ic[:, fc * 256:(fc + 1) * 256], fc == 0, False)
                        mm(ps[:, 0:256], YIM[:, y, cc * 128:(cc + 1) * 128], isn[:, fc * 256:(fc + 1) * 256], False, fc == 1)
                    tt("dve", sq["z2"][:, cc, :], ps[:, 0:256], sq["hx2"][:, cc, :], ALU.mult)

            for sq in seqs:
                fwd(sq, 0)
            for sq in seqs:
                inv1(sq)
            for sq in seqs:
                fwd(sq, 1)
            for sq in seqs:
                inv2(sq)

        def hy_make(src, g, hTb, defer=False, xo=None):
            make_hT(src, hTb, A1, colm[:, 0:8, :], g, xo if xo is not None else [0, 1024], [XN_H, HB + 50176], [XN_H, HB + 50176], defer=defer)

        def hy_proj_st(hTb, seglen, blk0, need_x2, HX2):
            HT_H = hTb
            nseg = 512 // seglen
            nch = 12 if need_x2 else 8

            def stA(ch):
                ps = bank(hold=True)
                for k in range(8):
                    mm(ps[:, :], WHU[:, k, ch * 128:(ch + 1) * 128], HT_H[:, k, :], k == 0, k == 7)
                return ps

            def stB(ch, ps):
                u = UBF[ch % 2]
                cp("act", u, ps[:, :])
                release(ps)
                u3 = u.rearrange("p (s t) -> p s t", s=nseg)
                py = bank()
                py3 = py[:, :].rearrange("p (s t) -> p s t", s=nseg)
                mm(py[:, :], DIAG[:, ch * 3 + 1, :], u, True, False)
                for sg_ in range(nseg):
                    o_ = sg_ * seglen
                    mm(py[:, o_ + 1:o_ + seglen], DIAG[:, ch * 3 + 0, :], u[:, o_:o_ + seglen - 1], False, False)
                    mm(py[:, o_:o_ + seglen - 1], DIAG[:, ch * 3 + 2, :], u[:, o_ + 1:o_ + seglen], False, sg_ == nseg - 1)
                if ch < 8:
                    cv = CVB[ch % 2]
                    act(cv, py[:, :], AF.Identity, bias=cbt[:, ch:ch + 1], scale=1.0)
                    return cv
                act(HX2[:, ch - 8, :], py[:, :], AF.Identity, bias=cbt[:, ch:ch + 1], scale=1.0)
                return None

            def stC(ch, cv):
                pst = bank()
                for b in range(4):
                    mm(pst[:, b * 128:(b + 1) * 128], cv[:, b * 128:(b + 1) * 128], IDENT, True, True, signal=(b == 3))
                dstT = HV if ch < 4 else HX1
                cp("act", dstT[:, blk0:blk0 + 4, (ch % 4) * 128:(ch % 4) * 128 + 128], pst[:, :].rearrange("p (b c) -> p b c", b=4))

            ps_next = stA(0)
            pend = None
            for ch in range(nch):
                ps = ps_next
                if ch + 1 < nch:
                    ps_next = stA(ch + 1)
                cv = stB(ch, ps)
                fill(3)
                if pend is not None:
                    stC(*pend)
                pend = (ch, cv) if cv is not None else None
            if pend is not None:
                stC(*pend)

        fgq = filter_gen(HY["s"], 1, EOD_S, BT_S, defer=True)
        fgq.pop(0)()
        hy_make(xs[512:1024, :], 1, HT_H2, defer=True, xo=[0, 0])
        mk = G.fillq; G.fillq = []
        while mk or fgq:
            if mk:
                G.fillq.append(mk.pop(0))
            if fgq:
                G.fillq.append(fgq.pop(0))
        hy_proj_st(HT_H, 64, 0, True, HX2)
        hy_make(xc[0:512, :], 0, HT_H, defer=True)
        hy_proj_st(HT_H2, 64, 4, False, HX2)
        flush()
        rc = f32v(2048, 8, 128)
        P.dma("sp", rc, rc_d)
        act(lgt, dect, AF.Sigmoid)
        act(lgt, lgt, AF.Ln)
        act(cdt, lgt, AF.Exp, scale=128.0)
        sc_ret = 128.0 ** -0.5
        tA = f32v(3072, 128); tB = f32v(3200, 128)
        for h in range(4):
            lf = lgt[:, h:h + 1]; lb = lgt[:, 4 + h:5 + h]
            act(tA, rc[:, 0, :], AF.Exp, scale=lf)
            tt("dve", tA, tA, rc[:, 1, :], ALU.mult)
            act(tB, rc[:, 2, :], AF.Exp, scale=lb)
            tt("dve", tB, tB, rc[:, 3, :], ALU.mult)
            tt("dve", tA, tA, tB, ALU.add)
            ts("dve", DM[:, h, :], tA, sc_ret, None, ALU.mult)
            act(QDF[:, h, :], rc[:, 4, :], AF.Exp, scale=lf)
            act(QDB[:, h, :], rc[:, 5, :], AF.Exp, scale=lb)
            act(tA, rc[:, 6, :], AF.Exp, scale=lf)
            ts("dve", KDF[:, h, :], tA, sc_ret, None, ALU.mult)
            act(tB, rc[:, 7, :], AF.Exp, scale=lb)
            ts("dve", KDB[:, h, :], tB, sc_ret, None, ALU.mult)

        G.fillq.extend(late_mod_thunks())
        hy_seq(HY["s"], 1, list(range(8)), HX2, Z2T[:, :, 1024:1536], 512)
        hy_make(xc[512:1024, :], 0, HT_H2, defer=True)
        hy_proj_st(HT_H, 256, 0, True, HX2B)
        flush()
        hy_proj_st(HT_H2, 256, 4, True, HX2)
        cseqs = []
        for i in range(2):
            hx = HX2B if i == 0 else HX2
            for s_ in range(2):
                cseqs.append(dict(blocks=[4 * i + 2 * s_, 4 * i + 2 * s_ + 1], slot=2 * i + s_,
                                  hx2=hx[:, :, s_ * 256:(s_ + 1) * 256],
                                  z2=Z2T[:, :, i * 512 + s_ * 256:i * 512 + (s_ + 1) * 256]))
        WRB_pre = [bfv(8704 + cb_ * 4096, 8, 512) for cb_ in range(4)]
        for cb_ in range(4):
            wdma(WRB_pre[cb_], win_d[:, cb_ * 512:(cb_ + 1) * 512].rearrange("(k p) n -> p k n", p=128), window=0)
        hy_ctx_multi(cseqs)
        WRB = [bfv(8704 + cb_ * 4096, 8, 512) for cb_ in range(8)]
        WRO = bfv(41472, 4, 1024); WHO = bfv(45568, 4, 1024); WO = bfv(49664, 8, 1024)
        HT_R = bfv(WK0, 8, 512)
        QT = bfv(WK0 + 4096, 4, 512); KT = bfv(WK0 + 6144, 4, 512)
        KTOK = bfv(WK0 + 8192, 4, 512); VTOK = bfv(WK0 + 10240, 4, 512); SG = bfv(WK0 + 12288, 4, 512)
        MIXT = bfv(WK0 + 4096, 8, 512)
        GOT = bfv(WK0 + 14336, 4, 512)
        ATTM2 = [bfv(WK0 + 16384 + i * 512, 4, 128) for i in range(2)]
        QTF2 = [bfv(WK0 + 17408 + i * 512, 4, 128) for i in range(2)]
        QTB2 = [bfv(WK0 + 18432 + i * 512, 4, 128) for i in range(2)]
        KFt = bfv(WK0 + 19456, 4, 128); KBt = bfv(WK0 + 19968, 4, 128)
        GOTOK2 = [bfv(WK0 + 20480 + i * 512, 4, 128) for i in range(2)]
        SFB = [bfv(WK0 + 21504 + i * 512, 4, 128) for i in range(4)]
        SBB = [bfv(WK0 + 23552 + i * 512, 4, 128) for i in range(4)]
        JUNK_R = WK0 + 4096; XN_R = WK0 + 6144
        XB_OFF = 0
        X1B = f32v(1024, 1024); G1BC = f32v(2048, 1024)
        SFR = f32v(3072, 4, 128); SBR = f32v(3584, 4, 128)
        S0F = f32v(7168, 4, 128); S0B = f32v(7680, 4, 128)

        for cb_ in range(4):
            wdma(WRB[4 + cb_], win_d[:, 3584 + cb_ * 512:3584 + (cb_ + 1) * 512].rearrange("(k p) n -> p k n", p=128))
        wdma(WRO, wro_d.rearrange("(k p) n -> p k n", p=128))
        wdma(WHO, who_d.rearrange("(k p) n -> p k n", p=128))
        for cb_ in range(2):
            wdma(WO[:, :, cb_ * 512:(cb_ + 1) * 512], wo_d[:, cb_ * 512:(cb_ + 1) * 512].rearrange("(k p) n -> p k n", p=128))

        def proj_tm(hT, col0, dst, func=None):
            for blk in range(4):
                ps = bank()
                for k in range(8):
                    mm(ps[:, :], hT[:, k, blk * 128:(blk + 1) * 128], WRB[col0 // 512][:, k, :], k == 0, k == 7)
                if func is None:
                    cp("dve", dst[:, blk, :], ps[:, :])
                else:
                    act(dst[:, blk, :], ps[:, :], func)

        def proj_fm(hT, col0, dst):
            for h in range(4):
                ps = bank()
                for k in range(8):
                    mm(ps[:, :], WRB[col0 // 512][:, k, h * 128:(h + 1) * 128], hT[:, k, :], k == 0, k == 7)
                cp("act", dst[:, h, :], ps[:, :])

        def kv_chunk(c, decay_t, scaled_t):
            tt("dve", scaled_t, KTOK[:, c, :].rearrange("p (h d) -> p h d", h=4), decay_t, ALU.mult)
            ps = bank()
            for h in range(4):
                mm(ps[:, h * 128:(h + 1) * 128], scaled_t[:, h, :], VTOK[:, c, h * 128:(h + 1) * 128], True, True, signal=(h == 3))
            return ps

        def state_step(S, ps, cd0, has_prev):
            ps3 = ps[:, :].rearrange("p (h v) -> p h v", h=4)
            if not has_prev:
                cp("dve", S, ps3)
            else:
                for h in range(4):
                    stt(S[:, h, :], S[:, h, :], cdt[:, cd0 + h:cd0 + h + 1], ps3[:, h, :], ALU.mult, ALU.add)

        def ret_st(seqs):
            items = []
            chains = []
            fS = [SFR, S0F]; bS = [SBR, S0B]
            scl = [(KFt, KBt), (QTF2[0], QTB2[0])]
            for si, (chunks, init_f, init_b, out_idx, slot0) in enumerate(seqs):
                n = len(chunks)
                Sf = fS[si]; Sb = bS[si]; kf_t, kb_t = scl[si]

                def fstep(i, c, Sf=Sf, kf_t=kf_t, slot0=slot0, init_f=init_f):
                    has = init_f or i > 0
                    if has:
                        cp("act", SFB[slot0 + i], Sf)
                    ps = kv_chunk(c, KDF, kf_t)
                    state_step(Sf, ps, 0, has)

                def bstep(i, c, Sb=Sb, kb_t=kb_t, slot0=slot0, init_b=init_b, n=n):
                    has = init_b or i < n - 1
                    if has:
                        cp("act", SBB[slot0 + i], Sb)
                    ps = kv_chunk(c, KDB, kb_t)
                    state_step(Sb, ps, 4, has)
                chains.append([(lambda i=i, c=c, f=fstep: f(i, c)) for i, c in enumerate(chunks)])
                chains.append([(lambda i=i, f=bstep, chunks=chunks: f(i, chunks[i])) for i in range(n - 1, -1, -1)])
                for i, c in enumerate(chunks):
                    items.append((c, slot0 + i, (init_f or i > 0), (init_b or i < n - 1)))
            for k in range(max(len(ch) for ch in chains)):
                for ch in chains:
                    if k < len(ch):
                        ch[k]()
            for si, (chunks, init_f, init_b, out_idx, slot0) in enumerate(seqs):
                if out_idx is not None:
                    P.dma("pool", nsf[out_idx].rearrange("h d v -> d h v"), fS[si])
                    P.dma("pool", nsb[out_idx].rearrange("h d v -> d h v"), bS[si])
            m = len(items)

            def s1(j):
                c, slot, use_f, use_b = items[j]
                csl = slice(c * 128, (c + 1) * 128)
                pa = bank()
                for h in range(4):
                    mm(pa[:, h * 128:(h + 1) * 128], KT[:, h, csl], QT[:, h, csl], True, True, signal=(h == 3))
                tt("dve", ATTM2[j % 2], pa[:, :].rearrange("p (h i) -> p h i", h=4), DM, ALU.mult)
                if use_f:
                    tt("pool", QTF2[j % 2], QT[:, :, csl], QDF, ALU.mult)
                if use_b:
                    tt("pool", QTB2[j % 2], QT[:, :, csl], QDB, ALU.mult)

            def s23(j):
                c, slot, use_f, use_b = items[j]
                ATTM = ATTM2[j % 2]; QTF = QTF2[j % 2]; QTB = QTB2[j % 2]; GOTOK = GOTOK2[j % 2]
                po = bank()
                for h in range(4):
                    o_ = po[:, h * 128:(h + 1) * 128]
                    mm(o_, ATTM[:, h, :], VTOK[:, c, h * 128:(h + 1) * 128], True, not (use_f or use_b))
                    if use_f:
                        mm(o_, QTF[:, h, :], SFB[slot][:, h, :], False, not use_b)
                    if use_b:
                        mm(o_, QTB[:, h, :], SBB[slot][:, h, :], False, True)
                o0 = 24 * (j % 2)
                sums = stat2[:, o0:o0 + 4]; ssq = stat2[:, o0 + 4:o0 + 8]; mean = stat2[:, o0 + 8:o0 + 12]
                m2 = stat2[:, o0 + 12:o0 + 16]; var = stat2[:, o0 + 16:o0 + 20]; rstd = stat2[:, o0 + 20:o0 + 24]
                osb = tmp()
                cp("act", osb, po[:, :])
                o3 = osb.rearrange("p (h v) -> p h v", h=4)
                P.op("dve", lambda e, sums=sums, o3=o3: e.tensor_reduce(out=sums, in_=o3, axis=AX.X, op=ALU.add), reads=[o3], writes=[sums])
                sq = tmp()
                tt("pool", sq, osb, osb, ALU.mult)
                sq3 = sq.rearrange("p (h v) -> p h v", h=4)
                P.op("dve", lambda e, ssq=ssq, sq3=sq3: e.tensor_reduce(out=ssq, in_=sq3, axis=AX.X, op=ALU.add), reads=[sq3], writes=[ssq])
                ts("dve", mean, sums, 1.0 / 128, None, ALU.mult)
                tt("dve", m2, mean, mean, ALU.mult)
                stt(var, ssq, 1.0 / 128, m2, ALU.mult, ALU.subtract)
                act(rstd, var, AF.Ln, bias=EPSC, scale=1.0)
                act(rstd, rstd, AF.Exp, scale=-0.5)
                t = tmp()
                t3 = t.rearrange("p (h v) -> p h v", h=4)
                stt(m2, mean, -1.0, rstd, ALU.mult, ALU.mult)
                for h in range(4):
                    act(t3[:, h, :], o3[:, h, :], AF.Identity, bias=m2[:, h:h + 1], scale=rstd[:, h:h + 1])
                tt("pool", GOTOK, t3, SG[:, c, :].rearrange("p (h v) -> p h v", h=4), ALU.mult)

            def s4(j):
                c = items[j][0]
                csl = slice(c * 128, (c + 1) * 128)
                GOTOK = GOTOK2[j % 2]
                pt = bank()
                for h in range(4):
                    mm(pt[:, h * 128:(h + 1) * 128], GOTOK[:, h, :], IDENT, True, True, signal=(h == 3))
                cp("act", GOT[:, :, csl], pt[:, :].rearrange("p (h i) -> p h i", h=4))

            s1(0)
            for j in range(m):
                if j + 1 < m:
                    s1(j + 1)
                s23(j)
                if j > 0:
                    s4(j - 1)
            s4(m - 1)

        G.x1q = "pool"

        def merge_out(src, g_unused, x1dst, nxt=None):
            for oc in range(8):
                pgr = bank()
                for k in range(8):
                    mm(pgr[:, :], WRB[4 + oc // 4][:, k, (oc % 4) * 128:(oc % 4 + 1) * 128], HT_R[:, k, :], k == 0, k == 7)
                pgh = bank()
                for k in range(8):
                    mm(pgh[:, :], WRB[6 + oc // 4][:, k, (oc % 4) * 128:(oc % 4 + 1) * 128], HT_R[:, k, :], k == 0, k == 7)
                pyr = bank()
                for k in range(4):
                    mm(pyr[:, :], WRO[:, k, oc * 128:(oc + 1) * 128], GOT[:, k, :], k == 0, k == 3)
                pyh = bank()
                for k in range(4):
                    mm(pyh[:, :], WHO[:, k, oc * 128:(oc + 1) * 128], src[:, k, :], k == 0, k == 3)
                sgr = tmp(); sgh = tmp(); m1 = tmp(); m2_ = tmp()
                act(sgr, pgr[:, :], AF.Sigmoid)
                act(sgh, pgh[:, :], AF.Sigmoid)
                tt("dve", m1, pyr[:, :], sgr, ALU.mult)
                tt("dve", m2_, pyh[:, :], sgh, ALU.mult)
                tt("pool", MIXT[:, oc, :], m1, m2_, ALU.add)
            if nxt is not None:
                nxt()
            for blk in range(4):
                xb = f32v(9216 if blk % 2 == 0 else 1024, 1024)
                P.dma("sp", xb, x1dst[1][blk * 128:(blk + 1) * 128, :])
                for hf in range(2):
                    ps = bank(hold=True)
                    for k in range(8):
                        mm(ps[:, :], MIXT[:, k, blk * 128:(blk + 1) * 128], WO[:, k, hf * 512:(hf + 1) * 512], k == 0, k == 7)
                    fill(2)
                    t = tmp()
                    tt("dve", t, ps[:, :], G1BC[:, hf * 512:(hf + 1) * 512], ALU.mult)
                    release(ps)
                    tt("dve", xb[:, hf * 512:(hf + 1) * 512], t, xb[:, hf * 512:(hf + 1) * 512], ALU.add)
                P.dma(G.x1q, x1dst[0][blk * 128:(blk + 1) * 128, :], xb)

        def proj_tm_blk(hT, blk, col0, dst, func=None):
            ps = bank()
            for k in range(8):
                mm(ps[:, :], hT[:, k, blk * 128:(blk + 1) * 128], WRB[col0 // 512][:, k, :], k == 0, k == 7)
            if func is None:
                cp("dve", dst[:, blk, :], ps[:, :])
            else:
                act(dst[:, blk, :], ps[:, :], func)

        XN_G = WK0 + 14336

        def ret_make(src, g, full, defer, with_early=True):
            def early(blk):
                proj_tm_blk(HT_R, blk, 512, KTOK)
                proj_tm_blk(HT_R, blk, 1024, VTOK)
                if full:
                    proj_tm_blk(HT_R, blk, 1536, SG, AF.Silu)
            make_hT(src, HT_R, A1, colm[:, 0:8, :], g, [XB_OFF, 8192], [XN_G, XN_G + 1024], [XN_G, XN_G + 1024],
                    after_blk=(early if with_early else None), defer=defer)
            return early

        def ret_rest(full, early_done):
            if not early_done:
                for blk in range(4):
                    proj_tm_blk(HT_R, blk, 512, KTOK)
                    proj_tm_blk(HT_R, blk, 1024, VTOK)
                    if full:
                        proj_tm_blk(HT_R, blk, 1536, SG, AF.Silu)
            if full:
                proj_fm(HT_R, 0, QT)
                proj_fm(HT_R, 512, KT)

        HT_F = bfv(67584, 8, 512)

        def ffn_make(st, defer, xoffs=None, junk=6144):
            make_hT(x1s[st * 512:(st + 1) * 512, :], HT_F, A2, colm[:, 24:32, :], 0 if st < 2 else 1,
                    xoffs if xoffs is not None else [XB_OFF, 9216], [junk, junk], None, defer=defer, f32mode=True)

        WF1 = bfv(0, 8, NIN)

        def load_wf1():
            for cb_ in range(6):
                wdt = 512 if cb_ < 5 else 256
                for c0 in (cb_ * 512, DFF + cb_ * 512):
                    wdma(WF1[:, :, c0:c0 + wdt], wf1_d[:, c0:c0 + wdt].rearrange("(k p) n -> p k n", p=128))

        P.dma("sp", G1BC, gsc[1, :].partition_broadcast(128))
        P.dma("sp", S0F, s0f_d.rearrange("h d v -> d h v"))
        P.dma("sp", S0B, s0b_d.rearrange("h d v -> d h v"))
        ret_make(xs[512:1024, :], 1, False, False)
        ret_rest(False, True)
        ret_make(xs[0:512, :], 1, True, True, with_early=False)
        cp("dve", SFR, S0F)
        for c in range(4):
            ps = kv_chunk(c, KDF, KFt)
            state_step(SFR, ps, 0, True)
            fill(2)
        cp("dve", SBR, S0B)
        for c in range(3, -1, -1):
            ps = kv_chunk(c, KDB, KBt)
            state_step(SBR, ps, 4, True)
            fill(2)
        flush()
        tS = tmp().rearrange("p (h v) -> p h v", h=4)
        ts("dve", tS, SFR, selt[:, 1:2], None, ALU.mult)
        stt(SFR, S0F, selt[:, 0:1], tS, ALU.mult, ALU.add)
        tS2 = tmp().rearrange("p (h v) -> p h v", h=4)
        ts("dve", tS2, SBR, selt[:, 0:1], None, ALU.mult)
        stt(SBR, S0B, selt[:, 1:2], tS2, ALU.mult, ALU.add)
        ret_rest(True, False)
        ret_st([([0, 1, 2, 3], True, True, None, 0)])

        def nxt_c0():
            P.dma("sp", G1BC, gsc[0, :].partition_broadcast(128))
            ret_make(xc[0:512, :], 0, True, True)
        merge_out(Z2T[:, :, 1024:1536], 1, (x1s[1024:1536, :], xs[0:512, :]), nxt=lambda: ret_make(xc[0:512, :], 0, True, True))
        P.dma("sp", G1BC, gsc[0, :].partition_broadcast(128))
        flush()
        ret_rest(True, True)
        ret_st([([0, 1], False, False, 0, 0), ([2, 3], False, False, 1, 2)])
        merge_out(Z2T[:, :, 0:512], 0, (x1s[0:512, :], xc[0:512, :]), nxt=lambda: ret_make(xc[512:1024, :], 0, True, True))
        flush()
        ret_rest(True, True)
        ret_st([([0, 1], False, False, 2, 0), ([2, 3], False, False, 3, 2)])
        G.x1q = "sp"
        merge_out(Z2T[:, :, 512:1024], 0, (x1s[512:1024, :], xc[512:1024, :]), nxt=lambda: (load_wf1(), ffn_make(0, True, xoffs=[XB_OFF, 8192], junk=7168)))

        WF2 = bfv(45056, 22, 1024)
        UT = bfv(71680, 22, 512)
        JUNK_F = 71680; XN_F = 72704
        G2 = [f32v(2048, 1024), f32v(3072, 1024)]
        FG = f32v(7168, 1024); OUTT = f32v(8192, 1024)
        for cb_ in range(11):
            wdma(WF2[:, cb_ * 2:(cb_ + 1) * 2, :], wf2_d[cb_ * 256:(cb_ + 1) * 256, :].rearrange("(k p) n -> p k n", p=128))
        P.dma("sp", G2[0], gsc[2, :].partition_broadcast(128))
        P.dma("sp", G2[1], gsc[3, :].partition_broadcast(128))
        P.dma("sp", FG, fg_d[0, :].partition_broadcast(128))
        G.ntmp = 4
        for st in range(3):
            g = 0 if st < 2 else 1
            src = x1s[st * 512:(st + 1) * 512, :]
            flush()
            for ch in range(22):
                pa = bank()
                for k in range(8):
                    mm(pa[:, :], WF1[:, k, ch * 128:(ch + 1) * 128], HT_F[:, k, :], k == 0, k == 7)
                pb = bank()
                for k in range(8):
                    mm(pb[:, :], WF1[:, k, DFF + ch * 128:DFF + (ch + 1) * 128], HT_F[:, k, :], k == 0, k == 7)
                sa = tmp()
                act(sa, pa[:, :], AF.Silu)
                tt("dve", UT[:, ch, :], pb[:, :], sa, ALU.mult)
            if st + 1 < 3:
                ffn_make(st + 1, True)
            for blk in range(4):
                P.dma("sp", X1B, src[blk * 128:(blk + 1) * 128, :])
                for hf in range(2):
                    ps = bank(hold=True)
                    for k in range(22):
                        mm(ps[:, :], UT[:, k, blk * 128:(blk + 1) * 128], WF2[:, k, hf * 512:(hf + 1) * 512], k == 0, k == 21)
                    fill(2)
                    t = tmp()
                    tt("dve", t, ps[:, :], G2[g][:, hf * 512:(hf + 1) * 512], ALU.mult)
                    release(ps)
                    tt("pool", X1B[:, hf * 512:(hf + 1) * 512], t, X1B[:, hf * 512:(hf + 1) * 512], ALU.add)
                ss = stat[:, 2:3]; rs = stat[:, 3:4]
                memset("dve", ss, 0.0)
                stt(OUTT, X1B, 1.0, X1B, ALU.mult, ALU.mult, accum=ss)
                act(rs, ss, AF.Ln, bias=EPSC, scale=1.0 / D)
                act(rs, rs, AF.Exp, scale=-0.5)
                stt(OUTT, X1B, rs, FG, ALU.mult, ALU.mult)
                if st < 2:
                    P.dma("pool", yc[st * 512 + blk * 128:st * 512 + (blk + 1) * 128, :], OUTT)
                else:
                    P.dma("pool", ys[blk * 128:(blk + 1) * 128, :], OUTT)
        P.finish()
        P.build()
    G.P = P
    return nc


import math
import ml_dtypes

_BF = ml_dtypes.bfloat16
_CACHE = {}


def _hy_consts(L, pos):
    f32 = np.float32
    n = pos.astype(np.float64)
    t = np.linspace(0.0, 1.0, L, dtype=f32)[pos][:, None]
    ang = (f32(2.0 * math.pi) * np.arange(L, dtype=f32)[:, None] / f32(L))[pos]
    bands = np.linspace(1e-4, 16 - 1, 16, dtype=f32)[None]
    z = np.concatenate([t, np.cos(bands * ang), -np.sin(bands * ang)], axis=-1).astype(f32)
    max_decay = math.log(1e-2) / 0.3
    min_decay = math.log(1e-2) / 1.5
    deltas = np.linspace(min_decay, max_decay, 512, dtype=f32)
    win = np.exp(-t * np.abs(deltas)[None, :]).astype(f32)
    winb = win.copy()
    winb[pos == 0, :] = 0.0
    nT = L // 128
    fidx = np.arange(L, dtype=np.float64) + 0.5
    ang2 = np.pi * np.outer(n, fidx) / L
    Cm = np.cos(ang2); Sm = np.sin(ang2)

    def fwd_tab(M):
        A = M.reshape(nT, 128, nT, 128)
        return np.ascontiguousarray(A.transpose(2, 1, 0, 3).reshape(nT, 128, nT * 128)).astype(_BF)

    def inv_tab(M):
        A = M.reshape(L // 256, 256, nT, 128)
        return np.ascontiguousarray(A.transpose(0, 3, 2, 1).reshape(L // 256, 128, nT * 256)).astype(_BF)
    return dict(zT=np.ascontiguousarray(z.T), win=win, winb=winb,
                tfc=fwd_tab(Cm), tfs=fwd_tab(Sm), tic=inv_tab(Cm), tis=inv_tab(Sm))


def _ret_consts():
    j = np.arange(128)[:, None].astype(np.float32)
    i = np.arange(128)[None, :].astype(np.float32)
    rc = np.zeros((128, 8, 128), np.float32)
    rc[:, 0] = np.where(i >= j, i - j, 0.0)
    rc[:, 1] = (i >= j)
    rc[:, 2] = np.where(j >= i, j - i, 0.0)
    rc[:, 3] = (j >= i)
    rc[:, 4] = np.broadcast_to(i + 1.0, (128, 128))
    rc[:, 5] = np.broadcast_to(128.0 - i, (128, 128))
    rc[:, 6] = np.broadcast_to(127.0 - j, (128, 128))
    rc[:, 7] = np.broadcast_to(j, (128, 128))
    return rc


def _col(v):
    return np.ascontiguousarray(np.asarray(v, np.float32).reshape(-1, 128).T)


def kernel(x_prompt, x_sample, state_ret_fwd, state_ret_bwd, c, c_ctx,
           norm1_g, norm2_g, w_ada, b_ada, w_in, ret_decay_fwd, ret_decay_bwd,
           hy_conv_w, hy_conv_b, hy_pos_w1, hy_pos_b1, hy_pos_w2, hy_pos_b2, hy_pos_w3,
           hy_sin_freq, hy_bias, w_ret_o, w_hy_o, w_out, w_ffn_in, w_ffn_out, final_g):
    f = lambda a: np.ascontiguousarray(np.asarray(a, np.float32))
    if "nc" not in _CACHE:
        _CACHE["nc"] = build_nc()
        _CACHE["hc"] = _hy_consts(256, np.arange(256))
        _CACHE["hs"] = [_hy_consts(1024, np.concatenate([np.arange(512) + hh * 512, np.arange(512) + (1 - hh) * 512])) for hh in range(2)]
        _CACHE["rc"] = _ret_consts()
    nc = _CACHE["nc"]
    x_prompt = f(x_prompt); x_sample = f(x_sample)
    common = {
        "n1g": _col(norm1_g[0]), "n2g": _col(norm2_g[0]), "fg": f(final_g).reshape(1, D),
        "w_ada": f(w_ada[0]), "b_col": _col(b_ada[0]), "b_row": f(b_ada[0]).reshape(1, 6144),
        "w_in": f(w_in[0]),
        "dec": np.concatenate([f(ret_decay_fwd[0]), f(ret_decay_bwd[0])]).reshape(1, 8),
        "cw": np.ascontiguousarray(f(hy_conv_w[0]).reshape(3, 12, 128).transpose(2, 1, 0)),
        "cb": np.ascontiguousarray(f(hy_conv_b[0]).reshape(12, 128).T),
        "pw1": f(hy_pos_w1[0]), "pb1": f(hy_pos_b1[0]).reshape(64, 1), "pw2": f(hy_pos_w2[0]),
        "pb2": f(hy_pos_b2[0]).reshape(64, 1), "pw3": f(hy_pos_w3[0]), "pfr": f(hy_sin_freq[0]).reshape(64, 1),
        "hyb": f(hy_bias[0]), "w_ro": f(w_ret_o[0]), "w_ho": f(w_hy_o[0]), "w_o": f(w_out[0]),
        "w_f1": f(w_ffn_in[0]), "w_f2": f(w_ffn_out[0]),
        "ident": np.eye(128, dtype=np.float32).astype(_BF), "rc": _CACHE["rc"], "identf": np.eye(128, dtype=np.float32),
    }
    for k, v in _CACHE["hc"].items():
        common[k + "_c"] = v
    in_maps = []
    for i in range(8):
        b = i // 2; hh = i % 2
        m = dict(common)
        m["xc"] = x_prompt[4 * i:4 * i + 4].reshape(1024, D)
        own = x_sample[b, hh * 512:(hh + 1) * 512]; oth = x_sample[b, (1 - hh) * 512:(2 - hh) * 512]
        m["xs"] = np.ascontiguousarray(np.concatenate([own, oth], axis=0))
        m["s0f"] = f(state_ret_fwd[b, 0]); m["s0b"] = f(state_ret_bwd[b, 0])
        m["cc"] = np.ascontiguousarray(np.concatenate([_col(c_ctx), _col(c[b])], axis=1))
        a = 1.0 if hh == 0 else 0.0
        m["sel"] = np.ascontiguousarray(np.broadcast_to(np.array([a, 1.0 - a], np.float32), (128, 2)))
        for k, v in _CACHE["hs"][hh].items():
            m[k + "_s"] = v
        in_maps.append(m)
    res = run_bass_kernel_spmd(nc, in_maps, core_ids=list(range(8)))
    R = res.results
    y_prompt = np.concatenate([R[i]["yc"].reshape(4, 256, D) for i in range(8)], axis=0)
    y_sample = np.stack([np.concatenate([R[2 * b]["ys"], R[2 * b + 1]["ys"]], axis=0) for b in range(4)], axis=0)
    nf = np.concatenate([R[i]["nsf"] for i in range(8)], axis=0).reshape(32, 1, 4, 128, 128)
    nb = np.concatenate([R[i]["nsb"] for i in range(8)], axis=0).reshape(32, 1, 4, 128, 128)
    return (y_prompt.astype(np.float32), y_sample.astype(np.float32), nf.astype(np.float32), nb.astype(np.float32))
```

```python
import numpy as np
import concourse.bass as bass
import concourse.mybir as mybir
from concourse.bass_utils import run_bass_kernel_spmd

F32 = mybir.dt.float32
BF16 = mybir.dt.bfloat16
AF = mybir.ActivationFunctionType
ALU = mybir.AluOpType
AX = mybir.AxisListType

N_DMA_SEMS = 12


def _rects(ap):
    t = ap.tensor
    dims = ap.ap
    off = int(ap.offset)
    sp = str(ap.space)
    if sp in ("SB", "PSUM"):
        shp = t.shape
        fs = 1
        for s in shp[1:]:
            fs *= s
        p0 = off // fs
        f0 = off % fs
        npart = dims[0][1]
        if sp == "PSUM":
            return [(t.name, 0, 128, 0, fs)]
        if len(dims) == 3 and dims[2][0] == 1 and 1 < dims[1][1] <= 32 and dims[1][0] > dims[2][1]:
            return [(t.name, p0, p0 + npart, f0 + r * dims[1][0], f0 + r * dims[1][0] + dims[2][1]) for r in range(dims[1][1])]
        ext = 0
        for st, cnt in dims[1:]:
            ext += abs(st) * (cnt - 1)
        return [(t.name, p0, p0 + npart, f0, f0 + ext + 1)]
    ext = 0
    for st, cnt in dims:
        ext += abs(st) * (cnt - 1)
    return [(t.name, 0, 1, off, off + ext + 1)]


class Prog:
    def __init__(self, nc):
        self.nc = nc
        self.lists = {"pe": [], "act": [], "dve": [], "pool": [], "sp": []}
        self.cnt = {"pe": 0, "act": 0, "dve": 0, "pool": 0}
        self.sems = {}
        self.seen = {e: {} for e in self.lists}
        self.regions = {}
        self.dma_cnt = {"sp": 0, "act": 0, "pool": 0}
        self.dma_sems = {}
        self.all_dma = []
        self.n_wait = 0
        self.n_ins = 0

    def _overlaps(self, r):
        recs = self.regions.get(r[0])
        if not recs:
            return []
        out = []
        for k, v in recs.items():
            if k[1] < r[2] and r[1] < k[2] and k[3] < r[4] and r[3] < k[4]:
                out.append((k, v))
        return out

    def _deps(self, reads, writes):
        deps = {}

        def add(d):
            if d is None:
                return
            k, v = d
            if deps.get(k, -1) < v:
                deps[k] = v
        for ap in reads:
            for r in _rects(ap):
                for k, v in self._overlaps(r):
                    add(v[0])
        for ap in writes:
            for r in _rects(ap):
                for k, v in self._overlaps(r):
                    add(v[0])
                    for d in v[1].items():
                        add(d)
        return deps

    def _commit(self, reads, writes, tok):
        for ap in writes:
            for r in _rects(ap):
                recs = self.regions.setdefault(r[0], {})
                for k in list(recs.keys()):
                    if k[1] >= r[1] and k[2] <= r[2] and k[3] >= r[3] and k[4] <= r[4]:
                        del recs[k]
                recs[r] = [tok, {}]
        for ap in reads:
            for r in _rects(ap):
                recs = self.regions.setdefault(r[0], {})
                if r not in recs:
                    recs[r] = [None, {}]
                rd = recs[r][1]
                if rd.get(tok[0], -1) < tok[1]:
                    rd[tok[0]] = tok[1]

    def _emit_waits(self, eng, deps, skip_self=False):
        for k, v in deps.items():
            if skip_self and k == eng:
                continue
            if self.seen[eng].get(k, 0) >= v:
                continue
            self.seen[eng][k] = v
            self.lists[eng].append(("wait", k, v))
            self.n_wait += 1

    def op(self, eng, fn, reads=(), writes=(), signal=True):
        writes = list(writes) + [ap for ap in reads if str(ap.space) == "PSUM"]
        deps = self._deps(reads, writes)
        self._emit_waits(eng, deps, skip_self=(eng == "pe"))
        if signal:
            self.cnt[eng] += 1
            tick = self.cnt[eng]
            self.lists[eng].append(("op", fn, eng, 1))
        else:
            tick = self.cnt[eng] + 1
            self.lists[eng].append(("op", fn, None, 0))
        self._commit(reads, writes, (eng, tick))
        self.n_ins += 1

    def dma(self, q, out, in_, after=(), **kw):
        deps = self._deps([in_], [out])
        for k_, v_ in after:
            deps[k_] = max(deps.get(k_, 0), v_)
        i = self.dma_cnt[q]
        self.dma_cnt[q] += 1
        slot = i % N_DMA_SEMS
        semkey = ("dma", q, slot)
        prev = 16 * (i // N_DMA_SEMS)
        if prev > 0:
            deps[semkey] = max(deps.get(semkey, 0), prev)
        self._emit_waits(q, deps)
        val = prev + 16
        self.lists[q].append(("dma", out, in_, kw, semkey))
        self._commit([in_], [out], (semkey, val))
        self.all_dma.append((semkey, val))
        self.n_ins += 1
        return (semkey, val)

    def finish(self, eng="sp"):
        last = {}
        for k, v in self.all_dma:
            last[k] = max(last.get(k, 0), v)
        self._emit_waits(eng, last)
        fin = {e: c for e, c in self.cnt.items() if c > 0}
        self._emit_waits(eng, fin)

    def build(self):
        nc = self.nc
        from contextlib import ExitStack
        with ExitStack() as es:
            for e in ("pe", "act", "dve", "pool"):
                self.sems[e] = es.enter_context(nc.semaphore("s_" + e))
            for q in ("sp", "act", "pool"):
                for s in range(N_DMA_SEMS):
                    self.sems[("dma", q, s)] = es.enter_context(nc.semaphore("d_%s_%d" % (q, s)))
            block = es.enter_context(nc.Block())
            sems = self.sems

            def replay(lst):
                def run(engine):
                    for it in lst:
                        if it[0] == "wait":
                            engine.wait_ge(sems[it[1]], it[2])
                        elif it[0] == "op":
                            ins = it[1](engine)
                            if it[2] is not None:
                                ins.then_inc(sems[it[2]], 1)
                        else:
                            _, out, in_, kw, semkey = it
                            engine.dma_start(out=out, in_=in_, **kw).then_inc(sems[semkey], 16)
                return run
            block.tensor(replay(self.lists["pe"]))
            block.scalar(replay(self.lists["act"]))
            block.vector(replay(self.lists["dve"]))
            block.gpsimd(replay(self.lists["pool"]))
            block.sync(replay(self.lists["sp"]))


D = 1024
NIN = 5632
DFF = 2816
EPS = 1e-6
MAGIC = 12582912.0
TWO_PI = 6.283185307179586

NBF = 83968
NF32 = 10240


def _prod(s):
    r = 1
    for v in s:
        r *= v
    return r


class Ctx:
    pass


def build_nc():
    nc = bass.Bass("TRN2", target_bir_lowering=False)
    P = Prog(nc)
    G = Ctx()

    def din(name, shape, dt=F32):
        return nc.dram_tensor(name, list(shape), dt, kind="ExternalInput").ap()

    def dout(name, shape, dt=F32):
        return nc.dram_tensor(name, list(shape), dt, kind="ExternalOutput").ap()

    xc = din("xc", [1024, D]); xs = din("xs", [1024, D])
    s0f_d = din("s0f", [4, 128, 128]); s0b_d = din("s0b", [4, 128, 128])
    cc_d = din("cc", [128, 16]); sel_d = din("sel", [128, 2])
    n1g_d = din("n1g", [128, 8]); n2g_d = din("n2g", [128, 8]); fg_d = din("fg", [1, D])
    wada_d = din("w_ada", [D, 6144]); bcol_d = din("b_col", [128, 48]); brow_d = din("b_row", [1, 6144])
    win_d = din("w_in", [D, NIN]); dec_d = din("dec", [1, 8])
    cw_d = din("cw", [128, 12, 3]); cb_d = din("cb", [128, 12])
    pw1_d = din("pw1", [33, 64]); pb1_d = din("pb1", [64, 1]); pw2_d = din("pw2", [64, 64]); pb2_d = din("pb2", [64, 1])
    pw3_d = din("pw3", [64, 2048]); pfr_d = din("pfr", [64, 1]); hyb_d = din("hyb", [2, 512])
    wro_d = din("w_ro", [512, D]); who_d = din("w_ho", [512, D]); wo_d = din("w_o", [D, D])
    wf1_d = din("w_f1", [D, NIN]); wf2_d = din("w_f2", [DFF, D])
    ident_d = din("ident", [128, 128], BF16); rc_d = din("rc", [128, 8, 128]); identf_d = din("identf", [128, 128])
    HY = {}
    for nm, L in (("c", 256), ("s", 1024)):
        nT = L // 128
        HY[nm] = dict(L=L, nT=nT, nF=nT, nTB=L // 256,
                      zT=din("zT_" + nm, [33, L]), win=din("win_" + nm, [L, 512]), winb=din("winb_" + nm, [L, 512]),
                      tfc=din("tfc_" + nm, [nT, 128, nT * 128], BF16), tfs=din("tfs_" + nm, [nT, 128, nT * 128], BF16),
                      tic=din("tic_" + nm, [L // 256, 128, nT * 256], BF16), tis=din("tis_" + nm, [L // 256, 128, nT * 256], BF16))
    yc = dout("yc", [1024, D]); ys = dout("ys", [512, D])
    nsf = dout("nsf", [4, 4, 128, 128]); nsb = dout("nsb", [4, 4, 128, 128])
    x1s = nc.dram_tensor("x1s", [1536, D], F32, kind="Internal").ap()
    kfs = nc.dram_tensor("kfs", [2, 2, 2, 8, 128, 512], BF16, kind="Internal").ap()
    gsc = nc.dram_tensor("gsc", [4, D], F32, kind="Internal").ap()

    from contextlib import ExitStack
    with ExitStack() as es:
        ABF = es.enter_context(nc.sbuf_tensor("abf", [128, NBF], BF16))
        AF_ = es.enter_context(nc.sbuf_tensor("af32", [128, NF32], F32))
        SM = es.enter_context(nc.sbuf_tensor("sm", [128, 512], F32))
        IDENTF = es.enter_context(nc.sbuf_tensor("identf_sb", [128, 128], F32))
        PS = [es.enter_context(nc.psum_tensor("ps%d" % i, [128, 512], F32)) for i in range(8)]

        def bfv(off, *shape):
            n = _prod(shape)
            assert off + n <= NBF, (off, n)
            ap = ABF[:, off:off + n]
            if len(shape) == 2:
                ap = ap.rearrange("p (a b) -> p a b", a=shape[0])
            return ap

        def f32v(off, *shape):
            n = _prod(shape)
            assert off + n <= NF32, (off, n)
            ap = AF_[:, off:off + n]
            if len(shape) == 2:
                ap = ap.rearrange("p (a b) -> p a b", a=shape[0])
            return ap

        G.banks = list(range(8)); G.bi = 0

        G.live = set()

        def bank(hold=False):
            for _ in range(16):
                b = G.banks[G.bi % len(G.banks)]
                G.bi += 1
                if b not in G.live:
                    break
            if hold:
                G.live.add(b)
            return PS[b]

        def release(ps):
            G.live.discard(PS.index(ps))

        def mm(out, lhsT, rhs, start, stop, signal=None):
            P.op("pe", lambda e: e.matmul(out, lhsT=lhsT, rhs=rhs, start=start, stop=stop),
                 reads=[lhsT, rhs], writes=[out], signal=(stop if signal is None else signal))

        def act(out, in_, func, bias=None, scale=None, accum=None):
            kw = {}
            rd = [in_]
            wr = [out]
            if bias is not None:
                kw["bias"] = bias
                if not isinstance(bias, float):
                    rd.append(bias)
            if scale is not None:
                kw["scale"] = scale
                if not isinstance(scale, float):
                    rd.append(scale)
            if accum is not None:
                kw["accum_out"] = accum
                wr.append(accum)
            P.op("act", lambda e: e.activation(out=out, in_=in_, func=func, **kw), reads=rd, writes=wr)

        def tt(eng, out, in0, in1, op):
            P.op(eng, lambda e: e.tensor_tensor(out=out, in0=in0, in1=in1, op=op), reads=[in0, in1], writes=[out])

        def ts(eng, out, in0, s1, s2, op0, op1=None):
            rd = [in0] + [s for s in (s1, s2) if s is not None and not isinstance(s, float)]
            if op1 is None:
                P.op(eng, lambda e: e.tensor_scalar(out=out, in0=in0, scalar1=s1, scalar2=None, op0=op0), reads=rd, writes=[out])
            else:
                P.op(eng, lambda e: e.tensor_scalar(out=out, in0=in0, scalar1=s1, scalar2=s2, op0=op0, op1=op1), reads=rd, writes=[out])

        def stt(out, in0, scalar, in1, op0, op1, accum=None):
            rd = [in0, in1] + ([] if isinstance(scalar, float) else [scalar])
            if accum is None:
                P.op("dve", lambda e: e.scalar_tensor_tensor(out=out, in0=in0, scalar=scalar, in1=in1, op0=op0, op1=op1), reads=rd, writes=[out])
            else:
                P.op("dve", lambda e: e.scalar_tensor_tensor(out=out, in0=in0, scalar=scalar, in1=in1, op0=op0, op1=op1, accum_out=accum), reads=rd, writes=[out, accum])

        def cp(eng, out, in_):
            if eng == "act":
                P.op("act", lambda e: e.copy(out=out, in_=in_), reads=[in_], writes=[out])
            else:
                P.op(eng, lambda e: e.tensor_copy(out=out, in_=in_), reads=[in_], writes=[out])

        def recip(out, in_):
            P.op("dve", lambda e: e.reciprocal(out=out, in_=in_), reads=[in_], writes=[out])

        def memset(eng, out, val):
            P.op(eng, lambda e: e.memset(out, val), reads=[], writes=[out])


        G.wtok = []

        def wdma(out, in_, window=2):
            after = [G.wtok[-window]] if (window and len(G.wtok) >= window) else []
            G.wtok.append(P.dma("pool", out, in_, after=after))

        colm = SM[:, 0:96].rearrange("p (a b) -> p a b", a=48)
        A1 = SM[:, 96:112].rearrange("p (a b) -> p a b", a=8)
        A2 = SM[:, 112:128].rearrange("p (a b) -> p a b", a=8)
        cond = SM[:, 128:144]
        selt = SM[:, 144:146]
        dect = SM[:, 146:154]
        lgt = SM[:, 154:162]
        cdt = SM[:, 162:170]
        n1g = SM[:, 170:178]; n2g = SM[:, 178:186]
        bcol = SM[:, 186:234]
        cbt = SM[:, 234:246]
        cwt = SM[:, 246:282].rearrange("p (a b) -> p a b", a=12)
        stat = SM[:, 282:330]
        pb1 = SM[0:64, 330:331]; pb2 = SM[0:64, 331:332]; pfr = SM[0:64, 332:333]
        ones_f = SM[:, 384:512]
        EPSC = SM[:, 333:334]
        stat2 = SM[:, 334:382]

        IDENT = ABF[:, NBF - 128:NBF]
        DM = bfv(0, 4, 128); QDF = bfv(512, 4, 128); QDB = bfv(1024, 4, 128); KDF = bfv(1536, 4, 128); KDB = bfv(2048, 4, 128)
        Z2T = bfv(2560, 4, 1536)
        WK0 = 57856

        P.dma("sp", IDENT, ident_d)
        P.dma("sp", IDENTF[:, :], identf_d)
        P.dma("sp", cond, cc_d)
        P.dma("sp", selt, sel_d)
        P.dma("sp", dect, dec_d[0, :].partition_broadcast(128))
        P.dma("sp", n1g, n1g_d); P.dma("sp", n2g, n2g_d)
        P.dma("sp", bcol, bcol_d)
        P.dma("sp", cbt, cb_d)
        P.dma("sp", SM[:, 246:282], cw_d.rearrange("p a b -> p (a b)"))
        P.dma("sp", pb1, pb1_d); P.dma("sp", pb2, pb2_d); P.dma("sp", pfr, pfr_d)
        memset("dve", ones_f, 1.0)
        memset("dve", EPSC, EPS)

        scf = f32v(1280, 16)
        act(scf, cond, AF.Silu)
        SCB = bfv(2560, 8, 2)
        cp("dve", SCB, scf.rearrange("p (g k) -> p k g", g=2))
        SCBB = bfv(2576, 2, 8 * 128)
        onesb = f32v(1296, 128)
        memset("dve", onesb, 1.0)
        for g in range(2):
            for k in range(8):
                ts("dve", SCBB[:, g, k * 128:(k + 1) * 128], onesb, scf[:, g * 8 + k:g * 8 + k + 1], None, ALU.mult)
        P.dma("pool", ABF[0:64, 8704 + 68608:8704 + 68608 + 2048], pw3_d)
        WA = [bfv(8704 + 12288 + i * 4096, 8, 512) for i in range(5)] + [bfv(8704 + 38912 + i * 4096, 8, 512) for i in range(3)]
        for blk in range(6):
            wdma(WA[blk], wada_d[:, blk * 512:(blk + 1) * 512].rearrange("(k p) n -> p k n", p=128), window=0)
        G.fillq = []

        def fill(n):
            for _ in range(n):
                if G.fillq:
                    G.fillq.pop(0)()

        def flush():
            while G.fillq:
                G.fillq.pop(0)()

        def make_hT(src, hT, Acol, shcol, g, xoffs, junkoffs, xnoffs, after_blk=None, defer=False, f32mode=False):
            def st1(blk):
                xb = f32v(xoffs[blk % 2], 1024)
                P.dma("sp", xb, src[blk * 128:(blk + 1) * 128, :])
                junk = f32v(junkoffs[blk % 2], 1024) if f32mode else bfv(junkoffs[blk % 2], 1024)
                ss = stat[:, 32 + 2 * blk:33 + 2 * blk]; rs = stat[:, 33 + 2 * blk:34 + 2 * blk]
                memset("dve", ss, 0.0)
                stt(junk, xb, 1.0, xb, ALU.mult, ALU.mult, accum=ss)
                act(rs, ss, AF.Ln, bias=EPSC, scale=1.0 / D)
                act(rs, rs, AF.Exp, scale=-0.5)

            def st2(blk):
                xb = f32v(xoffs[blk % 2], 1024)
                rs = stat[:, 33 + 2 * blk:34 + 2 * blk]
                xn = xb if f32mode else bfv(xnoffs[blk % 2], 1024)
                ts("dve", xn, xb, rs, None, ALU.mult)

            def st3(blk, hf):
                xn = f32v(xoffs[blk % 2], 1024) if f32mode else bfv(xnoffs[blk % 2], 1024)
                idn = IDENTF[:, :] if f32mode else IDENT
                pst = bank()
                for k4 in range(4):
                    k = hf * 4 + k4
                    mm(pst[:, k4 * 128:(k4 + 1) * 128], xn[:, k * 128:(k + 1) * 128], idn, True, True, signal=(k4 == 3))
                for k4 in range(4):
                    k = hf * 4 + k4
                    dst = hT[:, k, blk * 128:(blk + 1) * 128]
                    if k4 % 2 == 0:
                        ts("dve", dst, pst[:, k4 * 128:(k4 + 1) * 128], Acol[:, k, g:g + 1], shcol[:, k, g:g + 1], ALU.mult, ALU.add)
                    else:
                        act(dst, pst[:, k4 * 128:(k4 + 1) * 128], AF.Identity, bias=shcol[:, k, g:g + 1], scale=Acol[:, k, g:g + 1])
                if hf == 1 and after_blk is not None:
                    after_blk(blk)

            L = lambda f, *a: (lambda: f(*a))
            if xoffs[0] == xoffs[1]:
                stages = []
                for b_ in range(4):
                    stages += [L(st1, b_), L(st2, b_), L(st3, b_, 0), L(st3, b_, 1)]
            else:
                stages = [L(st1, 0), L(st1, 1), L(st2, 0), L(st3, 0, 0), L(st3, 0, 1), L(st1, 2), L(st2, 1), L(st3, 1, 0), L(st3, 1, 1),
                          L(st1, 3), L(st2, 2), L(st3, 2, 0), L(st3, 2, 1), L(st2, 3), L(st3, 3, 0), L(st3, 3, 1)]
            if defer:
                G.fillq.extend(stages)
            else:
                for t in stages:
                    t()

        TMP = [f32v(4096 + i * 512, 512) for i in range(6)]
        G.ti = 0
        G.ntmp = 6

        def tmp():
            t = TMP[G.ti % G.ntmp]
            G.ti += 1
            return t

        HB = 8704
        WHU = bfv(HB, 8, 1536)
        HT_H = bfv(HB + 12288, 8, 512)
        HV = bfv(HB + 16384, 8, 512)
        HX1 = bfv(HB + 20480, 8, 512)
        HX2 = bfv(HB + 24576, 4, 512)
        YRE = bfv(HB + 26624, 8, 512)
        YIM = bfv(HB + 30720, 8, 512)
        TFB = [[bfv(HB + 34816 + (b * 2 + s) * 1024, 1024) for s in range(2)] for b in range(2)]
        TIB = [[bfv(HB + 38912 + (b * 2 + s) * 2048, 2048) for s in range(2)] for b in range(2)]
        KFB = [[bfv(HB + 47104 + (b * 2 + s) * 512, 512) for s in range(2)] for b in range(2)]
        CVB = [bfv(HB + 49152 + i * 512, 512) for i in range(2)]
        JUNK_H = HB + 50176
        XN_H = HB + 51200
        EOD = [[bfv(HB + 52224 + (o * 2 + s) * 4096, 8, 512) for s in range(2)] for o in range(2)]
        PW3B = bfv(HB + 68608, 2048)
        H2T = bfv(HB + 70656, 512)
        KOUT = [bfv(HB + 71168 + i * 512, 512) for i in range(4)]

        for cb_ in range(3):
            wdma(WHU[:, :, cb_ * 512:(cb_ + 1) * 512], win_d[:, 2048 + cb_ * 512:2048 + (cb_ + 1) * 512].rearrange("(k p) n -> p k n", p=128), window=0)
        DIAG = bfv(HB + 60416, 36, 128)
        UBF = [bfv(HB + 65024 + i * 512, 512) for i in range(2)]
        HT_H2 = bfv(HB + 52224, 8, 512)
        HX2B = bfv(HB + 56320, 4, 512)
        CMB = [bfv(HB + 58368 + i * 512, 512) for i in range(4)]
        UREB = [bfv(HB + 66048 + i * 512, 512) for i in range(4)]

        RN = [f32v(1024 + o * 512, 512) for o in range(2)]
        BIASL = [f32v(2048 + o * 512, 512) for o in range(2)]
        PW1 = f32v(3072, 64)[0:33, :]
        PW2 = f32v(3136, 64)[0:64, :]
        WINB = [[f32v(7168 + (b * 2 + s) * 512, 512) for s in range(2)] for b in range(2)]
        H1 = f32v(9216, 512)[0:64, :]
        ZT = f32v(9728, 512)[0:33, :]

        P.dma("sp", PW1, pw1_d)
        P.dma("sp", PW2, pw2_d)

        def sin_layer(dst, ps, bcolv, W):
            a = tmp()[0:64, 0:W]
            ts("dve", a, ps, bcolv, pfr, ALU.add, ALU.mult)
            k = tmp()[0:64, 0:W]
            ts("dve", k, a, 1.0 / TWO_PI, MAGIC, ALU.mult, ALU.add)
            ts("dve", k, k, -MAGIC, None, ALU.add)
            r = tmp()[0:64, 0:W]
            stt(r, k, -TWO_PI, a, ALU.mult, ALU.add)
            ts("dve", r, r, 3.1415925, -3.1415925, ALU.min, ALU.max)
            act(dst, r, AF.Sin)

        BT = [bfv(4624 + i * 512, 512) for i in range(7)] + [bfv(HB + 32768 + i * 512, 512) for i in range(4)]
        ONESB = bfv(HB + 73216, 128)
        memset("dve", ONESB, 1.0)
        G.bti = 0

        def btile():
            t = BT[G.bti % len(BT)]
            G.bti += 1
            return t

        EOD_S = [[bfv(HB + 26624 + s_ * 4096, 8, 512) for s_ in range(2)], [bfv(HB + 38912 + s_ * 4096, 8, 512) for s_ in range(2)]]
        BT_S = [bfv(4624 + i * 512, 512) for i in range(7)] + [bfv(HB + 47104 + i * 512, 512) for i in range(4)]

        def filter_gen(hy, li, EODx, BTx, defer=False):
            L, nT = hy["L"], hy["nT"]
            st = dict(cnt=0, ko=0, bti=0)

            def bt():
                t = BTx[st["bti"] % len(BTx)]
                st["bti"] += 1
                return t

            def setup():
                G.banks = list(range(6)); G.bi = 0
                P.dma("sp", PW1, pw1_d)
                P.dma("sp", PW2, pw2_d)

            def mlp(nb):
                W = min(512, L)
                P.dma("sp", ZT[:, 0:W], hy["zT"][:, nb * 512:nb * 512 + W])
                ps1 = bank()
                mm(ps1[0:64, 0:W], PW1, ZT[:, 0:W], True, True)
                sin_layer(H1[:, 0:W], ps1[0:64, 0:W], pb1, W)
                ps2 = bank()
                mm(ps2[0:64, 0:W], PW2, H1[:, 0:W], True, True)
                sin_layer(H2T[0:64, 0:W], ps2[0:64, 0:W], pb2, W)

            def chunk(nb, c4):
                ncg = nb * 4 + c4
                wt, wbt = WINB[st["cnt"] % 2]; st["cnt"] += 1
                P.dma("sp", wt, hy["win"][ncg * 128:(ncg + 1) * 128, :])
                P.dma("sp", wbt, hy["winb"][ncg * 128:(ncg + 1) * 128, :])
                psq = []
                for q in range(4):
                    pq = bank()
                    mm(pq[:, :], H2T[0:64, c4 * 128:(c4 + 1) * 128], PW3B[0:64, q * 512:(q + 1) * 512], True, True)
                    psq.append(pq)
                for o in range(2):
                    fw = bt(); bw = bt(); af = bt(); ab = bt()
                    tt("dve", fw, psq[o * 2][:, :], wt, ALU.mult)
                    tt("dve", bw, psq[o * 2 + 1][:, :], wbt, ALU.mult)
                    tt("dve", EODx[o][0][:, ncg, :], fw, bw, ALU.add)
                    tt("pool", EODx[o][1][:, ncg, :], fw, bw, ALU.subtract)
                    act(af, fw, AF.Abs)
                    act(ab, bw, AF.Abs)
                    mm(PS[6 + o][:, :], ONESB, af, ncg == 0, False, signal=True)
                    mm(PS[6 + o][:, :], ONESB, ab, False, ncg == nT - 1, signal=True)

            def post():
                for o in range(2):
                    recip(RN[o], PS[6 + o][:, :])
                    P.dma("sp", BIASL[o], hyb_d[o, :].partition_broadcast(128))
                    ts("dve", BIASL[o], BIASL[o], 1.0 / L, None, ALU.mult)
                G.banks = list(range(8)); G.bi = 0

            def kdft(fc):
                tc_t, ts_t = TFB[fc % 2]
                P.dma("sp", tc_t[:, 0:nT * 128], hy["tfc"][fc])
                P.dma("sp", ts_t[:, 0:nT * 128], hy["tfs"][fc])
                for o in range(2):
                    ps = bank()
                    for n in range(nT):
                        mm(ps[:, :], tc_t[:, n * 128:(n + 1) * 128], EODx[o][0][:, n, :], n == 0, n == nT - 1)
                    t = tmp()
                    tt("dve", t, ps[:, :], RN[o], ALU.mult)
                    kre = KOUT[st["ko"] % 4]; st["ko"] += 1
                    stt(kre, t, 1.0 / L, BIASL[o], ALU.mult, ALU.add)
                    P.dma("act", kfs[li, o, 0, fc], kre)
                    ps2 = bank()
                    for n in range(nT):
                        mm(ps2[:, :], ts_t[:, n * 128:(n + 1) * 128], EODx[o][1][:, n, :], n == 0, n == nT - 1)
                    kim = KOUT[st["ko"] % 4]; st["ko"] += 1
                    stt(kim, ps2[:, :], 1.0 / L, RN[o], ALU.mult, ALU.mult)
                    P.dma("act", kfs[li, o, 1, fc], kim)

            L_ = lambda f, *a_: (lambda: f(*a_))
            th = [setup]
            for nb in range(L // 512 if L >= 512 else 1):
                th.append(L_(mlp, nb))
                for c4 in range(min(512, L) // 128):
                    th.append(L_(chunk, nb, c4))
            th.append(post)
            for fc in range(nT):
                th.append(L_(kdft, fc))
            if defer:
                return th
            for t_ in th:
                t_()
            return []

        filter_gen(HY["c"], 0, EOD, BT)
        for ch in range(12):
            for j in range(3):
                ts("dve", DIAG[:, ch * 3 + j, :], IDENT, cwt[:, ch, j:j + 1], None, ALU.mult)
        browt = [f32v(2048 + i * 512, 512) for i in range(2)]
        gtmp = [f32v(3072 + i * 512, 512) for i in range(2)]
        gi = 0
        G.gi = 0
        def mods_blk(blk, wa=None):
            if wa is None:
                wa = WA[blk]
            pst = bank()
            for j in range(4):
                for k in range(8):
                    mm(pst[:, j * 2:j * 2 + 2], wa[:, k, j * 128:(j + 1) * 128], SCB[:, k, :], k == 0, k == 7)
            tt("dve", colm[:, blk * 4:blk * 4 + 4, :], pst[:, 0:8].rearrange("p (a b) -> p a b", a=4),
               bcol[:, blk * 4:blk * 4 + 4].unsqueeze(2).to_broadcast([128, 4, 2]), ALU.add)
            if blk in (4, 5, 10, 11):
                gate = 0 if blk < 6 else 1
                half = blk % 2
                bt = browt[G.gi % 2]
                P.dma("sp", bt, brow_d[0, blk * 512:(blk + 1) * 512].partition_broadcast(128))
                for g in range(2):
                    psg = bank()
                    for k in range(8):
                        mm(psg[:, :], SCBB[:, g, k * 128:(k + 1) * 128], wa[:, k, :], k == 0, k == 7)
                    gt = gtmp[G.gi % 2]; G.gi += 1
                    tt("dve", gt, psg[:, :], bt, ALU.add)
                    P.dma("sp", gsc[gate * 2 + g:gate * 2 + g + 1, half * 512:(half + 1) * 512], gt[0:1, :])
        for blk in range(4):
            mods_blk(blk)
        ts("dve", A1, colm[:, 8:16, :], 1.0, None, ALU.add)
        tt("dve", A1, A1, n1g.unsqueeze(2).to_broadcast([128, 8, 2]), ALU.mult)
        make_hT(xs[0:512, :], HT_H, A1, colm[:, 0:8, :], 1, [0, 1024], [HB + 51200, HB + 73344], [HB + 51200, HB + 73344])
        for blk in range(4, 6):
            mods_blk(blk)

        WA_LATE = [bfv(HB + 52224, 8, 512), bfv(HB + 68608, 8, 512)]

        def late_mod_thunks():
            def dma_(blk):
                wdma(WA_LATE[blk % 2], wada_d[:, blk * 512:(blk + 1) * 512].rearrange("(k p) n -> p k n", p=128), window=0)

            def a2_():
                ts("dve", A2, colm[:, 32:40, :], 1.0, None, ALU.add)
                tt("dve", A2, A2, n2g.unsqueeze(2).to_broadcast([128, 8, 2]), ALU.mult)
            L_ = lambda f, *a_: (lambda: f(*a_))
            comp_ = lambda blk: mods_blk(blk, WA_LATE[blk % 2])
            th = [L_(dma_, 6), L_(dma_, 7)]
            for blk in range(6, 12):
                th.append(L_(comp_, blk))
                if blk + 2 < 12:
                    th.append(L_(dma_, blk + 2))
            th.append(a2_)
            return th


        G.kfi = 0

        def hy_fwd(hy, li, order, blocks):
            nT = hy["nT"]
            for fc in range(nT):
                tc_t, ts_t = TFB[fc % 2]
                P.dma("sp", tc_t[:, 0:nT * 128], hy["tfc"][fc])
                P.dma("sp", ts_t[:, 0:nT * 128], hy["tfs"][fc])
                kre, kim = KFB[G.kfi % 2]; G.kfi += 1
                P.dma("sp", kre, kfs[li, order, 0, fc])
                P.dma("sp", kim, kfs[li, order, 1, fc])
                pr = bank()
                for n in range(nT):
                    mm(pr[:, :], tc_t[:, n * 128:(n + 1) * 128], HV[:, blocks[n], :], n == 0, n == nT - 1)
                pi = bank()
                for n in range(nT):
                    mm(pi[:, :], ts_t[:, n * 128:(n + 1) * 128], HV[:, blocks[n], :], n == 0, n == nT - 1)
                ure = UREB[(G.kfi % 2) * 2]; uim = UREB[(G.kfi % 2) * 2 + 1]
                cp("act", ure, pr[:, :]); cp("act", uim, pi[:, :])
                t1, t2, t3, t4 = CMB
                tt("dve", t1, ure, kre, ALU.mult)
                tt("pool", t2, uim, kim, ALU.mult)
                tt("dve", t3, ure, kim, ALU.mult)
                tt("pool", t4, uim, kre, ALU.mult)
                tt("dve", YRE[:, fc, :], t1, t2, ALU.subtract)
                tt("dve", YIM[:, fc, :], t3, t4, ALU.add)
                fill(1)

        def hy_seq(hy, li, blocks, hx2, z2dst, own_len):
            nT = hy["nT"]; nF = nT
            hy_fwd(hy, li, 0, blocks)
            for tb in range(hy["nTB"]):
                ic, isn = TIB[tb % 2]
                P.dma("sp", ic[:, 0:nF * 256], hy["tic"][tb])
                P.dma("sp", isn[:, 0:nF * 256], hy["tis"][tb])
                for t2 in range(2):
                    tcn = tb * 2 + t2
                    ps = bank()
                    for fc in range(nF):
                        mm(ps[:, :], ic[:, fc * 256 + t2 * 128:fc * 256 + t2 * 128 + 128], YRE[:, fc, :], fc == 0, False)
                        mm(ps[:, :], isn[:, fc * 256 + t2 * 128:fc * 256 + t2 * 128 + 128], YIM[:, fc, :], False, fc == nF - 1)
                    tt("dve", HV[:, blocks[tcn], :], ps[:, :], HX1[:, blocks[tcn], :], ALU.mult)
                fill(1)
            hy_fwd(hy, li, 1, blocks)
            flush()
            for tb in range(own_len // 256):
                ic, isn = TIB[tb % 2]
                P.dma("sp", ic[:, 0:nF * 256], hy["tic"][tb])
                P.dma("sp", isn[:, 0:nF * 256], hy["tis"][tb])
                for cc in range(4):
                    ps = bank()
                    for fc in range(nF):
                        mm(ps[:, 0:256], YRE[:, fc, cc * 128:(cc + 1) * 128], ic[:, fc * 256:(fc + 1) * 256], fc == 0, False)
                        mm(ps[:, 0:256], YIM[:, fc, cc * 128:(cc + 1) * 128], isn[:, fc * 256:(fc + 1) * 256], False, fc == nF - 1)
                    tt("dve", z2dst[:, cc, tb * 256:(tb + 1) * 256], ps[:, 0:256], hx2[:, cc, tb * 256:(tb + 1) * 256], ALU.mult)

        def cmul(pr, pi, kre, kim, yre, yim):
            ure = UREB[(G.kfi % 2) * 2]; uim = UREB[(G.kfi % 2) * 2 + 1]; G.kfi += 1
            cp("act", ure, pr[:, :]); cp("act", uim, pi[:, :])
            t1, t2, t3, t4 = CMB
            tt("dve", t1, ure, kre, ALU.mult)
            tt("pool", t2, uim, kim, ALU.mult)
            tt("dve", t3, ure, kim, ALU.mult)
            tt("pool", t4, uim, kre, ALU.mult)
            tt("dve", yre, t1, t2, ALU.subtract)
            tt("dve", yim, t3, t4, ALU.add)

        def hy_ctx_multi(seqs):
            hy = HY["c"]
            CKF = [[(KFB[0][0], KFB[0][1]), (KFB[1][0], KFB[1][1])],
                   [(bfv(HB + 70656, 512), bfv(HB + 71168, 512)), (bfv(HB + 71680, 512), bfv(HB + 72192, 512))]]
            for fc in range(2):
                P.dma("sp", TFB[fc][0][:, 0:256], hy["tfc"][fc])
                P.dma("sp", TFB[fc][1][:, 0:256], hy["tfs"][fc])
                for o in range(2):
                    P.dma("sp", CKF[o][fc][0], kfs[0, o, 0, fc])
                    P.dma("sp", CKF[o][fc][1], kfs[0, o, 1, fc])
            ic, isn = TIB[0]
            P.dma("sp", ic[:, 0:512], hy["tic"][0])
            P.dma("sp", isn[:, 0:512], hy["tis"][0])

            def fwd(sq, order):
                for fc in range(2):
                    pr = bank()
                    for n in range(2):
                        mm(pr[:, :], TFB[fc][0][:, n * 128:(n + 1) * 128], HV[:, sq["blocks"][n], :], n == 0, n == 1)
                    pi = bank()
                    for n in range(2):
                        mm(pi[:, :], TFB[fc][1][:, n * 128:(n + 1) * 128], HV[:, sq["blocks"][n], :], n == 0, n == 1)
                    y = 2 * sq["slot"] + fc
                    cmul(pr, pi, CKF[order][fc][0], CKF[order][fc][1], YRE[:, y, :], YIM[:, y, :])

            def inv1(sq):
                for t2 in range(2):
                    ps = bank()
                    for fc in range(2):
                        y = 2 * sq["slot"] + fc
                        mm(ps[:, :], ic[:, fc * 256 + t2 * 128:fc * 256 + t2 * 128 + 128], YRE[:, y, :], fc == 0, False)
                        mm(ps[:, :], isn[:, fc * 256 + t2 * 128:fc * 256 + t2 * 128 + 128], YIM[:, y, :], False, fc == 1)
                    b_ = sq["blocks"][t2]
                    tt("dve", HV[:, b_, :], ps[:, :], HX1[:, b_, :], ALU.mult)

            def inv2(sq):
                for cc in range(4):
                    ps = bank()
                    for fc in range(2):
                        y = 2 * sq["slot"] + fc
                        mm(ps[:, 0:256], YRE[:, y, cc * 128:(cc + 1) * 128], ic[:, fc * 256:(fc + 1) * 256], fc == 0, False)
                        mm(ps[:, 0:256], YIM[:, y, cc * 128:(cc + 1) * 128], isn[:, fc * 256:(fc + 1) * 256], False, fc == 1)
                    tt("dve", sq["z2"][:, cc, :], ps[:, 0:256], sq["hx2"][:, cc, :], ALU.mult)

            for sq in seqs:
                fwd(sq, 0)
            for sq in seqs:
                inv1(sq)
            for sq in seqs:
                fwd(sq, 1)
            for sq in seqs:
                inv2(sq)

        def hy_make(src, g, hTb, defer=False, xo=None):
            make_hT(src, hTb, A1, colm[:, 0:8, :], g, xo if xo is not None else [0, 1024], [XN_H, HB + 50176], [XN_H, HB + 50176], defer=defer)

        def hy_proj_st(hTb, seglen, blk0, need_x2, HX2):
            HT_H = hTb
            nseg = 512 // seglen
            nch = 12 if need_x2 else 8

            def stA(ch):
                ps = bank(hold=True)
                for k in range(8):
                    mm(ps[:, :], WHU[:, k, ch * 128:(ch + 1) * 128], HT_H[:, k, :], k == 0, k == 7)
                return ps

            def stB(ch, ps):
                u = UBF[ch % 2]
                cp("act", u, ps[:, :])
                release(ps)
                u3 = u.rearrange("p (s t) -> p s t", s=nseg)
                py = bank()
                py3 = py[:, :].rearrange("p (s t) -> p s t", s=nseg)
                mm(py[:, :], DIAG[:, ch * 3 + 1, :], u, True, False)
                for sg_ in range(nseg):
                    o_ = sg_ * seglen
                    mm(py[:, o_ + 1:o_ + seglen], DIAG[:, ch * 3 + 0, :], u[:, o_:o_ + seglen - 1], False, False)
                    mm(py[:, o_:o_ + seglen - 1], DIAG[:, ch * 3 + 2, :], u[:, o_ + 1:o_ + seglen], False, sg_ == nseg - 1)
                if ch < 8:
                    cv = CVB[ch % 2]
                    act(cv, py[:, :], AF.Identity, bias=cbt[:, ch:ch + 1], scale=1.0)
                    return cv
                act(HX2[:, ch - 8, :], py[:, :], AF.Identity, bias=cbt[:, ch:ch + 1], scale=1.0)
                return None

            def stC(ch, cv):
                pst = bank()
                for b in range(4):
                    mm(pst[:, b * 128:(b + 1) * 128], cv[:, b * 128:(b + 1) * 128], IDENT, True, True, signal=(b == 3))
                dstT = HV if ch < 4 else HX1
                cp("act", dstT[:, blk0:blk0 + 4, (ch % 4) * 128:(ch % 4) * 128 + 128], pst[:, :].rearrange("p (b c) -> p b c", b=4))

            ps_next = stA(0)
            pend = None
            for ch in range(nch):
                ps = ps_next
                if ch + 1 < nch:
                    ps_next = stA(ch + 1)
                cv = stB(ch, ps)
                fill(3)
                if pend is not None:
                    stC(*pend)
                pend = (ch, cv) if cv is not None else None
            if pend is not None:
                stC(*pend)

        fgq = filter_gen(HY["s"], 1, EOD_S, BT_S, defer=True)
        fgq.pop(0)()
        hy_make(xs[512:1024, :], 1, HT_H2, defer=True, xo=[0, 0])
        mk = G.fillq; G.fillq = []
        while mk or fgq:
            if mk:
                G.fillq.append(mk.pop(0))
            if fgq:
                G.fillq.append(fgq.pop(0))
        hy_proj_st(HT_H, 64, 0, True, HX2)
        hy_make(xc[0:512, :], 0, HT_H, defer=True)
        hy_proj_st(HT_H2, 64, 4, False, HX2)
        flush()
        rc = f32v(2048, 8, 128)
        P.dma("sp", rc, rc_d)
        act(lgt, dect, AF.Sigmoid)
        act(lgt, lgt, AF.Ln)
        act(cdt, lgt, AF.Exp, scale=128.0)
        sc_ret = 128.0 ** -0.5
        tA = f32v(3072, 128); tB = f32v(3200, 128)
        for h in range(4):
            lf = lgt[:, h:h + 1]; lb = lgt[:, 4 + h:5 + h]
            act(tA, rc[:, 0, :], AF.Exp, scale=lf)
            tt("dve", tA, tA, rc[:, 1, :], ALU.mult)
            act(tB, rc[:, 2, :], AF.Exp, scale=lb)
            tt("dve", tB, tB, rc[:, 3, :], ALU.mult)
            tt("dve", tA, tA, tB, ALU.add)
            ts("dve", DM[:, h, :], tA, sc_ret, None, ALU.mult)
            act(QDF[:, h, :], rc[:, 4, :], AF.Exp, scale=lf)
            act(QDB[:, h, :], rc[:, 5, :], AF.Exp, scale=lb)
            act(tA, rc[:, 6, :], AF.Exp, scale=lf)
            ts("dve", KDF[:, h, :], tA, sc_ret, None, ALU.mult)
            act(tB, rc[:, 7, :], AF.Exp, scale=lb)
            ts("dve", KDB[:, h, :], tB, sc_ret, None, ALU.mult)

        G.fillq.extend(late_mod_thunks())
        hy_seq(HY["s"], 1, list(range(8)), HX2, Z2T[:, :, 1024:1536], 512)
        hy_make(xc[512:1024, :], 0, HT_H2, defer=True)
        hy_proj_st(HT_H, 256, 0, True, HX2B)
        flush()
        hy_proj_st(HT_H2, 256, 4, True, HX2)
        cseqs = []
        for i in range(2):
            hx = HX2B if i == 0 else HX2
            for s_ in range(2):
                cseqs.append(dict(blocks=[4 * i + 2 * s_, 4 * i + 2 * s_ + 1], slot=2 * i + s_,
                                  hx2=hx[:, :, s_ * 256:(s_ + 1) * 256],
                                  z2=Z2T[:, :, i * 512 + s_ * 256:i * 512 + (s_ + 1) * 256]))
        WRB_pre = [bfv(8704 + cb_ * 4096, 8, 512) for cb_ in range(4)]
        for cb_ in range(4):
            wdma(WRB_pre[cb_], win_d[:, cb_ * 512:(cb_ + 1) * 512].rearrange("(k p) n -> p k n", p=128), window=0)
        hy_ctx_multi(cseqs)
        WRB = [bfv(8704 + cb_ * 4096, 8, 512) for cb_ in range(8)]
        WRO = bfv(41472, 4, 1024); WHO = bfv(45568, 4, 1024); WO = bfv(49664, 8, 1024)
        HT_R = bfv(WK0, 8, 512)
        QT = bfv(WK0 + 4096, 4, 512); KT = bfv(WK0 + 6144, 4, 512)
        KTOK = bfv(WK0 + 8192, 4, 512); VTOK = bfv(WK0 + 10240, 4, 512); SG = bfv(WK0 + 12288, 4, 512)
        MIXT = bfv(WK0 + 4096, 8, 512)
        GOT = bfv(WK0 + 14336, 4, 512)
        ATTM2 = [bfv(WK0 + 16384 + i * 512, 4, 128) for i in range(2)]
        QTF2 = [bfv(WK0 + 17408 + i * 512, 4, 128) for i in range(2)]
        QTB2 = [bfv(WK0 + 18432 + i * 512, 4, 128) for i in range(2)]
        KFt = bfv(WK0 + 19456, 4, 128); KBt = bfv(WK0 + 19968, 4, 128)
        GOTOK2 = [bfv(WK0 + 20480 + i * 512, 4, 128) for i in range(2)]
        SFB = [bfv(WK0 + 21504 + i * 512, 4, 128) for i in range(4)]
        SBB = [bfv(WK0 + 23552 + i * 512, 4, 128) for i in range(4)]
        JUNK_R = WK0 + 4096; XN_R = WK0 + 6144
        XB_OFF = 0
        X1B = f32v(1024, 1024); G1BC = f32v(2048, 1024)
        SFR = f32v(3072, 4, 128); SBR = f32v(3584, 4, 128)
        S0F = f32v(7168, 4, 128); S0B = f32v(7680, 4, 128)

        for cb_ in range(4):
            wdma(WRB[4 + cb_], win_d[:, 3584 + cb_ * 512:3584 + (cb_ + 1) * 512].rearrange("(k p) n -> p k n", p=128))
        wdma(WRO, wro_d.rearrange("(k p) n -> p k n", p=128))
        wdma(WHO, who_d.rearrange("(k p) n -> p k n", p=128))
        for cb_ in range(2):
            wdma(WO[:, :, cb_ * 512:(cb_ + 1) * 512], wo_d[:, cb_ * 512:(cb_ + 1) * 512].rearrange("(k p) n -> p k n", p=128))

        def proj_tm(hT, col0, dst, func=None):
            for blk in range(4):
                ps = bank()
                for k in range(8):
                    mm(ps[:, :], hT[:, k, blk * 128:(blk + 1) * 128], WRB[col0 // 512][:, k, :], k == 0, k == 7)
                if func is None:
                    cp("dve", dst[:, blk, :], ps[:, :])
                else:
                    act(dst[:, blk, :], ps[:, :], func)

        def proj_fm(hT, col0, dst):
            for h in range(4):
                ps = bank()
                for k in range(8):
                    mm(ps[:, :], WRB[col0 // 512][:, k, h * 128:(h + 1) * 128], hT[:, k, :], k == 0, k == 7)
                cp("act", dst[:, h, :], ps[:, :])

        def kv_chunk(c, decay_t, scaled_t):
            tt("dve", scaled_t, KTOK[:, c, :].rearrange("p (h d) -> p h d", h=4), decay_t, ALU.mult)
            ps = bank()
            for h in range(4):
                mm(ps[:, h * 128:(h + 1) * 128], scaled_t[:, h, :], VTOK[:, c, h * 128:(h + 1) * 128], True, True, signal=(h == 3))
            return ps

        def state_step(S, ps, cd0, has_prev):
            ps3 = ps[:, :].rearrange("p (h v) -> p h v", h=4)
            if not has_prev:
                cp("dve", S, ps3)
            else:
                for h in range(4):
                    stt(S[:, h, :], S[:, h, :], cdt[:, cd0 + h:cd0 + h + 1], ps3[:, h, :], ALU.mult, ALU.add)

        def ret_st(seqs):
            items = []
            chains = []
            fS = [SFR, S0F]; bS = [SBR, S0B]
            scl = [(KFt, KBt), (QTF2[0], QTB2[0])]
            for si, (chunks, init_f, init_b, out_idx, slot0) in enumerate(seqs):
                n = len(chunks)
                Sf = fS[si]; Sb = bS[si]; kf_t, kb_t = scl[si]

                def fstep(i, c, Sf=Sf, kf_t=kf_t, slot0=slot0, init_f=init_f):
                    has = init_f or i > 0
                    if has:
                        cp("act", SFB[slot0 + i], Sf)
                    ps = kv_chunk(c, KDF, kf_t)
                    state_step(Sf, ps, 0, has)

                def bstep(i, c, Sb=Sb, kb_t=kb_t, slot0=slot0, init_b=init_b, n=n):
                    has = init_b or i < n - 1
                    if has:
                        cp("act", SBB[slot0 + i], Sb)
                    ps = kv_chunk(c, KDB, kb_t)
                    state_step(Sb, ps, 4, has)
                chains.append([(lambda i=i, c=c, f=fstep: f(i, c)) for i, c in enumerate(chunks)])
                chains.append([(lambda i=i, f=bstep, chunks=chunks: f(i, chunks[i])) for i in range(n - 1, -1, -1)])
                for i, c in enumerate(chunks):
                    items.append((c, slot0 + i, (init_f or i > 0), (init_b or i < n - 1)))
            for k in range(max(len(ch) for ch in chains)):
                for ch in chains:
                    if k < len(ch):
                        ch[k]()
            for si, (chunks, init_f, init_b, out_idx, slot0) in enumerate(seqs):
                if out_idx is not None:
                    P.dma("pool", nsf[out_idx].rearrange("h d v -> d h v"), fS[si])
                    P.dma("pool", nsb[out_idx].rearrange("h d v -> d h v"), bS[si])
            m = len(items)

            def s1(j):
                c, slot, use_f, use_b = items[j]
                csl = slice(c * 128, (c + 1) * 128)
                pa = bank()
                for h in range(4):
                    mm(pa[:, h * 128:(h + 1) * 128], KT[:, h, csl], QT[:, h, csl], True, True, signal=(h == 3))
                tt("dve", ATTM2[j % 2], pa[:, :].rearrange("p (h i) -> p h i", h=4), DM, ALU.mult)
                if use_f:
                    tt("pool", QTF2[j % 2], QT[:, :, csl], QDF, ALU.mult)
                if use_b:
                    tt("pool", QTB2[j % 2], QT[:, :, csl], QDB, ALU.mult)

            def s23(j):
                c, slot, use_f, use_b = items[j]
                ATTM = ATTM2[j % 2]; QTF = QTF2[j % 2]; QTB = QTB2[j % 2]; GOTOK = GOTOK2[j % 2]
                po = bank()
                for h in range(4):
                    o_ = po[:, h * 128:(h + 1) * 128]
                    mm(o_, ATTM[:, h, :], VTOK[:, c, h * 128:(h + 1) * 128], True, not (use_f or use_b))
                    if use_f:
                        mm(o_, QTF[:, h, :], SFB[slot][:, h, :], False, not use_b)
                    if use_b:
                        mm(o_, QTB[:, h, :], SBB[slot][:, h, :], False, True)
                o0 = 24 * (j % 2)
                sums = stat2[:, o0:o0 + 4]; ssq = stat2[:, o0 + 4:o0 + 8]; mean = stat2[:, o0 + 8:o0 + 12]
                m2 = stat2[:, o0 + 12:o0 + 16]; var = stat2[:, o0 + 16:o0 + 20]; rstd = stat2[:, o0 + 20:o0 + 24]
                osb = tmp()
                cp("act", osb, po[:, :])
                o3 = osb.rearrange("p (h v) -> p h v", h=4)
                P.op("dve", lambda e, sums=sums, o3=o3: e.tensor_reduce(out=sums, in_=o3, axis=AX.X, op=ALU.add), reads=[o3], writes=[sums])
                sq = tmp()
                tt("pool", sq, osb, osb, ALU.mult)
                sq3 = sq.rearrange("p (h v) -> p h v", h=4)
                P.op("dve", lambda e, ssq=ssq, sq3=sq3: e.tensor_reduce(out=ssq, in_=sq3, axis=AX.X, op=ALU.add), reads=[sq3], writes=[ssq])
                ts("dve", mean, sums, 1.0 / 128, None, ALU.mult)
                tt("dve", m2, mean, mean, ALU.mult)
                stt(var, ssq, 1.0 / 128, m2, ALU.mult, ALU.subtract)
                act(rstd, var, AF.Ln, bias=EPSC, scale=1.0)
                act(rstd, rstd, AF.Exp, scale=-0.5)
                t = tmp()
                t3 = t.rearrange("p (h v) -> p h v", h=4)
                stt(m2, mean, -1.0, rstd, ALU.mult, ALU.mult)
                for h in range(4):
                    act(t3[:, h, :], o3[:, h, :], AF.Identity, bias=m2[:, h:h + 1], scale=rstd[:, h:h + 1])
                tt("pool", GOTOK, t3, SG[:, c, :].rearrange("p (h v) -> p h v", h=4), ALU.mult)

            def s4(j):
                c = items[j][0]
                csl = slice(c * 128, (c + 1) * 128)
                GOTOK = GOTOK2[j % 2]
                pt = bank()
                for h in range(4):
                    mm(pt[:, h * 128:(h + 1) * 128], GOTOK[:, h, :], IDENT, True, True, signal=(h == 3))
                cp("act", GOT[:, :, csl], pt[:, :].rearrange("p (h i) -> p h i", h=4))

            s1(0)
            for j in range(m):
                if j + 1 < m:
                    s1(j + 1)
                s23(j)
                if j > 0:
                    s4(j - 1)
            s4(m - 1)

        G.x1q = "pool"

        def merge_out(src, g_unused, x1dst, nxt=None):
            for oc in range(8):
                pgr = bank()
                for k in range(8):
                    mm(pgr[:, :], WRB[4 + oc // 4][:, k, (oc % 4) * 128:(oc % 4 + 1) * 128], HT_R[:, k, :], k == 0, k == 7)
                pgh = bank()
                for k in range(8):
                    mm(pgh[:, :], WRB[6 + oc // 4][:, k, (oc % 4) * 128:(oc % 4 + 1) * 128], HT_R[:, k, :], k == 0, k == 7)
                pyr = bank()
                for k in range(4):
                    mm(pyr[:, :], WRO[:, k, oc * 128:(oc + 1) * 128], GOT[:, k, :], k == 0, k == 3)
                pyh = bank()
                for k in range(4):
                    mm(pyh[:, :], WHO[:, k, oc * 128:(oc + 1) * 128], src[:, k, :], k == 0, k == 3)
                sgr = tmp(); sgh = tmp(); m1 = tmp(); m2_ = tmp()
                act(sgr, pgr[:, :], AF.Sigmoid)
                act(sgh, pgh[:, :], AF.Sigmoid)
                tt("dve", m1, pyr[:, :], sgr, ALU.mult)
                tt("dve", m2_, pyh[:, :], sgh, ALU.mult)
                tt("pool", MIXT[:, oc, :], m1, m2_, ALU.add)
            if nxt is not None:
                nxt()
            for blk in range(4):
                xb = f32v(9216 if blk % 2 == 0 else 1024, 1024)
                P.dma("sp", xb, x1dst[1][blk * 128:(blk + 1) * 128, :])
                for hf in range(2):
                    ps = bank(hold=True)
                    for k in range(8):
                        mm(ps[:, :], MIXT[:, k, blk * 128:(blk + 1) * 128], WO[:, k, hf * 512:(hf + 1) * 512], k == 0, k == 7)
                    fill(2)
                    t = tmp()
                    tt("dve", t, ps[:, :], G1BC[:, hf * 512:(hf + 1) * 512], ALU.mult)
                    release(ps)
                    tt("dve", xb[:, hf * 512:(hf + 1) * 512], t, xb[:, hf * 512:(hf + 1) * 512], ALU.add)
                P.dma(G.x1q, x1dst[0][blk * 128:(blk + 1) * 128, :], xb)

        def proj_tm_blk(hT, blk, col0, dst, func=None):
            ps = bank()
            for k in range(8):
                mm(ps[:, :], hT[:, k, blk * 128:(blk + 1) * 128], WRB[col0 // 512][:, k, :], k == 0, k == 7)
            if func is None:
                cp("dve", dst[:, blk, :], ps[:, :])
            else:
                act(dst[:, blk, :], ps[:, :], func)

        XN_G = WK0 + 14336

        def ret_make(src, g, full, defer, with_early=True):
            def early(blk):
                if not full:
                    proj_tm_blk(HT_R, blk, 512, KTOK)
                proj_tm_blk(HT_R, blk, 1024, VTOK)
                if full:
                    proj_tm_blk(HT_R, blk, 1536, SG, AF.Silu)
            make_hT(src, HT_R, A1, colm[:, 0:8, :], g, [XB_OFF, 8192], [XN_G, XN_G + 1024], [XN_G, XN_G + 1024],
                    after_blk=(early if with_early else None), defer=defer)
            return early

        def ret_rest(full, early_done):
            if not early_done:
                for blk in range(4):
                    if not full:
                        proj_tm_blk(HT_R, blk, 512, KTOK)
                    proj_tm_blk(HT_R, blk, 1024, VTOK)
                    if full:
                        proj_tm_blk(HT_R, blk, 1536, SG, AF.Silu)
            if full:
                proj_fm(HT_R, 512, KT)
                for blk in range(4):
                    pt = bank()
                    for h in range(4):
                        mm(pt[:, h * 128:(h + 1) * 128], KT[:, h, blk * 128:(blk + 1) * 128], IDENT, True, True, signal=(h == 3))
                    cp("dve", KTOK[:, blk, :], pt[:, :])
                proj_fm(HT_R, 0, QT)

        HT_F = bfv(67584, 8, 512)

        def ffn_make(st, defer, xoffs=None, junk=6144):
            make_hT(x1s[st * 512:(st + 1) * 512, :], HT_F, A2, colm[:, 24:32, :], 0 if st < 2 else 1,
                    xoffs if xoffs is not None else [XB_OFF, 9216], [junk, junk], None, defer=defer, f32mode=True)

        WF1 = bfv(0, 8, NIN)

        def load_wf1():
            for cb_ in range(6):
                wdt = 512 if cb_ < 5 else 256
                for c0 in (cb_ * 512, DFF + cb_ * 512):
                    wdma(WF1[:, :, c0:c0 + wdt], wf1_d[:, c0:c0 + wdt].rearrange("(k p) n -> p k n", p=128))

        P.dma("sp", G1BC, gsc[1, :].partition_broadcast(128))
        P.dma("sp", S0F, s0f_d.rearrange("h d v -> d h v"))
        P.dma("sp", S0B, s0b_d.rearrange("h d v -> d h v"))
        ret_make(xs[512:1024, :], 1, False, False)
        ret_rest(False, True)
        ret_make(xs[0:512, :], 1, True, True, with_early=False)
        cp("dve", SFR, S0F)
        for c in range(4):
            ps = kv_chunk(c, KDF, KFt)
            state_step(SFR, ps, 0, True)
            fill(2)
        cp("dve", SBR, S0B)
        for c in range(3, -1, -1):
            ps = kv_chunk(c, KDB, KBt)
            state_step(SBR, ps, 4, True)
            fill(2)
        flush()
        tS = tmp().rearrange("p (h v) -> p h v", h=4)
        ts("dve", tS, SFR, selt[:, 1:2], None, ALU.mult)
        stt(SFR, S0F, selt[:, 0:1], tS, ALU.mult, ALU.add)
        tS2 = tmp().rearrange("p (h v) -> p h v", h=4)
        ts("dve", tS2, SBR, selt[:, 0:1], None, ALU.mult)
        stt(SBR, S0B, selt[:, 1:2], tS2, ALU.mult, ALU.add)
        ret_rest(True, False)
        ret_st([([0, 1, 2, 3], True, True, None, 0)])

        def nxt_c0():
            P.dma("sp", G1BC, gsc[0, :].partition_broadcast(128))
            ret_make(xc[0:512, :], 0, True, True)
        merge_out(Z2T[:, :, 1024:1536], 1, (x1s[1024:1536, :], xs[0:512, :]), nxt=lambda: ret_make(xc[0:512, :], 0, True, True))
        P.dma("sp", G1BC, gsc[0, :].partition_broadcast(128))
        flush()
        ret_rest(True, True)
        ret_st([([0, 1], False, False, 0, 0), ([2, 3], False, False, 1, 2)])
        merge_out(Z2T[:, :, 0:512], 0, (x1s[0:512, :], xc[0:512, :]), nxt=lambda: ret_make(xc[512:1024, :], 0, True, True))
        flush()
        ret_rest(True, True)
        ret_st([([0, 1], False, False, 2, 0), ([2, 3], False, False, 3, 2)])
        G.x1q = "sp"
        merge_out(Z2T[:, :, 512:1024], 0, (x1s[512:1024, :], xc[512:1024, :]), nxt=lambda: (load_wf1(), ffn_make(0, True, xoffs=[XB_OFF, 8192], junk=7168)))

        WF2 = bfv(45056, 22, 1024)
        UT = bfv(71680, 22, 512)
        JUNK_F = 71680; XN_F = 72704
        G2 = [f32v(2048, 1024), f32v(3072, 1024)]
        FG = f32v(7168, 1024); OUTT = f32v(8192, 1024)
        for cb_ in range(11):
            wdma(WF2[:, cb_ * 2:(cb_ + 1) * 2, :], wf2_d[cb_ * 256:(cb_ + 1) * 256, :].rearrange("(k p) n -> p k n", p=128))
        P.dma("sp", G2[0], gsc[2, :].partition_broadcast(128))
        P.dma("sp", G2[1], gsc[3, :].partition_broadcast(128))
        P.dma("sp", FG, fg_d[0, :].partition_broadcast(128))
        G.ntmp = 4
        for st in range(3):
            g = 0 if st < 2 else 1
            src = x1s[st * 512:(st + 1) * 512, :]
            flush()
            for ch in range(22):
                pa = bank()
                for k in range(8):
                    mm(pa[:, :], WF1[:, k, ch * 128:(ch + 1) * 128], HT_F[:, k, :], k == 0, k == 7)
                pb = bank()
                for k in range(8):
                    mm(pb[:, :], WF1[:, k, DFF + ch * 128:DFF + (ch + 1) * 128], HT_F[:, k, :], k == 0, k == 7)
                sa = tmp()
                act(sa, pa[:, :], AF.Silu)
                tt("dve", UT[:, ch, :], pb[:, :], sa, ALU.mult)
            if st + 1 < 3:
                ffn_make(st + 1, True)
            for blk in range(4):
                P.dma("sp", X1B, src[blk * 128:(blk + 1) * 128, :])
                for hf in range(2):
                    ps = bank(hold=True)
                    for k in range(22):
                        mm(ps[:, :], UT[:, k, blk * 128:(blk + 1) * 128], WF2[:, k, hf * 512:(hf + 1) * 512], k == 0, k == 21)
                    fill(2)
                    t = tmp()
                    tt("dve", t, ps[:, :], G2[g][:, hf * 512:(hf + 1) * 512], ALU.mult)
                    release(ps)
                    tt("pool", X1B[:, hf * 512:(hf + 1) * 512], t, X1B[:, hf * 512:(hf + 1) * 512], ALU.add)
                ss = stat[:, 2:3]; rs = stat[:, 3:4]
                memset("dve", ss, 0.0)
                stt(OUTT, X1B, 1.0, X1B, ALU.mult, ALU.mult, accum=ss)
                act(rs, ss, AF.Ln, bias=EPSC, scale=1.0 / D)
                act(rs, rs, AF.Exp, scale=-0.5)
                stt(OUTT, X1B, rs, FG, ALU.mult, ALU.mult)
                if st < 2:
                    P.dma("pool", yc[st * 512 + blk * 128:st * 512 + (blk + 1) * 128, :], OUTT)
                else:
                    P.dma("pool", ys[blk * 128:(blk + 1) * 128, :], OUTT)
        P.finish()
        P.build()
    G.P = P
    return nc


import math
import ml_dtypes

_BF = ml_dtypes.bfloat16
_CACHE = {}


def _hy_consts(L, pos):
    f32 = np.float32
    n = pos.astype(np.float64)
    t = np.linspace(0.0, 1.0, L, dtype=f32)[pos][:, None]
    ang = (f32(2.0 * math.pi) * np.arange(L, dtype=f32)[:, None] / f32(L))[pos]
    bands = np.linspace(1e-4, 16 - 1, 16, dtype=f32)[None]
    z = np.concatenate([t, np.cos(bands * ang), -np.sin(bands * ang)], axis=-1).astype(f32)
    max_decay = math.log(1e-2) / 0.3
    min_decay = math.log(1e-2) / 1.5
    deltas = np.linspace(min_decay, max_decay, 512, dtype=f32)
    win = np.exp(-t * np.abs(deltas)[None, :]).astype(f32)
    winb = win.copy()
    winb[pos == 0, :] = 0.0
    nT = L // 128
    fidx = np.arange(L, dtype=np.float64) + 0.5
    ang2 = np.pi * np.outer(n, fidx) / L
    Cm = np.cos(ang2); Sm = np.sin(ang2)

    def fwd_tab(M):
        A = M.reshape(nT, 128, nT, 128)
        return np.ascontiguousarray(A.transpose(2, 1, 0, 3).reshape(nT, 128, nT * 128)).astype(_BF)

    def inv_tab(M):
        A = M.reshape(L // 256, 256, nT, 128)
        return np.ascontiguousarray(A.transpose(0, 3, 2, 1).reshape(L // 256, 128, nT * 256)).astype(_BF)
    return dict(zT=np.ascontiguousarray(z.T), win=win, winb=winb,
                tfc=fwd_tab(Cm), tfs=fwd_tab(Sm), tic=inv_tab(Cm), tis=inv_tab(Sm))


def _ret_consts():
    j = np.arange(128)[:, None].astype(np.float32)
    i = np.arange(128)[None, :].astype(np.float32)
    rc = np.zeros((128, 8, 128), np.float32)
    rc[:, 0] = np.where(i >= j, i - j, 0.0)
    rc[:, 1] = (i >= j)
    rc[:, 2] = np.where(j >= i, j - i, 0.0)
    rc[:, 3] = (j >= i)
    rc[:, 4] = np.broadcast_to(i + 1.0, (128, 128))
    rc[:, 5] = np.broadcast_to(128.0 - i, (128, 128))
    rc[:, 6] = np.broadcast_to(127.0 - j, (128, 128))
    rc[:, 7] = np.broadcast_to(j, (128, 128))
    return rc


def _col(v):
    return np.ascontiguousarray(np.asarray(v, np.float32).reshape(-1, 128).T)


def kernel(x_prompt, x_sample, state_ret_fwd, state_ret_bwd, c, c_ctx,
           norm1_g, norm2_g, w_ada, b_ada, w_in, ret_decay_fwd, ret_decay_bwd,
           hy_conv_w, hy_conv_b, hy_pos_w1, hy_pos_b1, hy_pos_w2, hy_pos_b2, hy_pos_w3,
           hy_sin_freq, hy_bias, w_ret_o, w_hy_o, w_out, w_ffn_in, w_ffn_out, final_g):
    f = lambda a: np.ascontiguousarray(np.asarray(a, np.float32))
    if "nc" not in _CACHE:
        _CACHE["nc"] = build_nc()
        _CACHE["hc"] = _hy_consts(256, np.arange(256))
        _CACHE["hs"] = [_hy_consts(1024, np.concatenate([np.arange(512) + hh * 512, np.arange(512) + (1 - hh) * 512])) for hh in range(2)]
        _CACHE["rc"] = _ret_consts()
    nc = _CACHE["nc"]
    x_prompt = f(x_prompt); x_sample = f(x_sample)
    common = {
        "n1g": _col(norm1_g[0]), "n2g": _col(norm2_g[0]), "fg": f(final_g).reshape(1, D),
        "w_ada": f(w_ada[0]), "b_col": _col(b_ada[0]), "b_row": f(b_ada[0]).reshape(1, 6144),
        "w_in": f(w_in[0]),
        "dec": np.concatenate([f(ret_decay_fwd[0]), f(ret_decay_bwd[0])]).reshape(1, 8),
        "cw": np.ascontiguousarray(f(hy_conv_w[0]).reshape(3, 12, 128).transpose(2, 1, 0)),
        "cb": np.ascontiguousarray(f(hy_conv_b[0]).reshape(12, 128).T),
        "pw1": f(hy_pos_w1[0]), "pb1": f(hy_pos_b1[0]).reshape(64, 1), "pw2": f(hy_pos_w2[0]),
        "pb2": f(hy_pos_b2[0]).reshape(64, 1), "pw3": f(hy_pos_w3[0]), "pfr": f(hy_sin_freq[0]).reshape(64, 1),
        "hyb": f(hy_bias[0]), "w_ro": f(w_ret_o[0]), "w_ho": f(w_hy_o[0]), "w_o": f(w_out[0]),
        "w_f1": f(w_ffn_in[0]), "w_f2": f(w_ffn_out[0]),
        "ident": np.eye(128, dtype=np.float32).astype(_BF), "rc": _CACHE["rc"], "identf": np.eye(128, dtype=np.float32),
    }
    for k, v in _CACHE["hc"].items():
        common[k + "_c"] = v
    in_maps = []
    for i in range(8):
        b = i // 2; hh = i % 2
        m = dict(common)
        m["xc"] = x_prompt[4 * i:4 * i + 4].reshape(1024, D)
        own = x_sample[b, hh * 512:(hh + 1) * 512]; oth = x_sample[b, (1 - hh) * 512:(2 - hh) * 512]
        m["xs"] = np.ascontiguousarray(np.concatenate([own, oth], axis=0))
        m["s0f"] = f(state_ret_fwd[b, 0]); m["s0b"] = f(state_ret_bwd[b, 0])
        m["cc"] = np.ascontiguousarray(np.concatenate([_col(c_ctx), _col(c[b])], axis=1))
        a = 1.0 if hh == 0 else 0.0
        m["sel"] = np.ascontiguousarray(np.broadcast_to(np.array([a, 1.0 - a], np.float32), (128, 2)))
        for k, v in _CACHE["hs"][hh].items():
            m[k + "_s"] = v
        in_maps.append(m)
    res = run_bass_kernel_spmd(nc, in_maps, core_ids=list(range(8)))
    R = res.results
    y_prompt = np.concatenate([R[i]["yc"].reshape(4, 256, D) for i in range(8)], axis=0)
    y_sample = np.stack([np.concatenate([R[2 * b]["ys"], R[2 * b + 1]["ys"]], axis=0) for b in range(4)], axis=0)
    nf = np.concatenate([R[i]["nsf"] for i in range(8)], axis=0).reshape(32, 1, 4, 128, 128)
    nb = np.concatenate([R[i]["nsb"] for i in range(8)], axis=0).reshape(32, 1, 4, 128, 128)
    return (y_prompt.astype(np.float32), y_sample.astype(np.float32), nf.astype(np.float32), nb.astype(np.float32))
```

```python
import numpy as np
import concourse.bass as bass
import concourse.mybir as mybir
from concourse.bass_utils import run_bass_kernel_spmd

F32 = mybir.dt.float32
BF16 = mybir.dt.bfloat16
AF = mybir.ActivationFunctionType
ALU = mybir.AluOpType
AX = mybir.AxisListType

N_DMA_SEMS = 12


def _rects(ap):
    t = ap.tensor
    dims = ap.ap
    off = int(ap.offset)
    sp = str(ap.space)
    if sp in ("SB", "PSUM"):
        shp = t.shape
        fs = 1
        for s in shp[1:]:
            fs *= s
        p0 = off // fs
        f0 = off % fs
        npart = dims[0][1]
        if sp == "PSUM":
            return [(t.name, 0, 128, 0, fs)]
        if len(dims) == 3 and dims[2][0] == 1 and 1 < dims[1][1] <= 32 and dims[1][0] > dims[2][1]:
            return [(t.name, p0, p0 + npart, f0 + r * dims[1][0], f0 + r * dims[1][0] + dims[2][1]) for r in range(dims[1][1])]
        ext = 0
        for st, cnt in dims[1:]:
            ext += abs(st) * (cnt - 1)
        return [(t.name, p0, p0 + npart, f0, f0 + ext + 1)]
    ext = 0
    for st, cnt in dims:
        ext += abs(st) * (cnt - 1)
    return [(t.name, 0, 1, off, off + ext + 1)]


class Prog:
    def __init__(self, nc):
        self.nc = nc
        self.lists = {"pe": [], "act": [], "dve": [], "pool": [], "sp": []}
        self.cnt = {"pe": 0, "act": 0, "dve": 0, "pool": 0}
        self.sems = {}
        self.seen = {e: {} for e in self.lists}
        self.regions = {}
        self.dma_cnt = {"sp": 0, "act": 0, "pool": 0}
        self.dma_sems = {}
        self.all_dma = []
        self.n_wait = 0
        self.n_ins = 0

    def _overlaps(self, r):
        recs = self.regions.get(r[0])
        if not recs:
            return []
        out = []
        for k, v in recs.items():
            if k[1] < r[2] and r[1] < k[2] and k[3] < r[4] and r[3] < k[4]:
                out.append((k, v))
        return out

    def _deps(self, reads, writes):
        deps = {}

        def add(d):
            if d is None:
                return
            k, v = d
            if deps.get(k, -1) < v:
                deps[k] = v
        for ap in reads:
            for r in _rects(ap):
                for k, v in self._overlaps(r):
                    add(v[0])
        for ap in writes:
            for r in _rects(ap):
                for k, v in self._overlaps(r):
                    add(v[0])
                    for d in v[1].items():
                        add(d)
        return deps

    def _commit(self, reads, writes, tok):
        for ap in writes:
            for r in _rects(ap):
                recs = self.regions.setdefault(r[0], {})
                for k in list(recs.keys()):
                    if k[1] >= r[1] and k[2] <= r[2] and k[3] >= r[3] and k[4] <= r[4]:
                        del recs[k]
                recs[r] = [tok, {}]
        for ap in reads:
            for r in _rects(ap):
                recs = self.regions.setdefault(r[0], {})
                if r not in recs:
                    recs[r] = [None, {}]
                rd = recs[r][1]
                if rd.get(tok[0], -1) < tok[1]:
                    rd[tok[0]] = tok[1]

    def _emit_waits(self, eng, deps, skip_self=False):
        for k, v in deps.items():
            if skip_self and k == eng:
                continue
            if self.seen[eng].get(k, 0) >= v:
                continue
            self.seen[eng][k] = v
            self.lists[eng].append(("wait", k, v))
            self.n_wait += 1

    def op(self, eng, fn, reads=(), writes=(), signal=True):
        writes = list(writes) + [ap for ap in reads if str(ap.space) == "PSUM"]
        deps = self._deps(reads, writes)
        self._emit_waits(eng, deps, skip_self=(eng == "pe"))
        if signal:
            self.cnt[eng] += 1
            tick = self.cnt[eng]
            self.lists[eng].append(("op", fn, eng, 1))
        else:
            tick = self.cnt[eng] + 1
            self.lists[eng].append(("op", fn, None, 0))
        self._commit(reads, writes, (eng, tick))
        self.n_ins += 1

    def dma(self, q, out, in_, after=(), **kw):
        deps = self._deps([in_], [out])
        for k_, v_ in after:
            deps[k_] = max(deps.get(k_, 0), v_)
        i = self.dma_cnt[q]
        self.dma_cnt[q] += 1
        slot = i % N_DMA_SEMS
        semkey = ("dma", q, slot)
        prev = 16 * (i // N_DMA_SEMS)
        if prev > 0:
            deps[semkey] = max(deps.get(semkey, 0), prev)
        self._emit_waits(q, deps)
        val = prev + 16
        self.lists[q].append(("dma", out, in_, kw, semkey))
        self._commit([in_], [out], (semkey, val))
        self.all_dma.append((semkey, val))
        self.n_ins += 1
        return (semkey, val)

    def finish(self, eng="sp"):
        last = {}
        for k, v in self.all_dma:
            last[k] = max(last.get(k, 0), v)
        self._emit_waits(eng, last)
        fin = {e: c for e, c in self.cnt.items() if c > 0}
        self._emit_waits(eng, fin)

    def build(self):
        nc = self.nc
        from contextlib import ExitStack
        with ExitStack() as es:
            for e in ("pe", "act", "dve", "pool"):
                self.sems[e] = es.enter_context(nc.semaphore("s_" + e))
            for q in ("sp", "act", "pool"):
                for s in range(N_DMA_SEMS):
                    self.sems[("dma", q, s)] = es.enter_context(nc.semaphore("d_%s_%d" % (q, s)))
            block = es.enter_context(nc.Block())
            sems = self.sems

            def replay(lst):
                def run(engine):
                    for it in lst:
                        if it[0] == "wait":
                            engine.wait_ge(sems[it[1]], it[2])
                        elif it[0] == "op":
                            ins = it[1](engine)
                            if it[2] is not None:
                                ins.then_inc(sems[it[2]], 1)
                        else:
                            _, out, in_, kw, semkey = it
                            engine.dma_start(out=out, in_=in_, **kw).then_inc(sems[semkey], 16)
                return run
            block.tensor(replay(self.lists["pe"]))
            block.scalar(replay(self.lists["act"]))
            block.vector(replay(self.lists["dve"]))
            block.gpsimd(replay(self.lists["pool"]))
            block.sync(replay(self.lists["sp"]))


D = 1024
NIN = 5632
DFF = 2816
EPS = 1e-6
MAGIC = 12582912.0
TWO_PI = 6.283185307179586

NBF = 83968
NF32 = 10240


def _prod(s):
    r = 1
    for v in s:
        r *= v
    return r


class Ctx:
    pass


def build_nc():
    nc = bass.Bass("TRN2", target_bir_lowering=False)
    P = Prog(nc)
    G = Ctx()

    def din(name, shape, dt=F32):
        return nc.dram_tensor(name, list(shape), dt, kind="ExternalInput").ap()

    def dout(name, shape, dt=F32):
        return nc.dram_tensor(name, list(shape), dt, kind="ExternalOutput").ap()

    xc = din("xc", [1024, D]); xs = din("xs", [1024, D])
    s0f_d = din("s0f", [4, 128, 128]); s0b_d = din("s0b", [4, 128, 128])
    cc_d = din("cc", [128, 16]); sel_d = din("sel", [128, 2])
    n1g_d = din("n1g", [128, 8]); n2g_d = din("n2g", [128, 8]); fg_d = din("fg", [1, D])
    wada_d = din("w_ada", [D, 6144]); bcol_d = din("b_col", [128, 48]); brow_d = din("b_row", [1, 6144])
    win_d = din("w_in", [D, NIN]); dec_d = din("dec", [1, 8])
    cw_d = din("cw", [128, 12, 3]); cb_d = din("cb", [128, 12])
    pw1_d = din("pw1", [33, 64]); pb1_d = din("pb1", [64, 1]); pw2_d = din("pw2", [64, 64]); pb2_d = din("pb2", [64, 1])
    pw3_d = din("pw3", [64, 2048]); pfr_d = din("pfr", [64, 1]); hyb_d = din("hyb", [2, 512])
    wro_d = din("w_ro", [512, D]); who_d = din("w_ho", [512, D]); wo_d = din("w_o", [D, D])
    wf1_d = din("w_f1", [D, NIN]); wf2_d = din("w_f2", [DFF, D])
    ident_d = din("ident", [128, 128], BF16); rc_d = din("rc", [128, 8, 128]); identf_d = din("identf", [128, 128])
    HY = {}
    for nm, L in (("c", 256), ("s", 1024)):
        nT = L // 128
        HY[nm] = dict(L=L, nT=nT, nF=nT, nTB=L // 256,
                      zT=din("zT_" + nm, [33, L]), win=din("win_" + nm, [L, 512]), winb=din("winb_" + nm, [L, 512]),
                      tfc=din("tfc_" + nm, [nT, 128, nT * 128], BF16), tfs=din("tfs_" + nm, [nT, 128, nT * 128], BF16),
                      tic=din("tic_" + nm, [L // 256, 128, nT * 256], BF16), tis=din("tis_" + nm, [L // 256, 128, nT * 256], BF16))
    yc = dout("yc", [1024, D]); ys = dout("ys", [512, D])
    nsf = dout("nsf", [4, 4, 128, 128]); nsb = dout("nsb", [4, 4, 128, 128])
    x1s = nc.dram_tensor("x1s", [1536, D], F32, kind="Internal").ap()
    kfs = nc.dram_tensor("kfs", [2, 2, 2, 8, 128, 512], BF16, kind="Internal").ap()
    gsc = nc.dram_tensor("gsc", [4, D], F32, kind="Internal").ap()

    from contextlib import ExitStack
    with ExitStack() as es:
        ABF = es.enter_context(nc.sbuf_tensor("abf", [128, NBF], BF16))
        AF_ = es.enter_context(nc.sbuf_tensor("af32", [128, NF32], F32))
        SM = es.enter_context(nc.sbuf_tensor("sm", [128, 512], F32))
        IDENTF = es.enter_context(nc.sbuf_tensor("identf_sb", [128, 128], F32))
        PS = [es.enter_context(nc.psum_tensor("ps%d" % i, [128, 512], F32)) for i in range(8)]

        def bfv(off, *shape):
            n = _prod(shape)
            assert off + n <= NBF, (off, n)
            ap = ABF[:, off:off + n]
            if len(shape) == 2:
                ap = ap.rearrange("p (a b) -> p a b", a=shape[0])
            return ap

        def f32v(off, *shape):
            n = _prod(shape)
            assert off + n <= NF32, (off, n)
            ap = AF_[:, off:off + n]
            if len(shape) == 2:
                ap = ap.rearrange("p (a b) -> p a b", a=shape[0])
            return ap

        G.banks = list(range(8)); G.bi = 0

        G.live = set()

        def bank(hold=False):
            for _ in range(16):
                b = G.banks[G.bi % len(G.banks)]
                G.bi += 1
                if b not in G.live:
                    break
            if hold:
                G.live.add(b)
            return PS[b]

        def release(ps):
            G.live.discard(PS.index(ps))

        def mm(out, lhsT, rhs, start, stop, signal=None):
            P.op("pe", lambda e: e.matmul(out, lhsT=lhsT, rhs=rhs, start=start, stop=stop),
                 reads=[lhsT, rhs], writes=[out], signal=(stop if signal is None else signal))

        def act(out, in_, func, bias=None, scale=None, accum=None):
            kw = {}
            rd = [in_]
            wr = [out]
            if bias is not None:
                kw["bias"] = bias
                if not isinstance(bias, float):
                    rd.append(bias)
            if scale is not None:
                kw["scale"] = scale
                if not isinstance(scale, float):
                    rd.append(scale)
            if accum is not None:
                kw["accum_out"] = accum
                wr.append(accum)
            P.op("act", lambda e: e.activation(out=out, in_=in_, func=func, **kw), reads=rd, writes=wr)

        def tt(eng, out, in0, in1, op):
            P.op(eng, lambda e: e.tensor_tensor(out=out, in0=in0, in1=in1, op=op), reads=[in0, in1], writes=[out])

        def ts(eng, out, in0, s1, s2, op0, op1=None):
            rd = [in0] + [s for s in (s1, s2) if s is not None and not isinstance(s, float)]
            if op1 is None:
                P.op(eng, lambda e: e.tensor_scalar(out=out, in0=in0, scalar1=s1, scalar2=None, op0=op0), reads=rd, writes=[out])
            else:
                P.op(eng, lambda e: e.tensor_scalar(out=out, in0=in0, scalar1=s1, scalar2=s2, op0=op0, op1=op1), reads=rd, writes=[out])

        def stt(out, in0, scalar, in1, op0, op1, accum=None):
            rd = [in0, in1] + ([] if isinstance(scalar, float) else [scalar])
            if accum is None:
                P.op("dve", lambda e: e.scalar_tensor_tensor(out=out, in0=in0, scalar=scalar, in1=in1, op0=op0, op1=op1), reads=rd, writes=[out])
            else:
                P.op("dve", lambda e: e.scalar_tensor_tensor(out=out, in0=in0, scalar=scalar, in1=in1, op0=op0, op1=op1, accum_out=accum), reads=rd, writes=[out, accum])

        def cp(eng, out, in_):
            if eng == "act":
                P.op("act", lambda e: e.copy(out=out, in_=in_), reads=[in_], writes=[out])
            else:
                P.op(eng, lambda e: e.tensor_copy(out=out, in_=in_), reads=[in_], writes=[out])

        def recip(out, in_):
            P.op("dve", lambda e: e.reciprocal(out=out, in_=in_), reads=[in_], writes=[out])

        def memset(eng, out, val):
            P.op(eng, lambda e: e.memset(out, val), reads=[], writes=[out])


        G.wtok = []

        def wdma(out, in_, window=2):
            after = [G.wtok[-window]] if (window and len(G.wtok) >= window) else []
            G.wtok.append(P.dma("pool", out, in_, after=after))

        colm = SM[:, 0:96].rearrange("p (a b) -> p a b", a=48)
        A1 = SM[:, 96:112].rearrange("p (a b) -> p a b", a=8)
        A2 = SM[:, 112:128].rearrange("p (a b) -> p a b", a=8)
        cond = SM[:, 128:144]
        selt = SM[:, 144:146]
        dect = SM[:, 146:154]
        lgt = SM[:, 154:162]
        cdt = SM[:, 162:170]
        n1g = SM[:, 170:178]; n2g = SM[:, 178:186]
        bcol = SM[:, 186:234]
        cbt = SM[:, 234:246]
        cwt = SM[:, 246:282].rearrange("p (a b) -> p a b", a=12)
        stat = SM[:, 282:330]
        pb1 = SM[0:64, 330:331]; pb2 = SM[0:64, 331:332]; pfr = SM[0:64, 332:333]
        ones_f = SM[:, 384:512]
        EPSC = SM[:, 333:334]
        stat2 = SM[:, 334:382]

        IDENT = ABF[:, NBF - 128:NBF]
        DM = bfv(0, 4, 128); QDF = bfv(512, 4, 128); QDB = bfv(1024, 4, 128); KDF = bfv(1536, 4, 128); KDB = bfv(2048, 4, 128)
        Z2T = bfv(2560, 4, 1536)
        WK0 = 57856

        P.dma("sp", IDENT, ident_d)
        P.dma("sp", IDENTF[:, :], identf_d)
        P.dma("sp", cond, cc_d)
        P.dma("sp", selt, sel_d)
        P.dma("sp", dect, dec_d[0, :].partition_broadcast(128))
        P.dma("sp", n1g, n1g_d); P.dma("sp", n2g, n2g_d)
        P.dma("sp", bcol, bcol_d)
        P.dma("sp", cbt, cb_d)
        P.dma("sp", SM[:, 246:282], cw_d.rearrange("p a b -> p (a b)"))
        P.dma("sp", pb1, pb1_d); P.dma("sp", pb2, pb2_d); P.dma("sp", pfr, pfr_d)
        memset("dve", ones_f, 1.0)
        memset("dve", EPSC, EPS)

        scf = f32v(1280, 16)
        act(scf, cond, AF.Silu)
        SCB = bfv(2560, 8, 2)
        cp("dve", SCB, scf.rearrange("p (g k) -> p k g", g=2))
        SCBB = bfv(2576, 2, 8 * 128)
        onesb = f32v(1296, 128)
        memset("dve", onesb, 1.0)
        for g in range(2):
            for k in range(8):
                ts("dve", SCBB[:, g, k * 128:(k + 1) * 128], onesb, scf[:, g * 8 + k:g * 8 + k + 1], None, ALU.mult)
        P.dma("pool", ABF[0:64, 8704 + 68608:8704 + 68608 + 2048], pw3_d)
        WA = [bfv(8704 + 12288 + i * 4096, 8, 512) for i in range(5)] + [bfv(8704 + 38912 + i * 4096, 8, 512) for i in range(3)]
        for blk in range(6):
            wdma(WA[blk], wada_d[:, blk * 512:(blk + 1) * 512].rearrange("(k p) n -> p k n", p=128), window=0)
        G.fillq = []

        def fill(n):
            for _ in range(n):
                if G.fillq:
                    G.fillq.pop(0)()

        def flush():
            while G.fillq:
                G.fillq.pop(0)()

        def make_hT(src, hT, Acol, shcol, g, xoffs, junkoffs, xnoffs, after_blk=None, defer=False, f32mode=False):
            def st1(blk):
                xb = f32v(xoffs[blk % 2], 1024)
                P.dma("sp", xb, src[blk * 128:(blk + 1) * 128, :])
                junk = f32v(junkoffs[blk % 2], 1024) if f32mode else bfv(junkoffs[blk % 2], 1024)
                ss = stat[:, 32 + 2 * blk:33 + 2 * blk]; rs = stat[:, 33 + 2 * blk:34 + 2 * blk]
                memset("dve", ss, 0.0)
                stt(junk, xb, 1.0, xb, ALU.mult, ALU.mult, accum=ss)
                act(rs, ss, AF.Ln, bias=EPSC, scale=1.0 / D)
                act(rs, rs, AF.Exp, scale=-0.5)

            def st2(blk):
                xb = f32v(xoffs[blk % 2], 1024)
                rs = stat[:, 33 + 2 * blk:34 + 2 * blk]
                xn = xb if f32mode else bfv(xnoffs[blk % 2], 1024)
                ts("dve", xn, xb, rs, None, ALU.mult)

            def st3(blk, hf):
                xn = f32v(xoffs[blk % 2], 1024) if f32mode else bfv(xnoffs[blk % 2], 1024)
                idn = IDENTF[:, :] if f32mode else IDENT
                pst = bank()
                for k4 in range(4):
                    k = hf * 4 + k4
                    mm(pst[:, k4 * 128:(k4 + 1) * 128], xn[:, k * 128:(k + 1) * 128], idn, True, True, signal=(k4 == 3))
                for k4 in range(4):
                    k = hf * 4 + k4
                    dst = hT[:, k, blk * 128:(blk + 1) * 128]
                    if k4 % 2 == 0:
                        ts("dve", dst, pst[:, k4 * 128:(k4 + 1) * 128], Acol[:, k, g:g + 1], shcol[:, k, g:g + 1], ALU.mult, ALU.add)
                    else:
                        act(dst, pst[:, k4 * 128:(k4 + 1) * 128], AF.Identity, bias=shcol[:, k, g:g + 1], scale=Acol[:, k, g:g + 1])
                if hf == 1 and after_blk is not None:
                    after_blk(blk)

            L = lambda f, *a: (lambda: f(*a))
            if xoffs[0] == xoffs[1]:
                stages = []
                for b_ in range(4):
                    stages += [L(st1, b_), L(st2, b_), L(st3, b_, 0), L(st3, b_, 1)]
            else:
                stages = [L(st1, 0), L(st1, 1), L(st2, 0), L(st3, 0, 0), L(st3, 0, 1), L(st1, 2), L(st2, 1), L(st3, 1, 0), L(st3, 1, 1),
                          L(st1, 3), L(st2, 2), L(st3, 2, 0), L(st3, 2, 1), L(st2, 3), L(st3, 3, 0), L(st3, 3, 1)]
            if defer:
                G.fillq.extend(stages)
            else:
                for t in stages:
                    t()

        TMP = [f32v(4096 + i * 512, 512) for i in range(6)]
        G.ti = 0
        G.ntmp = 6

        def tmp():
            t = TMP[G.ti % G.ntmp]
            G.ti += 1
            return t

        HB = 8704
        WHU = bfv(HB, 8, 1536)
        HT_H = bfv(HB + 12288, 8, 512)
        HV = bfv(HB + 16384, 8, 512)
        HX1 = bfv(HB + 20480, 8, 512)
        HX2 = bfv(HB + 24576, 4, 512)
        YRE = bfv(HB + 26624, 8, 512)
        YIM = bfv(HB + 30720, 8, 512)
        TFB = [[bfv(HB + 34816 + (b * 2 + s) * 1024, 1024) for s in range(2)] for b in range(2)]
        TIB = [[bfv(HB + 38912 + (b * 2 + s) * 2048, 2048) for s in range(2)] for b in range(2)]
        KFB = [[bfv(HB + 47104 + (b * 2 + s) * 512, 512) for s in range(2)] for b in range(2)]
        CVB = [bfv(HB + 49152 + i * 512, 512) for i in range(2)]
        JUNK_H = HB + 50176
        XN_H = HB + 51200
        EOD = [[bfv(HB + 52224 + (o * 2 + s) * 4096, 8, 512) for s in range(2)] for o in range(2)]
        PW3B = bfv(HB + 68608, 2048)
        H2T = bfv(HB + 70656, 512)
        KOUT = [bfv(HB + 71168 + i * 512, 512) for i in range(4)]

        for cb_ in range(3):
            wdma(WHU[:, :, cb_ * 512:(cb_ + 1) * 512], win_d[:, 2048 + cb_ * 512:2048 + (cb_ + 1) * 512].rearrange("(k p) n -> p k n", p=128), window=0)
        DIAG = bfv(HB + 60416, 36, 128)
        UBF = [bfv(HB + 65024 + i * 512, 512) for i in range(2)]
        HT_H2 = bfv(HB + 52224, 8, 512)
        HX2B = bfv(HB + 56320, 4, 512)
        CMB = [bfv(HB + 58368 + i * 512, 512) for i in range(4)]
        UREB = [bfv(HB + 66048 + i * 512, 512) for i in range(4)]

        RN = [f32v(1024 + o * 512, 512) for o in range(2)]
        BIASL = [f32v(2048 + o * 512, 512) for o in range(2)]
        PW1 = f32v(3072, 64)[0:33, :]
        PW2 = f32v(3136, 64)[0:64, :]
        WINB = [[f32v(7168 + (b * 2 + s) * 512, 512) for s in range(2)] for b in range(2)]
        H1 = f32v(9216, 512)[0:64, :]
        ZT = f32v(9728, 512)[0:33, :]

        P.dma("sp", PW1, pw1_d)
        P.dma("sp", PW2, pw2_d)

        def sin_layer(dst, ps, bcolv, W):
            a = tmp()[0:64, 0:W]
            ts("dve", a, ps, bcolv, pfr, ALU.add, ALU.mult)
            k = tmp()[0:64, 0:W]
            ts("dve", k, a, 1.0 / TWO_PI, MAGIC, ALU.mult, ALU.add)
            ts("dve", k, k, -MAGIC, None, ALU.add)
            r = tmp()[0:64, 0:W]
            stt(r, k, -TWO_PI, a, ALU.mult, ALU.add)
            ts("dve", r, r, 3.1415925, -3.1415925, ALU.min, ALU.max)
            act(dst, r, AF.Sin)

        BT = [bfv(4624 + i * 512, 512) for i in range(7)] + [bfv(HB + 32768 + i * 512, 512) for i in range(4)]
        ONESB = bfv(HB + 73216, 128)
        memset("dve", ONESB, 1.0)
        G.bti = 0

        def btile():
            t = BT[G.bti % len(BT)]
            G.bti += 1
            return t

        EOD_S = [[bfv(HB + 26624 + s_ * 4096, 8, 512) for s_ in range(2)], [bfv(HB + 38912 + s_ * 4096, 8, 512) for s_ in range(2)]]
        BT_S = [bfv(4624 + i * 512, 512) for i in range(7)] + [bfv(HB + 47104 + i * 512, 512) for i in range(4)]

        def filter_gen(hy, li, EODx, BTx, defer=False):
            L, nT = hy["L"], hy["nT"]
            st = dict(cnt=0, ko=0, bti=0)

            def bt():
                t = BTx[st["bti"] % len(BTx)]
                st["bti"] += 1
                return t

            def setup():
                G.banks = list(range(6)); G.bi = 0
                P.dma("sp", PW1, pw1_d)
                P.dma("sp", PW2, pw2_d)

            def mlp(nb):
                W = min(512, L)
                P.dma("sp", ZT[:, 0:W], hy["zT"][:, nb * 512:nb * 512 + W])
                ps1 = bank()
                mm(ps1[0:64, 0:W], PW1, ZT[:, 0:W], True, True)
                sin_layer(H1[:, 0:W], ps1[0:64, 0:W], pb1, W)
                ps2 = bank()
                mm(ps2[0:64, 0:W], PW2, H1[:, 0:W], True, True)
                sin_layer(H2T[0:64, 0:W], ps2[0:64, 0:W], pb2, W)

            def chunk(nb, c4):
                ncg = nb * 4 + c4
                wt, wbt = WINB[st["cnt"] % 2]; st["cnt"] += 1
                P.dma("sp", wt, hy["win"][ncg * 128:(ncg + 1) * 128, :])
                P.dma("sp", wbt, hy["winb"][ncg * 128:(ncg + 1) * 128, :])
                psq = []
                for q in range(4):
                    pq = bank()
                    mm(pq[:, :], H2T[0:64, c4 * 128:(c4 + 1) * 128], PW3B[0:64, q * 512:(q + 1) * 512], True, True)
                    psq.append(pq)
                for o in range(2):
                    fw = bt(); bw = bt(); af = bt(); ab = bt()
                    tt("dve", fw, psq[o * 2][:, :], wt, ALU.mult)
                    tt("dve", bw, psq[o * 2 + 1][:, :], wbt, ALU.mult)
                    tt("dve", EODx[o][0][:, ncg, :], fw, bw, ALU.add)
                    tt("pool", EODx[o][1][:, ncg, :], fw, bw, ALU.subtract)
                    act(af, fw, AF.Abs)
                    act(ab, bw, AF.Abs)
                    mm(PS[6 + o][:, :], ONESB, af, ncg == 0, False, signal=True)
                    mm(PS[6 + o][:, :], ONESB, ab, False, ncg == nT - 1, signal=True)

            def post():
                for o in range(2):
                    recip(RN[o], PS[6 + o][:, :])
                    P.dma("sp", BIASL[o], hyb_d[o, :].partition_broadcast(128))
                    ts("dve", BIASL[o], BIASL[o], 1.0 / L, None, ALU.mult)
                G.banks = list(range(8)); G.bi = 0

            def kdft(fc):
                tc_t, ts_t = TFB[fc % 2]
                P.dma("sp", tc_t[:, 0:nT * 128], hy["tfc"][fc])
                P.dma("sp", ts_t[:, 0:nT * 128], hy["tfs"][fc])
                for o in range(2):
                    ps = bank()
                    for n in range(nT):
                        mm(ps[:, :], tc_t[:, n * 128:(n + 1) * 128], EODx[o][0][:, n, :], n == 0, n == nT - 1)
                    t = tmp()
                    tt("dve", t, ps[:, :], RN[o], ALU.mult)
                    kre = KOUT[st["ko"] % 4]; st["ko"] += 1
                    stt(kre, t, 1.0 / L, BIASL[o], ALU.mult, ALU.add)
                    P.dma("pool" if li == 1 else "act", kfs[li, o, 0, fc], kre)
                    ps2 = bank()
                    for n in range(nT):
                        mm(ps2[:, :], ts_t[:, n * 128:(n + 1) * 128], EODx[o][1][:, n, :], n == 0, n == nT - 1)
                    kim = KOUT[st["ko"] % 4]; st["ko"] += 1
                    stt(kim, ps2[:, :], 1.0 / L, RN[o], ALU.mult, ALU.mult)
                    P.dma("pool" if li == 1 else "act", kfs[li, o, 1, fc], kim)

            L_ = lambda f, *a_: (lambda: f(*a_))
            th = [setup]
            for nb in range(L // 512 if L >= 512 else 1):
                th.append(L_(mlp, nb))
                for c4 in range(min(512, L) // 128):
                    th.append(L_(chunk, nb, c4))
            th.append(post)
            for fc in range(nT):
                th.append(L_(kdft, fc))
            if defer:
                return th
            for t_ in th:
                t_()
            return []

        filter_gen(HY["c"], 0, EOD, BT)
        for ch in range(12):
            for j in range(3):
                ts("dve", DIAG[:, ch * 3 + j, :], IDENT, cwt[:, ch, j:j + 1], None, ALU.mult)
        browt = [f32v(2048 + i * 512, 512) for i in range(2)]
        gtmp = [f32v(3072 + i * 512, 512) for i in range(2)]
        gi = 0
        G.gi = 0
        def mods_blk(blk, wa=None):
            if wa is None:
                wa = WA[blk]
            pst = bank()
            for j in range(4):
                for k in range(8):
                    mm(pst[:, j * 2:j * 2 + 2], wa[:, k, j * 128:(j + 1) * 128], SCB[:, k, :], k == 0, k == 7)
            tt("dve", colm[:, blk * 4:blk * 4 + 4, :], pst[:, 0:8].rearrange("p (a b) -> p a b", a=4),
               bcol[:, blk * 4:blk * 4 + 4].unsqueeze(2).to_broadcast([128, 4, 2]), ALU.add)
            if blk in (4, 5, 10, 11):
                gate = 0 if blk < 6 else 1
                half = blk % 2
                bt = browt[G.gi % 2]
                P.dma("sp", bt, brow_d[0, blk * 512:(blk + 1) * 512].partition_broadcast(128))
                for g in range(2):
                    psg = bank()
                    for k in range(8):
                        mm(psg[:, :], SCBB[:, g, k * 128:(k + 1) * 128], wa[:, k, :], k == 0, k == 7)
                    gt = gtmp[G.gi % 2]; G.gi += 1
                    tt("dve", gt, psg[:, :], bt, ALU.add)
                    P.dma("sp", gsc[gate * 2 + g:gate * 2 + g + 1, half * 512:(half + 1) * 512], gt[0:1, :])
        for blk in range(4):
            mods_blk(blk)
        ts("dve", A1, colm[:, 8:16, :], 1.0, None, ALU.add)
        tt("dve", A1, A1, n1g.unsqueeze(2).to_broadcast([128, 8, 2]), ALU.mult)
        make_hT(xs[0:512, :], HT_H, A1, colm[:, 0:8, :], 1, [0, 1024], [HB + 51200, HB + 73344], [HB + 51200, HB + 73344])
        for blk in range(4, 6):
            mods_blk(blk)

        WA_LATE = [bfv(HB + 52224, 8, 512), bfv(HB + 68608, 8, 512)]

        def late_mod_thunks():
            def dma_(blk):
                wdma(WA_LATE[blk % 2], wada_d[:, blk * 512:(blk + 1) * 512].rearrange("(k p) n -> p k n", p=128), window=0)

            def a2_():
                ts("dve", A2, colm[:, 32:40, :], 1.0, None, ALU.add)
                tt("dve", A2, A2, n2g.unsqueeze(2).to_broadcast([128, 8, 2]), ALU.mult)
            L_ = lambda f, *a_: (lambda: f(*a_))
            comp_ = lambda blk: mods_blk(blk, WA_LATE[blk % 2])
            th = [L_(dma_, 6), L_(dma_, 7)]
            for blk in range(6, 12):
                th.append(L_(comp_, blk))
                if blk + 2 < 12:
                    th.append(L_(dma_, blk + 2))
            th.append(a2_)
            return th


        G.kfi = 0

        def hy_fwd(hy, li, order, blocks):
            nT = hy["nT"]
            for fc in range(nT):
                tc_t, ts_t = TFB[fc % 2]
                P.dma("sp", tc_t[:, 0:nT * 128], hy["tfc"][fc])
                P.dma("sp", ts_t[:, 0:nT * 128], hy["tfs"][fc])
                kre, kim = KFB[G.kfi % 2]; G.kfi += 1
                P.dma("sp", kre, kfs[li, order, 0, fc])
                P.dma("sp", kim, kfs[li, order, 1, fc])
                pr = bank()
                for n in range(nT):
                    mm(pr[:, :], tc_t[:, n * 128:(n + 1) * 128], HV[:, blocks[n], :], n == 0, n == nT - 1)
                pi = bank()
                for n in range(nT):
                    mm(pi[:, :], ts_t[:, n * 128:(n + 1) * 128], HV[:, blocks[n], :], n == 0, n == nT - 1)
                ure = UREB[(G.kfi % 2) * 2]; uim = UREB[(G.kfi % 2) * 2 + 1]
                cp("act", ure, pr[:, :]); cp("act", uim, pi[:, :])
                t1, t2, t3, t4 = CMB
                tt("dve", t1, ure, kre, ALU.mult)
                tt("pool", t2, uim, kim, ALU.mult)
                tt("dve", t3, ure, kim, ALU.mult)
                tt("pool", t4, uim, kre, ALU.mult)
                tt("dve", YRE[:, fc, :], t1, t2, ALU.subtract)
                tt("dve", YIM[:, fc, :], t3, t4, ALU.add)
                fill(1)

        def hy_seq(hy, li, blocks, hx2, z2dst, own_len):
            nT = hy["nT"]; nF = nT
            hy_fwd(hy, li, 0, blocks)
            for tb in range(hy["nTB"]):
                ic, isn = TIB[tb % 2]
                P.dma("sp", ic[:, 0:nF * 256], hy["tic"][tb])
                P.dma("sp", isn[:, 0:nF * 256], hy["tis"][tb])
                for t2 in range(2):
                    tcn = tb * 2 + t2
                    ps = bank()
                    for fc in range(nF):
                        mm(ps[:, :], ic[:, fc * 256 + t2 * 128:fc * 256 + t2 * 128 + 128], YRE[:, fc, :], fc == 0, False)
                        mm(ps[:, :], isn[:, fc * 256 + t2 * 128:fc * 256 + t2 * 128 + 128], YIM[:, fc, :], False, fc == nF - 1)
                    tt("dve", HV[:, blocks[tcn], :], ps[:, :], HX1[:, blocks[tcn], :], ALU.mult)
                fill(1)
            hy_fwd(hy, li, 1, blocks)
            flush()
            for tb in range(own_len // 256):
                ic, isn = TIB[tb % 2]
                P.dma("sp", ic[:, 0:nF * 256], hy["tic"][tb])
                P.dma("sp", isn[:, 0:nF * 256], hy["tis"][tb])
                for cc in range(4):
                    ps = bank()
                    for fc in range(nF):
                        mm(ps[:, 0:256], YRE[:, fc, cc * 128:(cc + 1) * 128], ic[:, fc * 256:(fc + 1) * 256], fc == 0, False)
                        mm(ps[:, 0:256], YIM[:, fc, cc * 128:(cc + 1) * 128], isn[:, fc * 256:(fc + 1) * 256], False, fc == nF - 1)
                    tt("dve", z2dst[:, cc, tb * 256:(tb + 1) * 256], ps[:, 0:256], hx2[:, cc, tb * 256:(tb + 1) * 256], ALU.mult)

        def cmul(pr, pi, kre, kim, yre, yim):
            ure = UREB[(G.kfi % 2) * 2]; uim = UREB[(G.kfi % 2) * 2 + 1]; G.kfi += 1
            cp("act", ure, pr[:, :]); cp("act", uim, pi[:, :])
            t1, t2, t3, t4 = CMB
            tt("dve", t1, ure, kre, ALU.mult)
            tt("pool", t2, uim, kim, ALU.mult)
            tt("dve", t3, ure, kim, ALU.mult)
            tt("pool", t4, uim, kre, ALU.mult)
            tt("dve", yre, t1, t2, ALU.subtract)
            tt("dve", yim, t3, t4, ALU.add)

        def hy_ctx_multi(seqs):
            hy = HY["c"]
            CKF = [[(KFB[0][0], KFB[0][1]), (KFB[1][0], KFB[1][1])],
                   [(bfv(HB + 70656, 512), bfv(HB + 71168, 512)), (bfv(HB + 71680, 512), bfv(HB + 72192, 512))]]
            for fc in range(2):
                P.dma("sp", TFB[fc][0][:, 0:256], hy["tfc"][fc])
                P.dma("sp", TFB[fc][1][:, 0:256], hy["tfs"][fc])
                for o in range(2):
                    P.dma("sp", CKF[o][fc][0], kfs[0, o, 0, fc])
                    P.dma("sp", CKF[o][fc][1], kfs[0, o, 1, fc])
            ic, isn = TIB[0]
            P.dma("sp", ic[:, 0:512], hy["tic"][0])
            P.dma("sp", isn[:, 0:512], hy["tis"][0])

            def fwd(sq, order):
                for fc in range(2):
                    pr = bank()
                    for n in range(2):
                        mm(pr[:, :], TFB[fc][0][:, n * 128:(n + 1) * 128], HV[:, sq["blocks"][n], :], n == 0, n == 1)
                    pi = bank()
                    for n in range(2):
                        mm(pi[:, :], TFB[fc][1][:, n * 128:(n + 1) * 128], HV[:, sq["blocks"][n], :], n == 0, n == 1)
                    y = 2 * sq["slot"] + fc
                    cmul(pr, pi, CKF[order][fc][0], CKF[order][fc][1], YRE[:, y, :], YIM[:, y, :])

            def inv1(sq):
                for t2 in range(2):
                    ps = bank()
                    for fc in range(2):
                        y = 2 * sq["slot"] + fc
                        mm(ps[:, :], ic[:, fc * 256 + t2 * 128:fc * 256 + t2 * 128 + 128], YRE[:, y, :], fc == 0, False)
                        mm(ps[:, :], isn[:, fc * 256 + t2 * 128:fc * 256 + t2 * 128 + 128], YIM[:, y, :], False, fc == 1)
                    b_ = sq["blocks"][t2]
                    tt("dve", HV[:, b_, :], ps[:, :], HX1[:, b_, :], ALU.mult)

            def inv2(sq):
                for cc in range(4):
                    ps = bank()
                    for fc in range(2):
                        y = 2 * sq["slot"] + fc
                        mm(ps[:, 0:256], YRE[:, y, cc * 128:(cc + 1) * 128], ic[:, fc * 256:(fc + 1) * 256], fc == 0, False)
                        mm(ps[:, 0:256], YIM[:, y, cc * 128:(cc + 1) * 128], isn[:, fc * 256:(fc + 1) * 256], False, fc == 1)
                    tt("dve", sq["z2"][:, cc, :], ps[:, 0:256], sq["hx2"][:, cc, :], ALU.mult)

            for sq in seqs:
                fwd(sq, 0)
            for sq in seqs:
                inv1(sq)
            for sq in seqs:
                fwd(sq, 1)
            for sq in seqs:
                inv2(sq)

        def hy_make(src, g, hTb, defer=False, xo=None):
            make_hT(src, hTb, A1, colm[:, 0:8, :], g, xo if xo is not None else [0, 1024], [XN_H, HB + 50176], [XN_H, HB + 50176], defer=defer)

        def hy_proj_st(hTb, seglen, blk0, need_x2, HX2):
            HT_H = hTb
            nseg = 512 // seglen
            nch = 12 if need_x2 else 8

            def stA(ch):
                ps = bank(hold=True)
                for k in range(8):
                    mm(ps[:, :], WHU[:, k, ch * 128:(ch + 1) * 128], HT_H[:, k, :], k == 0, k == 7)
                return ps

            def stB(ch, ps):
                u = UBF[ch % 2]
                cp("act", u, ps[:, :])
                release(ps)
                u3 = u.rearrange("p (s t) -> p s t", s=nseg)
                py = bank()
                py3 = py[:, :].rearrange("p (s t) -> p s t", s=nseg)
                mm(py[:, :], DIAG[:, ch * 3 + 1, :], u, True, False)
                for sg_ in range(nseg):
                    o_ = sg_ * seglen
                    mm(py[:, o_ + 1:o_ + seglen], DIAG[:, ch * 3 + 0, :], u[:, o_:o_ + seglen - 1], False, False)
                    mm(py[:, o_:o_ + seglen - 1], DIAG[:, ch * 3 + 2, :], u[:, o_ + 1:o_ + seglen], False, sg_ == nseg - 1)
                if ch < 8:
                    cv = CVB[ch % 2]
                    act(cv, py[:, :], AF.Identity, bias=cbt[:, ch:ch + 1], scale=1.0)
                    return cv
                act(HX2[:, ch - 8, :], py[:, :], AF.Identity, bias=cbt[:, ch:ch + 1], scale=1.0)
                return None

            def stC(ch, cv):
                pst = bank()
                for b in range(4):
                    mm(pst[:, b * 128:(b + 1) * 128], cv[:, b * 128:(b + 1) * 128], IDENT, True, True, signal=(b == 3))
                dstT = HV if ch < 4 else HX1
                cp("act", dstT[:, blk0:blk0 + 4, (ch % 4) * 128:(ch % 4) * 128 + 128], pst[:, :].rearrange("p (b c) -> p b c", b=4))

            ps_next = stA(0)
            pend = None
            for ch in range(nch):
                ps = ps_next
                if ch + 1 < nch:
                    ps_next = stA(ch + 1)
                cv = stB(ch, ps)
                fill(3)
                if pend is not None:
                    stC(*pend)
                pend = (ch, cv) if cv is not None else None
            if pend is not None:
                stC(*pend)

        fgq = filter_gen(HY["s"], 1, EOD_S, BT_S, defer=True)
        fgq.pop(0)()
        hy_make(xs[512:1024, :], 1, HT_H2, defer=True, xo=[0, 0])
        mk = G.fillq; G.fillq = []
        while mk or fgq:
            if mk:
                G.fillq.append(mk.pop(0))
            if fgq:
                G.fillq.append(fgq.pop(0))
        hy_proj_st(HT_H, 64, 0, True, HX2)
        hy_make(xc[0:512, :], 0, HT_H, defer=True)
        hy_proj_st(HT_H2, 64, 4, False, HX2)
        flush()
        rc = f32v(2048, 8, 128)
        P.dma("sp", rc, rc_d)
        act(lgt, dect, AF.Sigmoid)
        act(lgt, lgt, AF.Ln)
        act(cdt, lgt, AF.Exp, scale=128.0)
        sc_ret = 128.0 ** -0.5
        tA = f32v(3072, 128); tB = f32v(3200, 128)
        for h in range(4):
            lf = lgt[:, h:h + 1]; lb = lgt[:, 4 + h:5 + h]
            act(tA, rc[:, 0, :], AF.Exp, scale=lf)
            tt("dve", tA, tA, rc[:, 1, :], ALU.mult)
            act(tB, rc[:, 2, :], AF.Exp, scale=lb)
            tt("dve", tB, tB, rc[:, 3, :], ALU.mult)
            tt("dve", tA, tA, tB, ALU.add)
            ts("dve", DM[:, h, :], tA, sc_ret, None, ALU.mult)
            act(QDF[:, h, :], rc[:, 4, :], AF.Exp, scale=lf)
            act(QDB[:, h, :], rc[:, 5, :], AF.Exp, scale=lb)
            act(tA, rc[:, 6, :], AF.Exp, scale=lf)
            ts("dve", KDF[:, h, :], tA, sc_ret, None, ALU.mult)
            act(tB, rc[:, 7, :], AF.Exp, scale=lb)
            ts("dve", KDB[:, h, :], tB, sc_ret, None, ALU.mult)

        G.fillq.extend(late_mod_thunks())
        hy_seq(HY["s"], 1, list(range(8)), HX2, Z2T[:, :, 1024:1536], 512)
        hy_make(xc[512:1024, :], 0, HT_H2, defer=True)
        hy_proj_st(HT_H, 256, 0, True, HX2B)
        flush()
        hy_proj_st(HT_H2, 256, 4, True, HX2)
        cseqs = []
        for i in range(2):
            hx = HX2B if i == 0 else HX2
            for s_ in range(2):
                cseqs.append(dict(blocks=[4 * i + 2 * s_, 4 * i + 2 * s_ + 1], slot=2 * i + s_,
                                  hx2=hx[:, :, s_ * 256:(s_ + 1) * 256],
                                  z2=Z2T[:, :, i * 512 + s_ * 256:i * 512 + (s_ + 1) * 256]))
        WRB_pre = [bfv(8704 + cb_ * 4096, 8, 512) for cb_ in range(4)]
        for cb_ in range(4):
            wdma(WRB_pre[cb_], win_d[:, cb_ * 512:(cb_ + 1) * 512].rearrange("(k p) n -> p k n", p=128), window=0)
        hy_ctx_multi(cseqs)
        WRB = [bfv(8704 + cb_ * 4096, 8, 512) for cb_ in range(8)]
        WRO = bfv(41472, 4, 1024); WHO = bfv(45568, 4, 1024); WO = bfv(49664, 8, 1024)
        HT_R = bfv(WK0, 8, 512)
        QT = bfv(WK0 + 4096, 4, 512); KT = bfv(WK0 + 6144, 4, 512)
        KTOK = bfv(WK0 + 8192, 4, 512); VTOK = bfv(WK0 + 10240, 4, 512); SG = bfv(WK0 + 12288, 4, 512)
        MIXT = bfv(WK0 + 4096, 8, 512)
        GOT = bfv(WK0 + 14336, 4, 512)
        ATTM2 = [bfv(WK0 + 16384 + i * 512, 4, 128) for i in range(2)]
        QTF2 = [bfv(WK0 + 17408 + i * 512, 4, 128) for i in range(2)]
        QTB2 = [bfv(WK0 + 18432 + i * 512, 4, 128) for i in range(2)]
        KFt = bfv(WK0 + 19456, 4, 128); KBt = bfv(WK0 + 19968, 4, 128)
        GOTOK2 = [bfv(WK0 + 20480 + i * 512, 4, 128) for i in range(2)]
        SFB = [bfv(WK0 + 21504 + i * 512, 4, 128) for i in range(4)]
        SBB = [bfv(WK0 + 23552 + i * 512, 4, 128) for i in range(4)]
        JUNK_R = WK0 + 4096; XN_R = WK0 + 6144
        XB_OFF = 0
        X1B = f32v(1024, 1024); G1BC = f32v(2048, 1024)
        SFR = f32v(3072, 4, 128); SBR = f32v(3584, 4, 128)
        S0F = f32v(7168, 4, 128); S0B = f32v(7680, 4, 128)

        for cb_ in range(4):
            wdma(WRB[4 + cb_], win_d[:, 3584 + cb_ * 512:3584 + (cb_ + 1) * 512].rearrange("(k p) n -> p k n", p=128))
        wdma(WRO, wro_d.rearrange("(k p) n -> p k n", p=128))
        wdma(WHO, who_d.rearrange("(k p) n -> p k n", p=128))
        for cb_ in range(2):
            wdma(WO[:, :, cb_ * 512:(cb_ + 1) * 512], wo_d[:, cb_ * 512:(cb_ + 1) * 512].rearrange("(k p) n -> p k n", p=128))

        def proj_tm(hT, col0, dst, func=None):
            for blk in range(4):
                ps = bank()
                for k in range(8):
                    mm(ps[:, :], hT[:, k, blk * 128:(blk + 1) * 128], WRB[col0 // 512][:, k, :], k == 0, k == 7)
                if func is None:
                    cp("dve", dst[:, blk, :], ps[:, :])
                else:
                    act(dst[:, blk, :], ps[:, :], func)

        def proj_fm(hT, col0, dst):
            for h in range(4):
                ps = bank()
                for k in range(8):
                    mm(ps[:, :], WRB[col0 // 512][:, k, h * 128:(h + 1) * 128], hT[:, k, :], k == 0, k == 7)
                cp("act", dst[:, h, :], ps[:, :])

        def kv_chunk(c, decay_t, scaled_t):
            tt("dve", scaled_t, KTOK[:, c, :].rearrange("p (h d) -> p h d", h=4), decay_t, ALU.mult)
            ps = bank()
            for h in range(4):
                mm(ps[:, h * 128:(h + 1) * 128], scaled_t[:, h, :], VTOK[:, c, h * 128:(h + 1) * 128], True, True, signal=(h == 3))
            return ps

        def state_step(S, ps, cd0, has_prev):
            ps3 = ps[:, :].rearrange("p (h v) -> p h v", h=4)
            if not has_prev:
                cp("dve", S, ps3)
            else:
                for h in range(4):
                    stt(S[:, h, :], S[:, h, :], cdt[:, cd0 + h:cd0 + h + 1], ps3[:, h, :], ALU.mult, ALU.add)

        def ret_st(seqs):
            items = []
            chains = []
            fS = [SFR, S0F]; bS = [SBR, S0B]
            scl = [(KFt, KBt), (QTF2[0], QTB2[0])]
            for si, (chunks, init_f, init_b, out_idx, slot0) in enumerate(seqs):
                n = len(chunks)
                Sf = fS[si]; Sb = bS[si]; kf_t, kb_t = scl[si]

                def fstep(i, c, Sf=Sf, kf_t=kf_t, slot0=slot0, init_f=init_f):
                    has = init_f or i > 0
                    if has:
                        cp("act", SFB[slot0 + i], Sf)
                    ps = kv_chunk(c, KDF, kf_t)
                    state_step(Sf, ps, 0, has)

                def bstep(i, c, Sb=Sb, kb_t=kb_t, slot0=slot0, init_b=init_b, n=n):
                    has = init_b or i < n - 1
                    if has:
                        cp("act", SBB[slot0 + i], Sb)
                    ps = kv_chunk(c, KDB, kb_t)
                    state_step(Sb, ps, 4, has)
                chains.append([(lambda i=i, c=c, f=fstep: f(i, c)) for i, c in enumerate(chunks)])
                chains.append([(lambda i=i, f=bstep, chunks=chunks: f(i, chunks[i])) for i in range(n - 1, -1, -1)])
                for i, c in enumerate(chunks):
                    items.append((c, slot0 + i, (init_f or i > 0), (init_b or i < n - 1)))
            for k in range(max(len(ch) for ch in chains)):
                for ch in chains:
                    if k < len(ch):
                        ch[k]()
            for si, (chunks, init_f, init_b, out_idx, slot0) in enumerate(seqs):
                if out_idx is not None:
                    P.dma("pool", nsf[out_idx].rearrange("h d v -> d h v"), fS[si])
                    P.dma("pool", nsb[out_idx].rearrange("h d v -> d h v"), bS[si])
            m = len(items)

            def s1(j):
                c, slot, use_f, use_b = items[j]
                csl = slice(c * 128, (c + 1) * 128)
                pa = bank()
                for h in range(4):
                    mm(pa[:, h * 128:(h + 1) * 128], KT[:, h, csl], QT[:, h, csl], True, True, signal=(h == 3))
                tt("dve", ATTM2[j % 2], pa[:, :].rearrange("p (h i) -> p h i", h=4), DM, ALU.mult)
                if use_f:
                    tt("pool", QTF2[j % 2], QT[:, :, csl], QDF, ALU.mult)
                if use_b:
                    tt("pool", QTB2[j % 2], QT[:, :, csl], QDB, ALU.mult)

            def s23(j):
                c, slot, use_f, use_b = items[j]
                ATTM = ATTM2[j % 2]; QTF = QTF2[j % 2]; QTB = QTB2[j % 2]; GOTOK = GOTOK2[j % 2]
                po = bank()
                for h in range(4):
                    o_ = po[:, h * 128:(h + 1) * 128]
                    mm(o_, ATTM[:, h, :], VTOK[:, c, h * 128:(h + 1) * 128], True, not (use_f or use_b))
                    if use_f:
                        mm(o_, QTF[:, h, :], SFB[slot][:, h, :], False, not use_b)
                    if use_b:
                        mm(o_, QTB[:, h, :], SBB[slot][:, h, :], False, True)
                o0 = 24 * (j % 2)
                sums = stat2[:, o0:o0 + 4]; ssq = stat2[:, o0 + 4:o0 + 8]; mean = stat2[:, o0 + 8:o0 + 12]
                m2 = stat2[:, o0 + 12:o0 + 16]; var = stat2[:, o0 + 16:o0 + 20]; rstd = stat2[:, o0 + 20:o0 + 24]
                osb = tmp()
                cp("act", osb, po[:, :])
                o3 = osb.rearrange("p (h v) -> p h v", h=4)
                P.op("dve", lambda e, sums=sums, o3=o3: e.tensor_reduce(out=sums, in_=o3, axis=AX.X, op=ALU.add), reads=[o3], writes=[sums])
                sq = tmp()
                tt("pool", sq, osb, osb, ALU.mult)
                sq3 = sq.rearrange("p (h v) -> p h v", h=4)
                P.op("dve", lambda e, ssq=ssq, sq3=sq3: e.tensor_reduce(out=ssq, in_=sq3, axis=AX.X, op=ALU.add), reads=[sq3], writes=[ssq])
                ts("dve", mean, sums, 1.0 / 128, None, ALU.mult)
                tt("dve", m2, mean, mean, ALU.mult)
                stt(var, ssq, 1.0 / 128, m2, ALU.mult, ALU.subtract)
                act(rstd, var, AF.Ln, bias=EPSC, scale=1.0)
                act(rstd, rstd, AF.Exp, scale=-0.5)
                t = tmp()
                t3 = t.rearrange("p (h v) -> p h v", h=4)
                stt(m2, mean, -1.0, rstd, ALU.mult, ALU.mult)
                for h in range(4):
                    act(t3[:, h, :], o3[:, h, :], AF.Identity, bias=m2[:, h:h + 1], scale=rstd[:, h:h + 1])
                tt("pool", GOTOK, t3, SG[:, c, :].rearrange("p (h v) -> p h v", h=4), ALU.mult)

            def s4(j):
                c = items[j][0]
                csl = slice(c * 128, (c + 1) * 128)
                GOTOK = GOTOK2[j % 2]
                pt = bank()
                for h in range(4):
                    mm(pt[:, h * 128:(h + 1) * 128], GOTOK[:, h, :], IDENT, True, True, signal=(h == 3))
                cp("act", GOT[:, :, csl], pt[:, :].rearrange("p (h i) -> p h i", h=4))

            s1(0)
            for j in range(m):
                if j + 1 < m:
                    s1(j + 1)
                s23(j)
                if j > 0:
                    s4(j - 1)
            s4(m - 1)

        G.x1q = "pool"

        def merge_out(src, g_unused, x1dst, nxt=None):
            for oc in range(8):
                pgr = bank()
                for k in range(8):
                    mm(pgr[:, :], WRB[4 + oc // 4][:, k, (oc % 4) * 128:(oc % 4 + 1) * 128], HT_R[:, k, :], k == 0, k == 7)
                pgh = bank()
                for k in range(8):
                    mm(pgh[:, :], WRB[6 + oc // 4][:, k, (oc % 4) * 128:(oc % 4 + 1) * 128], HT_R[:, k, :], k == 0, k == 7)
                pyr = bank()
                for k in range(4):
                    mm(pyr[:, :], WRO[:, k, oc * 128:(oc + 1) * 128], GOT[:, k, :], k == 0, k == 3)
                pyh = bank()
                for k in range(4):
                    mm(pyh[:, :], WHO[:, k, oc * 128:(oc + 1) * 128], src[:, k, :], k == 0, k == 3)
                sgr = tmp(); sgh = tmp(); m1 = tmp(); m2_ = tmp()
                act(sgr, pgr[:, :], AF.Sigmoid)
                act(sgh, pgh[:, :], AF.Sigmoid)
                tt("dve", m1, pyr[:, :], sgr, ALU.mult)
                tt("dve", m2_, pyh[:, :], sgh, ALU.mult)
                tt("pool", MIXT[:, oc, :], m1, m2_, ALU.add)
            if nxt is not None:
                nxt()
            for blk in range(4):
                xb = f32v(9216 if blk % 2 == 0 else 1024, 1024)
                P.dma("sp", xb, x1dst[1][blk * 128:(blk + 1) * 128, :])
                for hf in range(2):
                    ps = bank(hold=True)
                    for k in range(8):
                        mm(ps[:, :], MIXT[:, k, blk * 128:(blk + 1) * 128], WO[:, k, hf * 512:(hf + 1) * 512], k == 0, k == 7)
                    fill(2)
                    t = tmp()
                    tt("dve", t, ps[:, :], G1BC[:, hf * 512:(hf + 1) * 512], ALU.mult)
                    release(ps)
                    tt("dve", xb[:, hf * 512:(hf + 1) * 512], t, xb[:, hf * 512:(hf + 1) * 512], ALU.add)
                P.dma(G.x1q, x1dst[0][blk * 128:(blk + 1) * 128, :], xb)

        def proj_tm_blk(hT, blk, col0, dst, func=None):
            ps = bank()
            for k in range(8):
                mm(ps[:, :], hT[:, k, blk * 128:(blk + 1) * 128], WRB[col0 // 512][:, k, :], k == 0, k == 7)
            if func is None:
                cp("dve", dst[:, blk, :], ps[:, :])
            else:
                act(dst[:, blk, :], ps[:, :], func)

        XN_G = WK0 + 14336

        def ret_make(src, g, full, defer, with_early=True):
            def early(blk):
                if not full:
                    proj_tm_blk(HT_R, blk, 512, KTOK)
                proj_tm_blk(HT_R, blk, 1024, VTOK)
                if full:
                    proj_tm_blk(HT_R, blk, 1536, SG, AF.Silu)
            make_hT(src, HT_R, A1, colm[:, 0:8, :], g, [XB_OFF, 8192], [XN_G, XN_G + 1024], [XN_G, XN_G + 1024],
                    after_blk=(early if with_early else None), defer=defer)
            return early

        def ret_rest(full, early_done):
            if not early_done:
                for blk in range(4):
                    if not full:
                        proj_tm_blk(HT_R, blk, 512, KTOK)
                    proj_tm_blk(HT_R, blk, 1024, VTOK)
                    if full:
                        proj_tm_blk(HT_R, blk, 1536, SG, AF.Silu)
            if full:
                proj_fm(HT_R, 512, KT)
                for blk in range(4):
                    pt = bank()
                    for h in range(4):
                        mm(pt[:, h * 128:(h + 1) * 128], KT[:, h, blk * 128:(blk + 1) * 128], IDENT, True, True, signal=(h == 3))
                    cp("dve", KTOK[:, blk, :], pt[:, :])
                proj_fm(HT_R, 0, QT)

        HT_F = bfv(67584, 8, 512)

        def ffn_make(st, defer, xoffs=None, junk=6144):
            make_hT(x1s[st * 512:(st + 1) * 512, :], HT_F, A2, colm[:, 24:32, :], 0 if st < 2 else 1,
                    xoffs if xoffs is not None else [XB_OFF, 9216], [junk, junk], None, defer=defer, f32mode=True)

        WF1 = bfv(0, 8, NIN)

        def load_wf1():
            for cb_ in range(6):
                wdt = 512 if cb_ < 5 else 256
                for c0 in (cb_ * 512, DFF + cb_ * 512):
                    wdma(WF1[:, :, c0:c0 + wdt], wf1_d[:, c0:c0 + wdt].rearrange("(k p) n -> p k n", p=128))

        P.dma("sp", G1BC, gsc[1, :].partition_broadcast(128))
        P.dma("sp", S0F, s0f_d.rearrange("h d v -> d h v"))
        P.dma("sp", S0B, s0b_d.rearrange("h d v -> d h v"))
        ret_make(xs[512:1024, :], 1, False, False)
        ret_rest(False, True)
        ret_make(xs[0:512, :], 1, True, True, with_early=False)
        cp("dve", SFR, S0F)
        for c in range(4):
            ps = kv_chunk(c, KDF, KFt)
            state_step(SFR, ps, 0, True)
            fill(2)
        cp("dve", SBR, S0B)
        for c in range(3, -1, -1):
            ps = kv_chunk(c, KDB, KBt)
            state_step(SBR, ps, 4, True)
            fill(2)
        flush()
        tS = tmp().rearrange("p (h v) -> p h v", h=4)
        ts("dve", tS, SFR, selt[:, 1:2], None, ALU.mult)
        stt(SFR, S0F, selt[:, 0:1], tS, ALU.mult, ALU.add)
        tS2 = tmp().rearrange("p (h v) -> p h v", h=4)
        ts("dve", tS2, SBR, selt[:, 0:1], None, ALU.mult)
        stt(SBR, S0B, selt[:, 1:2], tS2, ALU.mult, ALU.add)
        ret_rest(True, False)
        ret_st([([0, 1, 2, 3], True, True, None, 0)])

        def nxt_c0():
            P.dma("sp", G1BC, gsc[0, :].partition_broadcast(128))
            ret_make(xc[0:512, :], 0, True, True)
        merge_out(Z2T[:, :, 1024:1536], 1, (x1s[1024:1536, :], xs[0:512, :]), nxt=lambda: ret_make(xc[0:512, :], 0, True, True))
        P.dma("sp", G1BC, gsc[0, :].partition_broadcast(128))
        flush()
        ret_rest(True, True)
        ret_st([([0, 1], False, False, 0, 0), ([2, 3], False, False, 1, 2)])
        merge_out(Z2T[:, :, 0:512], 0, (x1s[0:512, :], xc[0:512, :]), nxt=lambda: ret_make(xc[512:1024, :], 0, True, True))
        flush()
        ret_rest(True, True)
        ret_st([([0, 1], False, False, 2, 0), ([2, 3], False, False, 3, 2)])
        G.x1q = "sp"
        merge_out(Z2T[:, :, 512:1024], 0, (x1s[512:1024, :], xc[512:1024, :]), nxt=lambda: (load_wf1(), ffn_make(0, True, xoffs=[XB_OFF, 8192], junk=7168)))

        WF2 = bfv(45056, 22, 1024)
        UT = bfv(71680, 22, 512)
        JUNK_F = 71680; XN_F = 72704
        G2 = [f32v(2048, 1024), f32v(3072, 1024)]
        FG = f32v(7168, 1024); OUTT = f32v(8192, 1024)
        for cb_ in range(11):
            wdma(WF2[:, cb_ * 2:(cb_ + 1) * 2, :], wf2_d[cb_ * 256:(cb_ + 1) * 256, :].rearrange("(k p) n -> p k n", p=128))
        P.dma("sp", G2[0], gsc[2, :].partition_broadcast(128))
        P.dma("sp", G2[1], gsc[3, :].partition_broadcast(128))
        P.dma("sp", FG, fg_d[0, :].partition_broadcast(128))
        G.ntmp = 4
        for st in range(3):
            g = 0 if st < 2 else 1
            src = x1s[st * 512:(st + 1) * 512, :]
            flush()
            for ch in range(22):
                pa = bank()
                for k in range(8):
                    mm(pa[:, :], WF1[:, k, ch * 128:(ch + 1) * 128], HT_F[:, k, :], k == 0, k == 7)
                pb = bank()
                for k in range(8):
                    mm(pb[:, :], WF1[:, k, DFF + ch * 128:DFF + (ch + 1) * 128], HT_F[:, k, :], k == 0, k == 7)
                sa = tmp()
                act(sa, pa[:, :], AF.Silu)
                tt("dve", UT[:, ch, :], pb[:, :], sa, ALU.mult)
            if st + 1 < 3:
                ffn_make(st + 1, True)
            for blk in range(4):
                P.dma("sp", X1B, src[blk * 128:(blk + 1) * 128, :])
                for hf in range(2):
                    ps = bank(hold=True)
                    for k in range(22):
                        mm(ps[:, :], UT[:, k, blk * 128:(blk + 1) * 128], WF2[:, k, hf * 512:(hf + 1) * 512], k == 0, k == 21)
                    fill(2)
                    t = tmp()
                    tt("dve", t, ps[:, :], G2[g][:, hf * 512:(hf + 1) * 512], ALU.mult)
                    release(ps)
                    tt("pool", X1B[:, hf * 512:(hf + 1) * 512], t, X1B[:, hf * 512:(hf + 1) * 512], ALU.add)
                ss = stat[:, 2:3]; rs = stat[:, 3:4]
                memset("dve", ss, 0.0)
                stt(OUTT, X1B, 1.0, X1B, ALU.mult, ALU.mult, accum=ss)
                act(rs, ss, AF.Ln, bias=EPSC, scale=1.0 / D)
                act(rs, rs, AF.Exp, scale=-0.5)
                stt(OUTT, X1B, rs, FG, ALU.mult, ALU.mult)
                if st < 2:
                    P.dma("pool", yc[st * 512 + blk * 128:st * 512 + (blk + 1) * 128, :], OUTT)
                else:
                    P.dma("pool", ys[blk * 128:(blk + 1) * 128, :], OUTT)
        P.finish()
        P.build()
    G.P = P
    return nc


import math
import ml_dtypes

_BF = ml_dtypes.bfloat16
_CACHE = {}


def _hy_consts(L, pos):
    f32 = np.float32
    n = pos.astype(np.float64)
    t = np.linspace(0.0, 1.0, L, dtype=f32)[pos][:, None]
    ang = (f32(2.0 * math.pi) * np.arange(L, dtype=f32)[:, None] / f32(L))[pos]
    bands = np.linspace(1e-4, 16 - 1, 16, dtype=f32)[None]
    z = np.concatenate([t, np.cos(bands * ang), -np.sin(bands * ang)], axis=-1).astype(f32)
    max_decay = math.log(1e-2) / 0.3
    min_decay = math.log(1e-2) / 1.5
    deltas = np.linspace(min_decay, max_decay, 512, dtype=f32)
    win = np.exp(-t * np.abs(deltas)[None, :]).astype(f32)
    winb = win.copy()
    winb[pos == 0, :] = 0.0
    nT = L // 128
    fidx = np.arange(L, dtype=np.float64) + 0.5
    ang2 = np.pi * np.outer(n, fidx) / L
    Cm = np.cos(ang2); Sm = np.sin(ang2)

    def fwd_tab(M):
        A = M.reshape(nT, 128, nT, 128)
        return np.ascontiguousarray(A.transpose(2, 1, 0, 3).reshape(nT, 128, nT * 128)).astype(_BF)

    def inv_tab(M):
        A = M.reshape(L // 256, 256, nT, 128)
        return np.ascontiguousarray(A.transpose(0, 3, 2, 1).reshape(L // 256, 128, nT * 256)).astype(_BF)
    return dict(zT=np.ascontiguousarray(z.T), win=win, winb=winb,
                tfc=fwd_tab(Cm), tfs=fwd_tab(Sm), tic=inv_tab(Cm), tis=inv_tab(Sm))


def _ret_consts():
    j = np.arange(128)[:, None].astype(np.float32)
    i = np.arange(128)[None, :].astype(np.float32)
    rc = np.zeros((128, 8, 128), np.float32)
    rc[:, 0] = np.where(i >= j, i - j, 0.0)
    rc[:, 1] = (i >= j)
    rc[:, 2] = np.where(j >= i, j - i, 0.0)
    rc[:, 3] = (j >= i)
    rc[:, 4] = np.broadcast_to(i + 1.0, (128, 128))
    rc[:, 5] = np.broadcast_to(128.0 - i, (128, 128))
    rc[:, 6] = np.broadcast_to(127.0 - j, (128, 128))
    rc[:, 7] = np.broadcast_to(j, (128, 128))
    return rc


def _col(v):
    return np.ascontiguousarray(np.asarray(v, np.float32).reshape(-1, 128).T)


def kernel(x_prompt, x_sample, state_ret_fwd, state_ret_bwd, c, c_ctx,
           norm1_g, norm2_g, w_ada, b_ada, w_in, ret_decay_fwd, ret_decay_bwd,
           hy_conv_w, hy_conv_b, hy_pos_w1, hy_pos_b1, hy_pos_w2, hy_pos_b2, hy_pos_w3,
           hy_sin_freq, hy_bias, w_ret_o, w_hy_o, w_out, w_ffn_in, w_ffn_out, final_g):
    f = lambda a: np.ascontiguousarray(np.asarray(a, np.float32))
    if "nc" not in _CACHE:
        _CACHE["nc"] = build_nc()
        _CACHE["hc"] = _hy_consts(256, np.arange(256))
        _CACHE["hs"] = [_hy_consts(1024, np.concatenate([np.arange(512) + hh * 512, np.arange(512) + (1 - hh) * 512])) for hh in range(2)]
        _CACHE["rc"] = _ret_consts()
    nc = _CACHE["nc"]
    x_prompt = f(x_prompt); x_sample = f(x_sample)
    common = {
        "n1g": _col(norm1_g[0]), "n2g": _col(norm2_g[0]), "fg": f(final_g).reshape(1, D),
        "w_ada": f(w_ada[0]), "b_col": _col(b_ada[0]), "b_row": f(b_ada[0]).reshape(1, 6144),
        "w_in": f(w_in[0]),
        "dec": np.concatenate([f(ret_decay_fwd[0]), f(ret_decay_bwd[0])]).reshape(1, 8),
        "cw": np.ascontiguousarray(f(hy_conv_w[0]).reshape(3, 12, 128).transpose(2, 1, 0)),
        "cb": np.ascontiguousarray(f(hy_conv_b[0]).reshape(12, 128).T),
        "pw1": f(hy_pos_w1[0]), "pb1": f(hy_pos_b1[0]).reshape(64, 1), "pw2": f(hy_pos_w2[0]),
        "pb2": f(hy_pos_b2[0]).reshape(64, 1), "pw3": f(hy_pos_w3[0]), "pfr": f(hy_sin_freq[0]).reshape(64, 1),
        "hyb": f(hy_bias[0]), "w_ro": f(w_ret_o[0]), "w_ho": f(w_hy_o[0]), "w_o": f(w_out[0]),
        "w_f1": f(w_ffn_in[0]), "w_f2": f(w_ffn_out[0]),
        "ident": np.eye(128, dtype=np.float32).astype(_BF), "rc": _CACHE["rc"], "identf": np.eye(128, dtype=np.float32),
    }
    for k, v in _CACHE["hc"].items():
        common[k + "_c"] = v
    in_maps = []
    for i in range(8):
        b = i // 2; hh = i % 2
        m = dict(common)
        m["xc"] = x_prompt[4 * i:4 * i + 4].reshape(1024, D)
        own = x_sample[b, hh * 512:(hh + 1) * 512]; oth = x_sample[b, (1 - hh) * 512:(2 - hh) * 512]
        m["xs"] = np.ascontiguousarray(np.concatenate([own, oth], axis=0))
        m["s0f"] = f(state_ret_fwd[b, 0]); m["s0b"] = f(state_ret_bwd[b, 0])
        m["cc"] = np.ascontiguousarray(np.concatenate([_col(c_ctx), _col(c[b])], axis=1))
        a = 1.0 if hh == 0 else 0.0
        m["sel"] = np.ascontiguousarray(np.broadcast_to(np.array([a, 1.0 - a], np.float32), (128, 2)))
        for k, v in _CACHE["hs"][hh].items():
            m[k + "_s"] = v
        in_maps.append(m)
    res = run_bass_kernel_spmd(nc, in_maps, core_ids=list(range(8)))
    R = res.results
    y_prompt = np.concatenate([R[i]["yc"].reshape(4, 256, D) for i in range(8)], axis=0)
    y_sample = np.stack([np.concatenate([R[2 * b]["ys"], R[2 * b + 1]["ys"]], axis=0) for b in range(4)], axis=0)
    nf = np.concatenate([R[i]["nsf"] for i in range(8)], axis=0).reshape(32, 1, 4, 128, 128)
    nb = np.concatenate([R[i]["nsb"] for i in range(8)], axis=0).reshape(32, 1, 4, 128, 128)
    return (y_prompt.astype(np.float32), y_sample.astype(np.float32), nf.astype(np.float32), nb.astype(np.float32))
```
